# Optimizing a Trainium2 kernel written in Bass

```python
import jax, jax.numpy as jnp
from jax import lax
import numpy as np

D_MODEL = 1024
BATCH = 4
SEQ = 8192
DEPTH = 1

HEAD_DIM = 64
DIFF_HEADS = 4
DIFF_V_DIM = 2 * HEAD_DIM
SB_HEADS = 8
D_DIFF = DIFF_HEADS * DIFF_V_DIM
D_SB = SB_HEADS * HEAD_DIM
D_MIX = D_DIFF + D_SB
D_FF = 2816
BLOCK_Q = 128
N_MOD = 9
RMS_EPS = 1e-6
FFN_RES_WEIGHT = 0.5
MIX_RES_WEIGHT = 1.0

kernel_name = "hybrid_diffattn_stickbreaking_macaron"


def rms_norm(x, gain):
    xf = x.astype(jnp.float32)
    y = xf * lax.rsqrt(jnp.mean(xf * xf, axis=-1, keepdims=True) + RMS_EPS)
    return (y * gain.astype(jnp.float32)).astype(x.dtype)


def swiglu(h, w_gate, w_up, w_down):
    return (jax.nn.silu(h @ w_gate) * (h @ w_up)) @ w_down


def sandwich(x, f, g_pre, g_post, shift, scale, gate, res_w):
    h = rms_norm(x, g_pre) * (1.0 + scale[:, None, :]) + shift[:, None, :]
    y = rms_norm(f(h), g_post)
    return x + res_w * gate[:, None, :] * y


def alibi_slopes(n):
    return 2.0 ** (-8.0 * jnp.arange(1, n + 1, dtype=jnp.float32) / n)


def hybrid_mixer(h, w_in, w_out, lam_q1, lam_k1, lam_q2, lam_k2, diff_subln, sb_beta, lambda_init):
    B, S, _ = h.shape
    nb = S // BLOCK_Q
    scale = 1.0 / np.sqrt(HEAD_DIM).astype(np.float32)
    proj = h @ w_in
    dq, dk, dv, sq, sk, sv = jnp.split(proj, 6, axis=-1)
    dq = dq.reshape(B, S, DIFF_HEADS, 2, HEAD_DIM)
    dk = dk.reshape(B, S, DIFF_HEADS, 2, HEAD_DIM)
    dv = dv.reshape(B, S, DIFF_HEADS, DIFF_V_DIM)
    sq = sq.reshape(B, S, SB_HEADS, HEAD_DIM)
    sk = sk.reshape(B, S, SB_HEADS, HEAD_DIM)
    sv = sv.reshape(B, S, SB_HEADS, HEAD_DIM)

    f32 = jnp.float32
    lam = (jnp.exp(jnp.sum(lam_q1.astype(f32) * lam_k1.astype(f32)))
           - jnp.exp(jnp.sum(lam_q2.astype(f32) * lam_k2.astype(f32))) + lambda_init)
    slopes = alibi_slopes(DIFF_HEADS)
    key_pos = jnp.arange(S)

    dq_blk = dq.reshape(B, nb, BLOCK_Q, DIFF_HEADS, 2, HEAD_DIM).transpose(1, 0, 2, 3, 4, 5)
    sq_blk = sq.reshape(B, nb, BLOCK_Q, SB_HEADS, HEAD_DIM).transpose(1, 0, 2, 3, 4)

    def block(args):
        i, q_d, q_s = args
        q_pos = i * BLOCK_Q + jnp.arange(BLOCK_Q)
        dist = q_pos[:, None] - key_pos[None, :]
        s_d = jnp.einsum('bqhcd,bkhcd->bhcqk', q_d, dk).astype(f32) * scale
        s_d = s_d - slopes[None, :, None, None, None] * dist.astype(f32)
        s_d = jnp.where(dist >= 0, s_d, -jnp.inf)
        p = jax.nn.softmax(s_d, axis=-1)
        w_diff = p[:, :, 0] - lam * p[:, :, 1]
        o_d = jnp.einsum('bhqk,bkhe->bqhe', w_diff.astype(dv.dtype), dv)
        z = jnp.einsum('bqhd,bkhd->bhqk', q_s, sk).astype(f32) * scale
        strict = dist > 0
        log_beta = jax.nn.log_sigmoid(z)
        log_1m = jnp.where(strict, jax.nn.log_sigmoid(-z), 0.0)
        suffix = lax.cumsum(log_1m, axis=3, reverse=True) - log_1m
        att = jnp.where(strict, jnp.exp(log_beta + suffix), 0.0)
        o_s = jnp.einsum('bhqk,bkhd->bqhd', att.astype(sv.dtype), sv)
        return o_d, o_s

    o_d, o_s = lax.map(block, (jnp.arange(nb), dq_blk, sq_blk))
    o_d = o_d.transpose(1, 0, 2, 3, 4).reshape(B, S, DIFF_HEADS, DIFF_V_DIM)
    o_s = o_s.transpose(1, 0, 2, 3, 4).reshape(B, S, D_SB)
    o_d = (rms_norm(o_d, diff_subln) * (1.0 - lambda_init)).reshape(B, S, D_DIFF)
    o_s = rms_norm(o_s, sb_beta)
    return jnp.concatenate([o_d, o_s], axis=-1) @ w_out


def setup_inputs(seed: int = 0) -> dict:
    key = jax.random.key(seed)
    ks = jax.random.split(key, 24)
    f32 = jnp.float32
    nrm = lambda k, shape, s: jax.random.normal(k, shape, f32) * s
    gain = lambda k, n: 1.0 + nrm(k, (DEPTH, n), 0.02)
    D = D_MODEL
    return {
        "x": nrm(ks[0], (BATCH, SEQ, D), 1.0),
        "c": nrm(ks[1], (BATCH, D), 1.0),
        "w_ada": nrm(ks[2], (DEPTH, D, N_MOD * D), 0.5 * D ** -0.5),
        "b_ada": nrm(ks[3], (DEPTH, N_MOD * D), 0.01),
        "ffn1_g_pre": gain(ks[4], D),
        "ffn1_g_post": gain(ks[5], D),
        "ffn1_w_gate": nrm(ks[6], (DEPTH, D, D_FF), D ** -0.5),
        "ffn1_w_up": nrm(ks[7], (DEPTH, D, D_FF), D ** -0.5),
        "ffn1_w_down": nrm(ks[8], (DEPTH, D_FF, D), D_FF ** -0.5),
        "mix_g_pre": gain(ks[9], D),
        "mix_g_post": gain(ks[10], D),
        "w_in": nrm(ks[11], (DEPTH, D, 3 * D_MIX), D ** -0.5),
        "w_out": nrm(ks[12], (DEPTH, D_MIX, D), D_MIX ** -0.5),
        "lam_q1": nrm(ks[13], (DEPTH, HEAD_DIM), 0.1),
        "lam_k1": nrm(ks[14], (DEPTH, HEAD_DIM), 0.1),
        "lam_q2": nrm(ks[15], (DEPTH, HEAD_DIM), 0.1),
        "lam_k2": nrm(ks[16], (DEPTH, HEAD_DIM), 0.1),
        "diff_subln": gain(ks[17], DIFF_V_DIM),
        "sb_beta": gain(ks[18], D_SB),
        "ffn2_g_pre": gain(ks[19], D),
        "ffn2_g_post": gain(ks[20], D),
        "ffn2_w_gate": nrm(ks[21], (DEPTH, D, D_FF), D ** -0.5),
        "ffn2_w_up": nrm(ks[22], (DEPTH, D, D_FF), D ** -0.5),
        "ffn2_w_down": nrm(ks[23], (DEPTH, D_FF, D), D_FF ** -0.5),
    }


def reference(x, c, w_ada, b_ada, ffn1_g_pre, ffn1_g_post, ffn1_w_gate, ffn1_w_up, ffn1_w_down,
              mix_g_pre, mix_g_post, w_in, w_out, lam_q1, lam_k1, lam_q2, lam_k2, diff_subln, sb_beta,
              ffn2_g_pre, ffn2_g_post, ffn2_w_gate, ffn2_w_up, ffn2_w_down):
    for l in range(DEPTH):
        lambda_init = 0.8 - 0.6 * float(np.exp(-0.3 * l))
        mod = jax.nn.silu(c) @ w_ada[l] + b_ada[l]
        (sh1, sc1, g1, sh2, sc2, g2, sh3, sc3, g3) = jnp.split(mod, N_MOD, axis=-1)
        x = sandwich(x, lambda h: swiglu(h, ffn1_w_gate[l], ffn1_w_up[l], ffn1_w_down[l]),
                     ffn1_g_pre[l], ffn1_g_post[l], sh1, sc1, g1, FFN_RES_WEIGHT)
        x = sandwich(x, lambda h: hybrid_mixer(h, w_in[l], w_out[l], lam_q1[l], lam_k1[l], lam_q2[l],
                                               lam_k2[l], diff_subln[l], sb_beta[l], lambda_init),
                     mix_g_pre[l], mix_g_post[l], sh2, sc2, g2, MIX_RES_WEIGHT)
        x = sandwich(x, lambda h: swiglu(h, ffn2_w_gate[l], ffn2_w_up[l], ffn2_w_down[l]),
                     ffn2_g_pre[l], ffn2_g_post[l], sh3, sc3, g3, FFN_RES_WEIGHT)
    return x
```

```python
import numpy as np
import concourse.bass as bass
import concourse.mybir as mybir
from concourse.bass_utils import run_bass_kernel_spmd

F32 = mybir.dt.float32
BF16 = mybir.dt.bfloat16
AF = mybir.ActivationFunctionType
ALU = mybir.AluOpType
AX = mybir.AxisListType

ENGS = ["pe", "act", "dve", "pool", "sp"]
NDSEM = 8


class Buf:
    def __init__(self, name):
        self.name = name
        self.last_write = None
        self.reads = []


class Op:
    __slots__ = ("eng", "fn", "deps", "needed", "sem", "count", "dma", "phase", "prewait")

    def __init__(self, eng, fn, deps, dma, phase):
        self.eng = eng
        self.fn = fn
        self.deps = deps
        self.needed = False
        self.sem = None
        self.count = None
        self.dma = dma
        self.phase = phase
        self.prewait = None


class Prog:
    def __init__(self, nc):
        self.nc = nc
        self.streams = {e: [] for e in ENGS}
        self.phase = 0
        self.phase_ops = {}

    def op(self, eng, fn, reads=(), writes=(), dma=False):
        deps = []
        seen = set()

        def add(d):
            if d is not None and id(d) not in seen:
                seen.add(id(d))
                deps.append(d)

        for b in reads:
            add(b.last_write)
        for b in writes:
            add(b.last_write)
            for r in b.reads:
                add(r)
        o = Op(eng, fn, deps, dma, self.phase)
        for d in deps:
            if d.eng == "pe" and eng == "pe" and not d.dma and not dma:
                continue
            d.needed = True
        self.streams[eng].append(o)
        for b in reads:
            b.reads.append(o)
        for b in writes:
            b.last_write = o
            b.reads = []
        return o

    def barrier(self):
        deps = []
        for e in ENGS:
            st = self.streams[e]
            for o in reversed(st):
                if not o.dma and o.fn is not None:
                    deps.append(o)
                    break
            nd = 0
            for o in reversed(st):
                if o.dma:
                    deps.append(o)
                    nd += 1
                    if nd >= NDSEM:
                        break
        for d in deps:
            d.needed = True
        for e in ENGS:
            o = Op(e, None, list(deps), False, self.phase)
            self.streams[e].append(o)

    def next_phase(self):
        self.barrier()
        self.phase += 1

    def emit(self):
        nc = self.nc
        nph = self.phase + 1
        self._cms = []
        sems = {}
        for ph in range(nph):
            for e in ENGS:
                cm = nc.semaphore(f"s_{e}_{ph}")
                sems[(e, ph)] = cm.__enter__()
                self._cms.append(cm)
        dsems = {}
        for e in ENGS:
            for i in range(NDSEM):
                cm = nc.semaphore(f"d_{e}_{i}")
                dsems[(e, i)] = cm.__enter__()
                self._cms.append(cm)
        final = {}
        for e in ENGS:
            cnt = {}
            nd = 0
            for o in self.streams[e]:
                if o.dma:
                    slot = nd % NDSEM
                    o.sem = dsems[(e, slot)]
                    o.count = 16 * (nd // NDSEM + 1)
                    o.prewait = (o.sem, 16 * (nd // NDSEM)) if nd >= NDSEM else None
                    nd += 1
                elif o.needed:
                    c = cnt.get(o.phase, 0) + 1
                    cnt[o.phase] = c
                    o.sem = sems[(e, o.phase)]
                    o.count = c
                    assert c < 30000, (e, o.phase, c)
            for ph, c in cnt.items():
                final[(e, ph)] = c
        self.final = final
        engobj = {"pe": nc.tensor, "act": nc.scalar, "dve": nc.vector, "pool": nc.gpsimd, "sp": nc.sync}
        streams = self.streams

        def run(ename, eng):
            known = {}
            for o in streams[ename]:
                waits = {}
                for d in o.deps:
                    if d.eng == "pe" and ename == "pe" and not d.dma and not o.dma and o.fn is not None:
                        continue
                    assert d.sem is not None, (d.eng, ename)
                    k = id(d.sem)
                    if known.get(k, 0) >= d.count:
                        continue
                    if k not in waits or waits[k][1] < d.count:
                        waits[k] = (d.sem, d.count)
                if o.prewait is not None:
                    k = id(o.prewait[0])
                    if known.get(k, 0) < o.prewait[1]:
                        if k not in waits or waits[k][1] < o.prewait[1]:
                            waits[k] = o.prewait
                for k, (s, c) in waits.items():
                    eng.wait_ge(s, c)
                    known[k] = c
                if o.fn is None:
                    continue
                inst = o.fn(eng)
                if o.dma:
                    inst.then_inc(o.sem, 16)
                elif o.needed:
                    inst.then_inc(o.sem, 1)

        with nc.Block() as block:
            @block.tensor
            def _(e):
                run("pe", e)

            @block.scalar
            def _(e):
                run("act", e)

            @block.vector
            def _(e):
                run("dve", e)

            @block.gpsimd
            def _(e):
                run("pool", e)

            @block.sync
            def _(e):
                run("sp", e)
        for cm in reversed(self._cms):
            cm.__exit__(None, None, None)


D = 1024
S = 8192
NBLK = 64
DFF = 2816
NFF = 22
EPS = 1e-6
NEG = -32768.0
LAMBDA_INIT = 0.2


def rr(i, n):
    return i % n


def build_program(debug=False, stop_after=99):
    nc = bass.Bass("TRN2", target_bir_lowering=False)
    P = Prog(nc)

    def din(name, shape, dt=F32):
        return nc.dram_tensor(name, list(shape), dt, kind="ExternalInput").ap()

    xin = din("xin", [S, D])
    c_pk = din("c_pk", [128, 8])
    b_pk = din("b_pk", [128, 72])
    b_row = din("b_row", [1, 9 * D])
    w_ada = din("w_ada", [D, 9 * D])
    gpre_pk = din("gpre_pk", [128, 24])
    gpost_row = din("gpost_row", [3, D])
    ffw = []
    for f in (1, 2):
        ffw.append((din(f"f{f}_wg", [D, DFF]), din(f"f{f}_wu", [D, DFF]), din(f"f{f}_wd", [DFF, D])))
    w_in = din("w_in", [D, 3 * D])
    w_out = din("w_out", [D, D])
    lam_in = din("lam_in", [4, 64])
    subln_row = din("subln_row", [1, 128])
    sbbeta_pk = din("sbbeta_pk", [128, 4])
    ident_d = din("ident_d", [128, 128])
    smask_d = din("smask_d", [128, 128])
    dmask_d = din("dmask_d", [3, 128, 512])
    kx_d = din("kx_d", [4, S])
    qx_d = din("qx_d", [4, 4, S // 2])
    vflag_d = din("vflag_d", [128, NBLK])

    out = nc.dram_tensor("out", [S // 2, D], F32, kind="ExternalOutput").ap()
    skind = "ExternalOutput" if debug else "Internal"

    def dscr(name, shape, dt, dbg=False):
        return nc.dram_tensor(name, list(shape), dt, kind=(skind if dbg else "Internal")).ap()

    DBG = dscr("DBG", [4, 128, 512], F32, True)
    DBGB = dscr("DBGB", [8, 256], BF16, True)
    X1 = dscr("X1", [S, D], F32, True)
    CAT = dscr("CAT", [S // 2, D], F32, True)
    X2 = dscr("X2", [S // 2, D], F32, True)
    GREP = dscr("GREP", [3, 128, D], F32)
    KTD = dscr("KTD", [4, 128, S], BF16)
    KTS = dscr("KTS", [4, 128, S], BF16)
    QTD = dscr("QTD", [4, 128, S // 2], BF16)
    QTS = dscr("QTS", [4, 128, S // 2], BF16)
    VD = dscr("VD", [S, 512], BF16)
    VS = dscr("VS", [S, 512], BF16)

    import contextlib
    es = contextlib.ExitStack()

    def sb(name, shape, dt=F32):
        return es.enter_context(nc.sbuf_tensor(name, list(shape), dt))

    def ps(name, shape, dt=F32):
        return es.enter_context(nc.psum_tensor(name, list(shape), dt))

    with es:
        identf = sb("identf", [128, 128], F32)
        identb = sb("identb", [128, 128], BF16)
        modT = sb("modT", [128, 72], F32)
        Aco = sb("Aco", [128, 24], F32)
        gpre = sb("gpre", [128, 24], F32)
        mhalf = sb("mhalf", [128, 8], F32)
        lamneg = sb("lamneg", [128, 1], F32)
        sublnw = sb("sublnw", [128, 128], F32)
        sbbeta = sb("sbbeta", [128, 4], F32)
        vflag = sb("vflag", [128, NBLK], F32)
        B_const = Buf("const")
        B_modT = Buf("modT")

        P.op("sp", lambda e: e.dma_start(out=identf[:], in_=ident_d), writes=[B_const], dma=True)
        P.op("pool", lambda e: e.dma_start(out=identb[:], in_=ident_d), writes=[B_const], dma=True)
        P.op("sp", lambda e: e.dma_start(out=gpre[:], in_=gpre_pk), writes=[B_const], dma=True)
        P.op("sp", lambda e: e.dma_start(out=sbbeta[:], in_=sbbeta_pk), writes=[B_const], dma=True)
        P.op("sp", lambda e: e.dma_start(out=vflag[:], in_=vflag_d), writes=[B_const], dma=True)
        P.op("sp", lambda e: e.dma_start(out=sublnw[:], in_=subln_row.partition_broadcast(128)), writes=[B_const], dma=True)
        P.op("pool", lambda e: e.memset(mhalf[:], -0.5), writes=[B_const])

        with contextlib.ExitStack() as es2:
            def sb2(name, shape, dt=F32):
                return es2.enter_context(nc.sbuf_tensor(name, list(shape), dt))

            def ps2(name, shape, dt=F32):
                return es2.enter_context(nc.psum_tensor(name, list(shape), dt))
            ct = sb2("ct", [128, 8])
            sfl = sb2("sfl", [128, 8])
            sbf = sb2("sbf", [128, 8], BF16)
            onesb = sb2("onesb", [128, 128], BF16)
            srep = sb2("srep", [128, 8, 128], BF16)
            bpk = sb2("bpk", [128, 72])
            wa = [sb2(f"wa{i}", [128, 8, D], BF16) for i in range(2)]
            brow = sb2("brow", [128, D])
            gprow = sb2("gprow", [128, D])
            grep = sb2("grep", [128, D])
            lamt = sb2("lamt", [128, 4, 64])
            lamp = sb2("lamp", [128, 2, 64])
            lams = sb2("lams", [128, 2])
            pmod = ps2("pmod", [128, 512])
            pgr = [ps2(f"pgr{i}", [128, 512]) for i in range(2)]
            Bc, Bs, Bsrep, Bbpk, Bpm = Buf("c"), Buf("s"), Buf("srep"), Buf("bpk"), Buf("pm")
            Bwa = [Buf("wa0"), Buf("wa1")]
            Bbrow, Bgprow, Bgrep, Bpgr = Buf("brow"), Buf("gprow"), Buf("grep"), [Buf("pgr0"), Buf("pgr1")]
            Blam = Buf("lam")
            P.op("sp", lambda e: e.dma_start(out=ct[:], in_=c_pk), writes=[Bc], dma=True)
            P.op("sp", lambda e: e.dma_start(out=bpk[:], in_=b_pk), writes=[Bbpk], dma=True)
            P.op("act", lambda e: e.activation(out=sfl[:], in_=ct[:], func=AF.Silu), reads=[Bc], writes=[Bs])
            P.op("dve", lambda e: e.tensor_copy(out=sbf[:], in_=sfl[:]), reads=[Bs], writes=[Bs])
            P.op("dve", lambda e: e.memset(onesb[:], 1.0), writes=[Bsrep])
            for k in range(8):
                P.op("dve", lambda e, k=k: e.tensor_scalar(out=srep[:, k, :], in0=onesb[:], scalar1=sfl[:, k:k + 1], scalar2=None, op0=ALU.mult),
                     reads=[Bs, Bsrep], writes=[Bsrep])
            P.op("sp", lambda e: e.dma_start(out=lamt[:].rearrange("p a b -> p (a b)"), in_=lam_in.rearrange("a b -> (a b)").rearrange("(o n) -> o n", o=1).partition_broadcast(128)), writes=[Blam], dma=True)
            P.op("dve", lambda e: e.tensor_tensor(out=lamp[:, 0, :], in0=lamt[:, 0, :], in1=lamt[:, 1, :], op=ALU.mult), reads=[Blam], writes=[Blam])
            P.op("dve", lambda e: e.tensor_tensor(out=lamp[:, 1, :], in0=lamt[:, 2, :], in1=lamt[:, 3, :], op=ALU.mult), reads=[Blam], writes=[Blam])
            P.op("dve", lambda e: e.reduce_sum(out=lams[:], in_=lamp[:], axis=AX.X), reads=[Blam], writes=[Blam])
            P.op("act", lambda e: e.activation(out=lams[:], in_=lams[:], func=AF.Exp), reads=[Blam], writes=[Blam])
            P.op("dve", lambda e: e.tensor_tensor(out=lamneg[:], in0=lams[:, 1:2], in1=lams[:, 0:1], op=ALU.subtract), reads=[Blam], writes=[Blam])
            P.op("dve", lambda e: e.tensor_scalar(out=lamneg[:], in0=lamneg[:], scalar1=-LAMBDA_INIT, scalar2=None, op0=ALU.add), reads=[Blam], writes=[B_const, Blam])
            P.op("dve", lambda e: e.tensor_scalar(out=sublnw[:], in0=sublnw[:], scalar1=1.0 - LAMBDA_INIT, scalar2=None, op0=ALU.mult), reads=[B_const], writes=[B_const])

            for v in range(9):
                w = wa[v % 2]
                Bw = Bwa[v % 2]
                for k in range(8):
                    P.op("pool", lambda e, k=k, v=v, w=w: e.dma_start(out=w[:, k, :], in_=w_ada[k * 128:(k + 1) * 128, v * D:(v + 1) * D]),
                         writes=[Bw], dma=True)
                for jc in range(8):
                    j = v * 8 + jc
                    for k in range(8):
                        P.op("pe", lambda e, j=j, jc=jc, k=k, w=w: e.matmul(pmod[:, j:j + 1], lhsT=w[:, k, jc * 128:(jc + 1) * 128], rhs=sbf[:, k:k + 1],
                                                                         start=(k == 0), stop=(k == 7)),
                             reads=[Bw, Bs], writes=[Bpm])
                if v % 3 == 2:
                    gi = v // 3
                    resw = 1.0 if gi == 1 else 0.5
                    P.op("sp", lambda e, v=v: e.dma_start(out=brow[:], in_=b_row[0:1, v * D:(v + 1) * D].partition_broadcast(128)), writes=[Bbrow], dma=True)
                    P.op("sp", lambda e, gi=gi: e.dma_start(out=gprow[:], in_=gpost_row[gi:gi + 1, :].partition_broadcast(128)), writes=[Bgprow], dma=True)
                    for hf in range(2):
                        pg = pgr[hf]
                        for k in range(8):
                            P.op("pe", lambda e, k=k, hf=hf, pg=pg, w=w: e.matmul(pg[:], lhsT=srep[:, k, :], rhs=w[:, k, hf * 512:(hf + 1) * 512],
                                                                              start=(k == 0), stop=(k == 7)),
                                 reads=[Bw, Bsrep], writes=[Bpgr[hf]])
                        P.op("dve", lambda e, hf=hf, pg=pg: e.tensor_tensor(out=grep[:, hf * 512:(hf + 1) * 512], in0=pg[:], in1=brow[:, hf * 512:(hf + 1) * 512], op=ALU.add),
                             reads=[Bpgr[hf], Bbrow], writes=[Bgrep])
                    P.op("dve", lambda e, resw=resw: e.scalar_tensor_tensor(out=grep[:], in0=grep[:], scalar=resw, in1=gprow[:], op0=ALU.mult, op1=ALU.mult),
                         reads=[Bgrep, Bgprow], writes=[Bgrep])
                    P.op("sp", lambda e, gi=gi: e.dma_start(out=GREP[gi], in_=grep[:]), reads=[Bgrep], dma=True)
            P.op("dve", lambda e: e.tensor_tensor(out=modT[:], in0=pmod[:, 0:72], in1=bpk[:], op=ALU.add), reads=[Bpm, Bbpk], writes=[B_modT])
            for i in range(3):
                P.op("dve", lambda e, i=i: e.scalar_tensor_tensor(out=Aco[:, i * 8:(i + 1) * 8], in0=modT[:, (3 * i + 1) * 8:(3 * i + 2) * 8], scalar=1.0,
                                                                 in1=gpre[:, i * 8:(i + 1) * 8], op0=ALU.add, op1=ALU.mult),
                     reads=[B_modT, B_const], writes=[B_modT])
            P.next_phase()

        def Bsh(i):
            return modT[:, (3 * i) * 8:(3 * i) * 8 + 8]

        def ffn_phase(tag, src_rows, dst_rows, ngroups, li, wts):
            with contextlib.ExitStack() as es2:
                def sb2(name, shape, dt=F32):
                    return es2.enter_context(nc.sbuf_tensor(f"{tag}s_{name}", list(shape), dt))

                def ps2(name, shape, dt=F32):
                    return es2.enter_context(nc.psum_tensor(f"{tag}p_{name}", list(shape), dt))
                wg = sb2("wg", [128, 8, DFF], BF16)
                wu = sb2("wu", [128, 8, DFF], BF16)
                wd = sb2("wd", [128, NFF, D], BF16)
                G = sb2("G", [128, D])
                xt = [sb2(f"xt{i}", [128, 2, D]) for i in range(3)]
                hT = [sb2(f"hT{i}", [128, 8, 256], BF16) for i in range(2)]
                junk = sb2("junk", [128, D], BF16)
                ss = [sb2(f"ss{i}", [128, 2]) for i in range(2)]
                rstd = [sb2(f"rstd{i}", [128, 2]) for i in range(2)]
                sd = [sb2(f"sd{i}", [128, 2]) for i in range(2)]
                ss2 = [sb2(f"ss2{i}", [128, 4]) for i in range(2)]
                rstd2 = [sb2(f"rstd2{i}", [128, 2]) for i in range(2)]
                sg = [sb2(f"sg{i}", [128, 256]) for i in range(2)]
                mj = [sb2(f"mj{i}", [128, 256], BF16) for i in range(3)]
                tmp = [sb2(f"tmp{i}", [128, 512]) for i in range(2)]
                ptr = [ps2(f"ptr{i}", [128, 2, 256]) for i in range(2)]
                pgu = [ps2(f"pgu{i}", [128, 512]) for i in range(2)]
                py = [[ps2(f"py{i}{h}", [128, 512]) for h in range(2)] for i in range(2)]
                Bw = Buf("w")
                BG = Buf("G")
                Bxt = [Buf("xt0"), Buf("xt1"), Buf("xt2")]
                BhT = [Buf("hT0"), Buf("hT1")]
                Bjunk = Buf("junk")
                Bst = [Buf("st0"), Buf("st1")]
                Bst2 = [Buf("st20"), Buf("st21")]
                Bsg = [Buf("sg0"), Buf("sg1")]
                Bmj = [Buf(f"mj{i}") for i in range(3)]
                Btmp = [Buf("tmp0"), Buf("tmp1")]
                Bptr = [Buf("ptr0"), Buf("ptr1")]
                Bpgu = [Buf("pgu0"), Buf("pgu1")]
                Bpy = [[Buf(f"py{i}{h}") for h in range(2)] for i in range(2)]
                w_g, w_u, w_d = wts
                for k in range(8):
                    for hf in range(2):
                        P.op("pool", lambda e, k=k, hf=hf: e.dma_start(out=wg[:, k, hf * 1408:(hf + 1) * 1408], in_=w_g[k * 128:(k + 1) * 128, hf * 1408:(hf + 1) * 1408]), writes=[Bw], dma=True)
                        P.op("pool", lambda e, k=k, hf=hf: e.dma_start(out=wu[:, k, hf * 1408:(hf + 1) * 1408], in_=w_u[k * 128:(k + 1) * 128, hf * 1408:(hf + 1) * 1408]), writes=[Bw], dma=True)
                for j in range(NFF):
                    P.op("pool", lambda e, j=j: e.dma_start(out=wd[:, j, :], in_=w_d[j * 128:(j + 1) * 128, :]), writes=[Bw], dma=True)
                P.op("sp", lambda e: e.dma_start(out=G[:], in_=GREP[li]), writes=[BG], dma=True)
                A = Aco[:, li * 8:(li + 1) * 8]
                Bv = Bsh(li)

                def stage_load(g):
                    x3 = g % 3
                    P.op("sp", lambda e: e.dma_start(out=xt[x3][:], in_=src_rows(g).rearrange("(i p) d -> p i d", p=128)), writes=[Bxt[x3]], dma=True)

                def stage_pre(g):
                    b = g % 2
                    x3 = g % 3
                    for i in range(2):
                        P.op("act", lambda e, i=i: e.activation(out=junk[:], in_=xt[x3][:, i, :], func=AF.Square, accum_out=ss[b][:, i:i + 1]),
                             reads=[Bxt[x3]], writes=[Bjunk, Bst[b]])
                    P.op("dve", lambda e: e.tensor_scalar(out=sd[b][:], in0=ss[b][:], scalar1=1.0 / D, scalar2=EPS, op0=ALU.mult, op1=ALU.add),
                         reads=[Bst[b]], writes=[Bst[b]])
                    P.op("pool", lambda e: e.tensor_tensor(out=rstd[b][:], in0=sd[b][:], in1=mhalf[:, 0:2], op=ALU.pow), reads=[Bst[b], B_const], writes=[Bst[b]])
                    P.op("dve", lambda e: e.reciprocal(out=sd[b][:], in_=rstd[b][:]), reads=[Bst[b]], writes=[Bst[b]])
                    for i in range(2):
                        P.op("dve", lambda e, i=i: e.tensor_scalar(out=xt[x3][:, i, :], in0=xt[x3][:, i, :], scalar1=rstd[b][:, i:i + 1], scalar2=None, op0=ALU.mult),
                             reads=[Bst[b], Bxt[x3]], writes=[Bxt[x3]])
                    for r in range(4):
                        pt = ptr[r % 2]
                        for cc in range(2):
                            c = 2 * r + cc
                            for i in range(2):
                                P.op("pe", lambda e, c=c, cc=cc, i=i, pt=pt: e.transpose(out=pt[:, cc, i * 128:(i + 1) * 128], in_=xt[x3][:, i, c * 128:(c + 1) * 128], identity=identf[:]),
                                     reads=[Bxt[x3], B_const], writes=[Bptr[r % 2]])
                        for cc in range(2):
                            c = 2 * r + cc
                            P.op("dve", lambda e, c=c, cc=cc, pt=pt: e.tensor_scalar(out=hT[b][:, c, :], in0=pt[:, cc, :], scalar1=A[:, c:c + 1], scalar2=Bv[:, c:c + 1], op0=ALU.mult, op1=ALU.add),
                                 reads=[Bptr[r % 2], B_modT], writes=[BhT[b]])

                def gu(g, j):
                    b = g % 2
                    pg = pgu[j % 2]
                    for k in range(8):
                        P.op("pe", lambda e, k=k: e.matmul(pg[:, 0:256], lhsT=wg[:, k, j * 128:(j + 1) * 128], rhs=hT[b][:, k, :], start=(k == 0), stop=False, skip_group_check=True),
                             reads=[Bw, BhT[b]], writes=[Bpgu[j % 2]])
                    for k in range(8):
                        P.op("pe", lambda e, k=k: e.matmul(pg[:, 256:512], lhsT=wu[:, k, j * 128:(j + 1) * 128], rhs=hT[b][:, k, :], start=False, stop=(k == 7), skip_group_check=True),
                             reads=[Bw, BhT[b]], writes=[Bpgu[j % 2]])
                    P.op("act", lambda e: e.activation(out=sg[j % 2][:], in_=pg[:, 0:256], func=AF.Silu), reads=[Bpgu[j % 2]], writes=[Bsg[j % 2]])
                    P.op("dve", lambda e: e.tensor_tensor(out=mj[j % 3][:], in0=sg[j % 2][:], in1=pg[:, 256:512], op=ALU.mult),
                         reads=[Bsg[j % 2], Bpgu[j % 2]], writes=[Bmj[j % 3]])

                def down(g, j):
                    for i in range(2):
                        for hf in range(2):
                            P.op("pe", lambda e, i=i, hf=hf: e.matmul(py[i][hf][:], lhsT=mj[j % 3][:, i * 128:(i + 1) * 128], rhs=wd[:, j, hf * 512:(hf + 1) * 512],
                                                                   start=(j == 0), stop=(j == NFF - 1)),
                                 reads=[Bmj[j % 3], Bw], writes=[Bpy[i][hf]])

                def stage_main(g):
                    b = g % 2
                    x3 = g % 3
                    gu(g, 0)
                    for j in range(NFF):
                        if j + 1 < NFF:
                            gu(g, j + 1)
                        down(g, j)
                    for i in range(2):
                        for hf in range(2):
                            P.op("act", lambda e, i=i, hf=hf: e.activation(out=junk[:, 0:512], in_=py[i][hf][:], func=AF.Square, accum_out=ss2[b][:, 2 * i + hf:2 * i + hf + 1]),
                                 reads=[Bpy[i][hf]], writes=[Bjunk, Bst2[b]])
                    P.op("dve", lambda e: e.tensor_tensor(out=rstd2[b][:], in0=ss2[b][:, 0:4:2], in1=ss2[b][:, 1:4:2], op=ALU.add), reads=[Bst2[b]], writes=[Bst2[b]])
                    P.op("dve", lambda e: e.tensor_scalar(out=rstd2[b][:], in0=rstd2[b][:], scalar1=1.0 / D, scalar2=EPS, op0=ALU.mult, op1=ALU.add), reads=[Bst2[b]], writes=[Bst2[b]])
                    P.op("pool", lambda e: e.tensor_tensor(out=rstd2[b][:], in0=rstd2[b][:], in1=mhalf[:, 0:2], op=ALU.pow), reads=[Bst2[b], B_const], writes=[Bst2[b]])
                    n = 0
                    for i in range(2):
                        for hf in range(2):
                            t = tmp[n % 2]
                            Bt = Btmp[n % 2]
                            n += 1
                            P.op("dve", lambda e, i=i, hf=hf, t=t: e.scalar_tensor_tensor(out=t[:], in0=py[i][hf][:], scalar=rstd2[b][:, i:i + 1], in1=G[:, hf * 512:(hf + 1) * 512], op0=ALU.mult, op1=ALU.mult),
                                 reads=[Bpy[i][hf], Bst2[b], BG], writes=[Bt])
                            P.op("dve", lambda e, i=i, hf=hf, t=t: e.scalar_tensor_tensor(out=xt[x3][:, i, hf * 512:(hf + 1) * 512], in0=xt[x3][:, i, hf * 512:(hf + 1) * 512], scalar=sd[b][:, i:i + 1], in1=t[:], op0=ALU.mult, op1=ALU.add),
                                 reads=[Bt, Bst[b], Bxt[x3]], writes=[Bxt[x3]])
                    P.op("sp", lambda e: e.dma_start(out=dst_rows(g).rearrange("(i p) d -> p i d", p=128), in_=xt[x3][:]), reads=[Bxt[x3]], dma=True)

                for t in range(ngroups + 2):
                    if t < ngroups:
                        stage_load(t)
                    if 0 <= t - 1 < ngroups:
                        stage_pre(t - 1)
                    if 0 <= t - 2 < ngroups:
                        stage_main(t - 2)
                P.next_phase()

        if stop_after >= 1:
            ng1 = S // 256
            ffn_phase("f1", lambda g: xin[g * 256:(g + 1) * 256, :], lambda g: X1[g * 256:(g + 1) * 256, :], ng1, 0, ffw[0])
        def proj_phase():
            with contextlib.ExitStack() as es2:
                def sb2(name, shape, dt=F32):
                    return es2.enter_context(nc.sbuf_tensor(f"pj_{name}", list(shape), dt))

                def ps2(name, shape, dt=F32):
                    return es2.enter_context(nc.psum_tensor(f"pjp_{name}", list(shape), dt))
                win = sb2("win", [128, 8, 3 * D], BF16)
                xt = [sb2(f"xt{i}", [128, 4, D]) for i in range(2)]
                hT = [sb2(f"hT{i}", [128, 8, 512], BF16) for i in range(2)]
                junk = sb2("junk", [128, D], BF16)
                ss = [sb2(f"ss{i}", [128, 4]) for i in range(2)]
                rstd = [sb2(f"rstd{i}", [128, 4]) for i in range(2)]
                ev = [sb2(f"ev{i}", [128, 512], BF16) for i in range(4)]
                ptr = [ps2(f"ptr{i}", [128, 512]) for i in range(2)]
                po = [ps2(f"po{i}", [128, 512]) for i in range(4)]
                Bw = Buf("win")
                Bxt = [Buf("xt0"), Buf("xt1")]
                BhT = [Buf("hT0"), Buf("hT1")]
                Bjunk = Buf("junk")
                Bst = [Buf("st0"), Buf("st1")]
                Bev = [Buf(f"ev{i}") for i in range(4)]
                Bptr = [Buf("ptr0"), Buf("ptr1")]
                Bpo = [Buf(f"po{i}") for i in range(4)]
                for k in range(8):
                    for t3 in range(3):
                        P.op("pool", lambda e, k=k, t3=t3: e.dma_start(out=win[:, k, t3 * D:(t3 + 1) * D], in_=w_in[k * 128:(k + 1) * 128, t3 * D:(t3 + 1) * D]), writes=[Bw], dma=True)
                A = Aco[:, 8:16]
                Bv = Bsh(1)
                NG = S // 512
                cnt = [0]

                def load(g):
                    b = g % 2
                    P.op("sp", lambda e: e.dma_start(out=xt[b][:], in_=X1[g * 512:(g + 1) * 512, :].rearrange("(i p) d -> p i d", p=128)), writes=[Bxt[b]], dma=True)

                def pre(g):
                    b = g % 2
                    for i in range(4):
                        P.op("act", lambda e, i=i: e.activation(out=junk[:], in_=xt[b][:, i, :], func=AF.Square, accum_out=ss[b][:, i:i + 1]),
                             reads=[Bxt[b]], writes=[Bjunk, Bst[b]])
                    P.op("dve", lambda e: e.tensor_scalar(out=ss[b][:], in0=ss[b][:], scalar1=1.0 / D, scalar2=EPS, op0=ALU.mult, op1=ALU.add), reads=[Bst[b]], writes=[Bst[b]])
                    P.op("pool", lambda e: e.tensor_tensor(out=rstd[b][:], in0=ss[b][:], in1=mhalf[:, 0:4], op=ALU.pow), reads=[Bst[b], B_const], writes=[Bst[b]])
                    for i in range(4):
                        P.op("dve", lambda e, i=i: e.tensor_scalar(out=xt[b][:, i, :], in0=xt[b][:, i, :], scalar1=rstd[b][:, i:i + 1], scalar2=None, op0=ALU.mult),
                             reads=[Bst[b], Bxt[b]], writes=[Bxt[b]])
                    for c in range(8):
                        pt = ptr[c % 2]
                        for i in range(4):
                            P.op("pe", lambda e, c=c, i=i, pt=pt: e.transpose(out=pt[:, i * 128:(i + 1) * 128], in_=xt[b][:, i, c * 128:(c + 1) * 128], identity=identf[:]),
                                 reads=[Bxt[b], B_const], writes=[Bptr[c % 2]])
                        P.op("dve", lambda e, c=c, pt=pt: e.tensor_scalar(out=hT[b][:, c, :], in0=pt[:], scalar1=A[:, c:c + 1], scalar2=Bv[:, c:c + 1], op0=ALU.mult, op1=ALU.add),
                             reads=[Bptr[c % 2], B_modT], writes=[BhT[b]])

                def main(g):
                    b = g % 2
                    for cc in range(8):
                        col = 512 + cc * 128 if cc < 4 else 2048 + (cc - 4) * 128
                        n = cnt[0] % 4
                        cnt[0] += 1
                        for k in range(8):
                            P.op("pe", lambda e, k=k, col=col, n=n: e.matmul(po[n][:], lhsT=win[:, k, col:col + 128], rhs=hT[b][:, k, :], start=(k == 0), stop=(k == 7)),
                                 reads=[Bw, BhT[b]], writes=[Bpo[n]])
                        P.op("act", lambda e, n=n: e.activation(out=ev[n][:], in_=po[n][:], func=AF.Copy), reads=[Bpo[n]], writes=[Bev[n]])
                        dst = (KTD[cc] if cc < 4 else KTS[cc - 4])[:, g * 512:(g + 1) * 512]
                        P.op("sp", lambda e, n=n, dst=dst: e.dma_start(out=dst, in_=ev[n][:]), reads=[Bev[n]], dma=True)
                    for i in range(4):
                        for sec, dstT in ((1024, VD), (2560, VS)):
                            n = cnt[0] % 4
                            cnt[0] += 1
                            for k in range(8):
                                P.op("pe", lambda e, k=k, sec=sec, n=n, i=i: e.matmul(po[n][:], lhsT=hT[b][:, k, i * 128:(i + 1) * 128], rhs=win[:, k, sec:sec + 512], start=(k == 0), stop=(k == 7)),
                                     reads=[Bw, BhT[b]], writes=[Bpo[n]])
                            blk = 4 * g + i
                            P.op("dve", lambda e, n=n, blk=blk: e.tensor_scalar(out=ev[n][:], in0=po[n][:], scalar1=vflag[:, blk:blk + 1], scalar2=None, op0=ALU.mult),
                                 reads=[Bpo[n], B_const], writes=[Bev[n]])
                            dst = dstT[blk * 128:(blk + 1) * 128, :]
                            P.op("sp", lambda e, n=n, dst=dst: e.dma_start(out=dst, in_=ev[n][:]), reads=[Bev[n]], dma=True)
                    for cc in range(8):
                        col = cc * 128 if cc < 4 else 1536 + (cc - 4) * 128
                        n = cnt[0] % 4
                        cnt[0] += 1
                        for s in range(2):
                            for k in range(8):
                                P.op("pe", lambda e, k=k, col=col, n=n, s=s: e.matmul(po[n][:, s * 128:(s + 1) * 128], lhsT=win[:, k, col:col + 128], rhs=hT[b][:, k, s * 256:s * 256 + 128],
                                                                                start=(k == 0 and s == 0), stop=(k == 7), skip_group_check=True),
                                     reads=[Bw, BhT[b]], writes=[Bpo[n]])
                        P.op("act", lambda e, n=n: e.activation(out=ev[n][:, 0:256], in_=po[n][:, 0:256], func=AF.Copy, scale=0.125), reads=[Bpo[n]], writes=[Bev[n]])
                        dst = (QTD[cc] if cc < 4 else QTS[cc - 4])[:, g * 256:(g + 1) * 256]
                        P.op("sp", lambda e, n=n, dst=dst: e.dma_start(out=dst, in_=ev[n][:, 0:256]), reads=[Bev[n]], dma=True)

                load(0)
                for t in range(NG + 1):
                    if t + 1 < NG:
                        load(t + 1)
                    if t < NG:
                        pre(t)
                    if t >= 1:
                        main(t - 1)
                P.next_phase()

        def diff_phase():
            with contextlib.ExitStack() as es2:
                def sb2(name, shape, dt=F32):
                    return es2.enter_context(nc.sbuf_tensor(f"da_{name}", list(shape), dt))

                def ps2(name, shape, dt=F32):
                    return es2.enter_context(nc.psum_tensor(f"dap_{name}", list(shape), dt))
                KTa = [[sb2(f"KT{hb}{c}", [128, S], BF16) for c in range(2)] for hb in range(2)]
                QTa = [[sb2(f"QT{hb}{c}", [128, S // 2], BF16) for c in range(2)] for hb in range(2)]
                Va = [sb2(f"Va{hb}", [128, NBLK, 130], BF16) for hb in range(2)]
                dmb = sb2("dmb", [128, 3, 512], BF16)
                pT = [sb2(f"pT{i}", [128, 512], BF16) for i in range(4)]
                od = [sb2(f"od{i}", [128, 2, 128]) for i in range(3)]
                rs = [sb2(f"rs{i}", [128, 8]) for i in range(2)]
                junk = sb2("junk", [128, 128], BF16)
                pS = [ps2(f"pS{i}", [128, 512]) for i in range(3)]
                pO = [[ps2(f"pO{st}{i}", [128, 512]) for i in range(2)] for st in range(2)]
                Bop = [Buf("op0"), Buf("op1")]
                Bva1 = [Buf("va1_0"), Buf("va1_1")]
                Bdm = Buf("dm")
                BpT = [Buf(f"pT{i}") for i in range(4)]
                Bod = [Buf(f"od{i}") for i in range(3)]
                Brs = [Buf("rs0"), Buf("rs1")]
                Bjunk = Buf("junk")
                BpS = [Buf(f"pS{i}") for i in range(3)]
                BpO = [[Buf(f"pO{st}{i}") for i in range(2)] for st in range(2)]
                P.op("pool", lambda e: e.dma_start(out=dmb[:], in_=dmask_d.rearrange("j p n -> p j n")), writes=[Bdm], dma=True)
                for hb in range(2):
                    P.op("pool", lambda e, hb=hb: e.memset(Va[hb][:, :, 128:130], 1.0), writes=[Bva1[hb]])

                def load_head(hd):
                    hb = hd % 2
                    for c in range(2):
                        P.op("sp", lambda e, c=c: e.dma_start(out=KTa[hb][c][0:64, :], in_=KTD[hd][c * 64:(c + 1) * 64, :]), writes=[Bop[hb]], dma=True)
                        P.op("pool", lambda e, c=c: e.dma_start(out=KTa[hb][c][64:68, :].rearrange("r (a b) -> r a b", b=2048), in_=kx_d.rearrange("r (a b) -> r a b", b=2048)), writes=[Bop[hb]], dma=True)
                        P.op("sp", lambda e, c=c: e.dma_start(out=QTa[hb][c][0:64, :], in_=QTD[hd][c * 64:(c + 1) * 64, :]), writes=[Bop[hb]], dma=True)
                        P.op("pool", lambda e, c=c: e.dma_start(out=QTa[hb][c][64:68, :].rearrange("r (a b) -> r a b", b=2048), in_=qx_d[hd].rearrange("r (a b) -> r a b", b=2048)), writes=[Bop[hb]], dma=True)
                    for kq in range(8):
                        P.op("sp", lambda e, kq=kq: e.dma_start(out=Va[hb][:, kq * 8:(kq + 1) * 8, 0:128], in_=VD[kq * 1024:(kq + 1) * 1024, hd * 128:(hd + 1) * 128].rearrange("(k p) e -> p k e", p=128)), writes=[Bop[hb]], dma=True)

                qn = [0]
                gn = [0]

                def compute_head(hd):
                    hb = hd % 2
                    K0, K1 = KTa[hb]
                    Q0, Q1 = QTa[hb]
                    V = Va[hb]
                    rdeps = [Bop[hb], Bva1[hb]]
                    tiles = [(j, kb) for j in range(16) for kb in range(4 * j, NBLK)]
                    N = len(tiles)
                    base = qn[0]
                    qn[0] += N

                    def A(n):
                        j, kb = tiles[n]
                        q = (base + n) % 3
                        t4 = (base + n) % 4
                        psq = pS[q]
                        rel = kb - 4 * j
                        P.op("pe", lambda e: e.matmul(psq[:, 0:256], lhsT=K0[0:68, kb * 128:(kb + 1) * 128], rhs=Q0[0:68, j * 256:(j + 1) * 256], start=True, stop=False, skip_group_check=True),
                             reads=rdeps, writes=[BpS[q]])
                        P.op("pe", lambda e: e.matmul(psq[:, 256:512], lhsT=K1[0:68, kb * 128:(kb + 1) * 128], rhs=Q1[0:68, j * 256:(j + 1) * 256], start=False, stop=(rel >= 3), skip_group_check=True),
                             reads=rdeps, writes=[BpS[q]])
                        if rel < 3:
                            P.op("pe", lambda e: e.matmul(psq[:, 0:512], lhsT=identb[:], rhs=dmb[:, rel, :], start=False, stop=True, skip_group_check=True),
                                 reads=[Bdm, B_const], writes=[BpS[q]])
                        P.op("act", lambda e: e.activation(out=pT[t4][:], in_=psq[:], func=AF.Exp), reads=[BpS[q]], writes=[BpT[t4]])

                    def B(n):
                        j, kb = tiles[n]
                        t4 = (base + n) % 4
                        rel = kb - 4 * j
                        st = j % 2
                        for m in range(2):
                            pOm = pO[st][m]
                            for s in range(2):
                                if s == 1 and rel < 2:
                                    continue
                                P.op("pe", lambda e, m=m, s=s, pOm=pOm: e.matmul(pOm[:, s * 130:s * 130 + 129], lhsT=pT[t4][:, m * 256 + s * 128:m * 256 + (s + 1) * 128], rhs=V[:, kb, 0:129],
                                                                             start=(rel == 0 and s == 0), stop=(kb == NBLK - 1), skip_group_check=True),
                                     reads=[BpT[t4]] + rdeps, writes=[BpO[st][m]])
                        if kb == NBLK - 1:
                            norm(j, st)

                    def norm(j, st):
                        g3 = gn[0] % 3
                        g2 = gn[0] % 2
                        gn[0] += 1
                        o = od[g3]
                        r = rs[g2]
                        pO0, pO1 = pO[st]
                        for m in range(2):
                            for s in range(2):
                                P.op("dve", lambda e, m=m, s=s: e.reciprocal(out=r[:, 2 * m + s:2 * m + s + 1], in_=pO[st][m][:, s * 130 + 128:s * 130 + 129]), reads=[BpO[st][m]], writes=[Brs[g2]])
                        P.op("dve", lambda e: e.tensor_scalar(out=r[:, 4:6], in0=r[:, 2:4], scalar1=lamneg[:, 0:1], scalar2=None, op0=ALU.mult), reads=[Brs[g2], B_const], writes=[Brs[g2]])
                        for s in range(2):
                            P.op("dve", lambda e, s=s: e.tensor_scalar(out=o[:, s, :], in0=pO0[:, s * 130:s * 130 + 128], scalar1=r[:, s:s + 1], scalar2=None, op0=ALU.mult),
                                 reads=[BpO[st][0], Brs[g2]], writes=[Bod[g3]])
                            P.op("dve", lambda e, s=s: e.scalar_tensor_tensor(out=o[:, s, :], in0=pO1[:, s * 130:s * 130 + 128], scalar=r[:, 4 + s:5 + s], in1=o[:, s, :], op0=ALU.mult, op1=ALU.add),
                                 reads=[BpO[st][1], Brs[g2], Bod[g3]], writes=[Bod[g3]])
                            P.op("act", lambda e, s=s: e.activation(out=junk[:], in_=o[:, s, :], func=AF.Square, accum_out=r[:, 6 + s:7 + s]), reads=[Bod[g3]], writes=[Bjunk, Brs[g2]])
                        P.op("dve", lambda e: e.tensor_scalar(out=r[:, 6:8], in0=r[:, 6:8], scalar1=1.0 / 128, scalar2=EPS, op0=ALU.mult, op1=ALU.add), reads=[Brs[g2]], writes=[Brs[g2]])
                        P.op("pool", lambda e: e.tensor_tensor(out=r[:, 6:8], in0=r[:, 6:8], in1=mhalf[:, 0:2], op=ALU.pow), reads=[Brs[g2], B_const], writes=[Brs[g2]])
                        for s in range(2):
                            P.op("dve", lambda e, s=s: e.scalar_tensor_tensor(out=o[:, s, :], in0=o[:, s, :], scalar=r[:, 6 + s:7 + s], in1=sublnw[:], op0=ALU.mult, op1=ALU.mult),
                                 reads=[Bod[g3], Brs[g2], B_const], writes=[Bod[g3]])
                        dst = CAT[j * 256:(j + 1) * 256, hd * 128:(hd + 1) * 128].rearrange("(s p) e -> p s e", p=128)
                        P.op("sp", lambda e: e.dma_start(out=dst, in_=o[:]), reads=[Bod[g3]], dma=True)

                    for n in range(N + 2):
                        if n < N:
                            A(n)
                        if 0 <= n - 2 < N:
                            B(n - 2)

                load_head(0)
                for hd in range(4):
                    if hd + 1 < 4:
                        load_head(hd + 1)
                    compute_head(hd)
                P.next_phase()

        def sb_phase():
            with contextlib.ExitStack() as es2:
                def sb2(name, shape, dt=F32):
                    return es2.enter_context(nc.sbuf_tensor(f"sa_{name}", list(shape), dt))

                def ps2(name, shape, dt=F32):
                    return es2.enter_context(nc.psum_tensor(f"sap_{name}", list(shape), dt))
                KTs = [sb2(f"KT{hb}", [128, S], BF16) for hb in range(2)]
                Qz = [[sb2(f"Qz{hb}{h2}", [128, S // 2], BF16) for h2 in range(2)] for hb in range(2)]
                Vs = [sb2(f"Vs{hb}", [128, NBLK, 128], BF16) for hb in range(2)]
                osb = sb2("osb", [128, 32, 128])
                smb = sb2("smb", [128, 128], BF16)
                om = [sb2(f"om{i}", [128, 512]) for i in range(3)]
                cp = [sb2(f"cp{i}", [128, 513]) for i in range(4)]
                att = [sb2(f"att{i}", [128, 512], BF16) for i in range(4)]
                attT = [sb2(f"attT{i}", [128, 512], BF16) for i in range(3)]
                pz = [ps2(f"pz{i}", [128, 512]) for i in range(3)]
                pTt = [ps2(f"pTt{i}", [128, 1024], BF16) for i in range(2)]
                pOs = [ps2(f"pOs{i}", [128, 512]) for i in range(2)]
                Bop = [Buf("op0"), Buf("op1")]
                Bz = [Buf("z0"), Buf("z1")]
                Bosb = Buf("osb")
                Bsm = Buf("sm")
                Bom = [Buf(f"om{i}") for i in range(3)]
                Bcp = [Buf(f"cp{i}") for i in range(4)]
                Bc0 = [Buf(f"c0{i}") for i in range(4)]
                Batt = [Buf(f"att{i}") for i in range(4)]
                BattT = [Buf(f"attT{i}") for i in range(3)]
                Bpz = [Buf(f"pz{i}") for i in range(3)]
                BpTt = [Buf(f"pTt{i}") for i in range(2)]
                BpOs = [Buf(f"pOs{i}") for i in range(2)]
                P.op("pool", lambda e: e.dma_start(out=smb[:], in_=smask_d), writes=[Bsm], dma=True)
                for hb in range(2):
                    P.op("pool", lambda e, hb=hb: e.memset(Qz[hb][0][64:128, :], 0.0), writes=[Bz[hb]])
                    P.op("pool", lambda e, hb=hb: e.memset(Qz[hb][1][0:64, :], 0.0), writes=[Bz[hb]])

                def load_pair(ch):
                    hb = ch % 2
                    P.op("sp", lambda e: e.dma_start(out=KTs[hb][:], in_=KTS[ch]), writes=[Bop[hb]], dma=True)
                    P.op("sp", lambda e: e.dma_start(out=Qz[hb][0][0:64, :], in_=QTS[ch][0:64, :]), writes=[Bop[hb]], dma=True)
                    P.op("sp", lambda e: e.dma_start(out=Qz[hb][1][64:128, :], in_=QTS[ch][64:128, :]), writes=[Bop[hb]], dma=True)
                    for kq in range(8):
                        P.op("sp", lambda e, kq=kq: e.dma_start(out=Vs[hb][:, kq * 8:(kq + 1) * 8, :], in_=VS[kq * 1024:(kq + 1) * 1024, ch * 128:(ch + 1) * 128].rearrange("(k p) e -> p k e", p=128)), writes=[Bop[hb]], dma=True)

                state = {"n": 0, "slot": 0}

                def compute_pair(ch):
                    hb = ch % 2
                    KT = KTs[hb]
                    V = Vs[hb]
                    rdeps = [Bop[hb], Bz[hb]]
                    tasks = []
                    for h2 in range(2):
                        for i in range(32):
                            b0 = 2 * i
                            first = True
                            if i % 2 == 1:
                                tasks.append((h2, i, b0, 2, True, b0 + 2 >= NBLK))
                                b0 += 2
                                first = False
                            while b0 < NBLK:
                                tasks.append((h2, i, b0, 4, first, b0 + 4 >= NBLK))
                                first = False
                                b0 += 4
                    N = len(tasks)
                    base = state["n"]
                    slotmap = {}

                    def A(n):
                        h2, i, b0, nb, first, last = tasks[n]
                        gnn = base + n
                        W = nb * 128
                        z = pz[gnn % 3]
                        Q = Qz[hb][h2]
                        P.op("pe", lambda e: e.matmul(z[:, 0:W], lhsT=Q[:, i * 128:(i + 1) * 128], rhs=KT[:, b0 * 128:b0 * 128 + W], start=True, stop=(not first), skip_group_check=True),
                             reads=rdeps, writes=[Bpz[gnn % 3]])
                        if first:
                            P.op("pe", lambda e: e.matmul(z[:, 0:128], lhsT=identb[:], rhs=smb[:], start=False, stop=True, skip_group_check=True),
                                 reads=[Bsm, B_const], writes=[Bpz[gnn % 3]])
                        o = om[gnn % 3]
                        P.op("act", lambda e: e.activation(out=o[:, 0:W], in_=z[:, 0:W], func=AF.Sigmoid, scale=-1.0), reads=[Bpz[gnn % 3]], writes=[Bom[gnn % 3]])
                        c = cp[gnn % 4]
                        if first:
                            P.op("dve", lambda e: e.tensor_tensor_scan(out=c[:, 1:W + 1], data0=o[:, 0:W], data1=z[:, 0:W], initial=1.0, op0=ALU.mult, op1=ALU.bypass),
                                 reads=[Bom[gnn % 3], Bpz[gnn % 3]], writes=[Bcp[gnn % 4]])
                        else:
                            pc = cp[(gnn - 1) % 4]
                            pW = tasks[n - 1][3] * 128
                            P.op("dve", lambda e: e.tensor_tensor_scan(out=c[:, 1:W + 1], data0=o[:, 0:W], data1=z[:, 0:W], initial=pc[:, pW:pW + 1], op0=ALU.mult, op1=ALU.bypass),
                                 reads=[Bom[gnn % 3], Bpz[gnn % 3], Bcp[(gnn - 1) % 4]], writes=[Bcp[gnn % 4]])

                    def A2(n):
                        h2, i, b0, nb, first, last = tasks[n]
                        gnn = base + n
                        W = nb * 128
                        c = cp[gnn % 4]
                        if first:
                            P.op("pool", lambda e: e.memset(c[:, 0:1], 1.0), writes=[Bc0[gnn % 4]])
                        else:
                            pc = cp[(gnn - 1) % 4]
                            pW = tasks[n - 1][3] * 128
                            P.op("dve", lambda e: e.tensor_copy(out=c[:, 0:1], in_=pc[:, pW:pW + 1]), reads=[Bcp[(gnn - 1) % 4]], writes=[Bc0[gnn % 4]])
                        a = att[gnn % 4]
                        P.op("pool", lambda e: e.tensor_tensor(out=a[:, 0:W], in0=c[:, 0:W], in1=c[:, 1:W + 1], op=ALU.subtract), reads=[Bcp[gnn % 4], Bc0[gnn % 4]], writes=[Batt[gnn % 4]])

                    def Bst(n):
                        h2, i, b0, nb, first, last = tasks[n]
                        gnn = base + n
                        W = nb * 128
                        a = att[gnn % 4]
                        t = pTt[gnn % 2]
                        for b in range(nb):
                            P.op("pe", lambda e, b=b: e.transpose(out=t[:, b * 128:(b + 1) * 128], in_=a[:, b * 128:(b + 1) * 128], identity=identb[:]),
                                 reads=[Batt[gnn % 4], B_const], writes=[BpTt[gnn % 2]])
                        aT = attT[gnn % 3]
                        P.op("act", lambda e: e.activation(out=aT[:, 0:W], in_=t[:, 0:W], func=AF.Copy), reads=[BpTt[gnn % 2]], writes=[BattT[gnn % 3]])

                    def C(n):
                        h2, i, b0, nb, first, last = tasks[n]
                        gnn = base + n
                        aT = attT[gnn % 3]
                        if first:
                            slotmap[(h2, i)] = state["slot"] % 2
                            state["slot"] += 1
                        sl = slotmap[(h2, i)]
                        for b in range(nb):
                            P.op("pe", lambda e, b=b: e.matmul(pOs[sl][:, 0:64], lhsT=aT[:, b * 128:(b + 1) * 128], rhs=V[:, b0 + b, h2 * 64:(h2 + 1) * 64], start=(first and b == 0), stop=(last and b == nb - 1)),
                                 reads=[BattT[gnn % 3]] + rdeps, writes=[BpOs[sl]])
                        if last:
                            P.op("dve", lambda e: e.tensor_copy(out=osb[:, i, h2 * 64:(h2 + 1) * 64], in_=pOs[sl][:, 0:64]), reads=[BpOs[sl]], writes=[Bosb])

                    for n in range(N + 5):
                        if n < N:
                            A(n)
                        if 0 <= n - 1 < N:
                            A2(n - 1)
                        if 0 <= n - 4 < N:
                            Bst(n - 4)
                        if 0 <= n - 5 < N:
                            C(n - 5)
                    state["n"] += N
                    for iq in range(4):
                        dst = CAT[iq * 1024:(iq + 1) * 1024, 512 + ch * 128:512 + (ch + 1) * 128].rearrange("(i p) e -> p i e", p=128)
                        P.op("sp", lambda e, iq=iq, dst=dst: e.dma_start(out=dst, in_=osb[:, iq * 8:(iq + 1) * 8, :]), reads=[Bosb], dma=True)

                load_pair(0)
                for ch in range(4):
                    if ch + 1 < 4:
                        load_pair(ch + 1)
                    compute_pair(ch)
                P.next_phase()

        def out_phase():
            with contextlib.ExitStack() as es2:
                def sb2(name, shape, dt=F32):
                    return es2.enter_context(nc.sbuf_tensor(f"op_{name}", list(shape), dt))

                def ps2(name, shape, dt=F32):
                    return es2.enter_context(nc.psum_tensor(f"opp_{name}", list(shape), dt))
                wout = sb2("wout", [128, 8, D], BF16)
                G = sb2("G", [128, D])
                ct = [sb2(f"ct{i}", [128, 4, D]) for i in range(2)]
                x1t = [sb2(f"x1t{i}", [128, 4, D]) for i in range(2)]
                cT = [sb2(f"cT{i}", [128, 8, 512], BF16) for i in range(2)]
                junk = sb2("junk", [128, 512], BF16)
                ss = [sb2(f"ss{i}", [128, 4]) for i in range(2)]
                st2 = [sb2(f"st2{i}", [128, 4]) for i in range(2)]
                tmp = [sb2(f"tmp{i}", [128, 512]) for i in range(2)]
                ptr = [ps2(f"ptr{i}", [128, 512]) for i in range(2)]
                po = [ps2(f"po{i}", [128, 512]) for i in range(4)]
                Bw, BG = Buf("w"), Buf("G")
                Bct = [Buf("ct0"), Buf("ct1")]
                Bx1 = [Buf("x10"), Buf("x11")]
                BcT = [Buf("cT0"), Buf("cT1")]
                Bjunk = Buf("junk")
                Bss = [Buf("ss0"), Buf("ss1")]
                Bst2 = [Buf("st20"), Buf("st21")]
                Btmp = [Buf("tmp0"), Buf("tmp1")]
                Bptr = [Buf("ptr0"), Buf("ptr1")]
                Bpo = [Buf(f"po{i}") for i in range(4)]
                for k in range(8):
                    P.op("pool", lambda e, k=k: e.dma_start(out=wout[:, k, :], in_=w_out[k * 128:(k + 1) * 128, :]), writes=[Bw], dma=True)
                P.op("sp", lambda e: e.dma_start(out=G[:], in_=GREP[1]), writes=[BG], dma=True)
                NG = 8
                X1v = X1.rearrange("(blk two p) d -> p blk two d", two=2, p=128)
                cnt = [0, 0, 0]

                def load(g):
                    b = g % 2
                    P.op("sp", lambda e: e.dma_start(out=ct[b][:], in_=CAT[g * 512:(g + 1) * 512, :].rearrange("(i p) d -> p i d", p=128)), writes=[Bct[b]], dma=True)
                    P.op("sp", lambda e: e.dma_start(out=x1t[b][:], in_=X1v[:, 4 * g:4 * g + 4, 0, :]), writes=[Bx1[b]], dma=True)

                def pre(g):
                    b = g % 2
                    for i in range(4):
                        P.op("act", lambda e, i=i: e.activation(out=junk[:], in_=ct[b][:, i, 512:1024], func=AF.Square, accum_out=ss[b][:, i:i + 1]),
                             reads=[Bct[b]], writes=[Bjunk, Bss[b]])
                    P.op("dve", lambda e: e.tensor_scalar(out=ss[b][:], in0=ss[b][:], scalar1=1.0 / 512, scalar2=EPS, op0=ALU.mult, op1=ALU.add), reads=[Bss[b]], writes=[Bss[b]])
                    P.op("pool", lambda e: e.tensor_tensor(out=ss[b][:], in0=ss[b][:], in1=mhalf[:, 0:4], op=ALU.pow), reads=[Bss[b], B_const], writes=[Bss[b]])
                    for i in range(4):
                        P.op("dve", lambda e, i=i: e.tensor_scalar(out=ct[b][:, i, 512:1024], in0=ct[b][:, i, 512:1024], scalar1=ss[b][:, i:i + 1], scalar2=None, op0=ALU.mult),
                             reads=[Bss[b], Bct[b]], writes=[Bct[b]])
                    for c in range(8):
                        pt = ptr[c % 2]
                        for i in range(4):
                            P.op("pe", lambda e, c=c, i=i, pt=pt: e.transpose(out=pt[:, i * 128:(i + 1) * 128], in_=ct[b][:, i, c * 128:(c + 1) * 128], identity=identf[:]),
                                 reads=[Bct[b], B_const], writes=[Bptr[c % 2]])
                        if c < 4:
                            P.op("act", lambda e, c=c, pt=pt: e.activation(out=cT[b][:, c, :], in_=pt[:], func=AF.Copy), reads=[Bptr[c % 2]], writes=[BcT[b]])
                        else:
                            P.op("dve", lambda e, c=c, pt=pt: e.tensor_scalar(out=cT[b][:, c, :], in0=pt[:], scalar1=sbbeta[:, c - 4:c - 3], scalar2=None, op0=ALU.mult),
                                 reads=[Bptr[c % 2], B_const], writes=[BcT[b]])

                def main(g):
                    b = g % 2
                    for i in range(4):
                        pp = []
                        for hf in range(2):
                            n = cnt[0] % 4
                            cnt[0] += 1
                            pp.append(n)
                            for c in range(8):
                                P.op("pe", lambda e, c=c, n=n, hf=hf, i=i: e.matmul(po[n][:], lhsT=cT[b][:, c, i * 128:(i + 1) * 128], rhs=wout[:, c, hf * 512:(hf + 1) * 512], start=(c == 0), stop=(c == 7)),
                                     reads=[BcT[b], Bw], writes=[Bpo[n]])
                        s2 = cnt[1] % 2
                        cnt[1] += 1
                        r = st2[s2]
                        for hf in range(2):
                            P.op("act", lambda e, r=r, hf=hf, n=pp[hf]: e.activation(out=junk[:], in_=po[n][:], func=AF.Square, accum_out=r[:, hf:hf + 1]), reads=[Bpo[pp[hf]]], writes=[Bjunk, Bst2[s2]])
                        P.op("dve", lambda e, r=r: e.tensor_tensor(out=r[:, 2:3], in0=r[:, 0:1], in1=r[:, 1:2], op=ALU.add), reads=[Bst2[s2]], writes=[Bst2[s2]])
                        P.op("dve", lambda e, r=r: e.tensor_scalar(out=r[:, 2:3], in0=r[:, 2:3], scalar1=1.0 / D, scalar2=EPS, op0=ALU.mult, op1=ALU.add), reads=[Bst2[s2]], writes=[Bst2[s2]])
                        P.op("pool", lambda e, r=r: e.tensor_tensor(out=r[:, 3:4], in0=r[:, 2:3], in1=mhalf[:, 0:1], op=ALU.pow), reads=[Bst2[s2], B_const], writes=[Bst2[s2]])
                        for hf in range(2):
                            tn = cnt[2] % 2
                            cnt[2] += 1
                            t = tmp[tn]
                            P.op("dve", lambda e, r=r, hf=hf, n=pp[hf], t=t: e.scalar_tensor_tensor(out=t[:], in0=po[n][:], scalar=r[:, 3:4], in1=G[:, hf * 512:(hf + 1) * 512], op0=ALU.mult, op1=ALU.mult),
                                 reads=[Bpo[pp[hf]], Bst2[s2], BG], writes=[Btmp[tn]])
                            P.op("dve", lambda e, r=r, hf=hf, t=t, i=i: e.tensor_tensor(out=x1t[b][:, i, hf * 512:(hf + 1) * 512], in0=x1t[b][:, i, hf * 512:(hf + 1) * 512], in1=t[:], op=ALU.add),
                                 reads=[Btmp[tn], Bx1[b]], writes=[Bx1[b]])
                    P.op("sp", lambda e: e.dma_start(out=X2[g * 512:(g + 1) * 512, :].rearrange("(i p) d -> p i d", p=128), in_=x1t[b][:]), reads=[Bx1[b]], dma=True)

                load(0)
                for t in range(NG + 1):
                    if t < NG:
                        pre(t)
                    if t >= 1:
                        main(t - 1)
                    if t + 1 < NG:
                        load(t + 1)
                P.next_phase()

        if stop_after >= 2:
            proj_phase()
        if stop_after >= 3:
            diff_phase()
        if stop_after >= 4:
            sb_phase()
        if stop_after >= 5:
            out_phase()
        if stop_after >= 6:
            ffn_phase("f2", lambda g: X2[g * 256:(g + 1) * 256, :], lambda g: out[g * 256:(g + 1) * 256, :], S // 2 // 256, 2, ffw[1])
        P.emit()
    return nc


def _tables():
    ident = np.eye(128, dtype=np.float32)
    r = np.arange(128)
    smask = np.where(r[None, :] <= r[:, None], NEG, 0.0).astype(np.float32)
    kk = r[:, None]
    qq = r[None, :]
    tri = np.where(kk >= qq, 0.0, NEG).astype(np.float32)
    full = np.zeros((128, 128), np.float32)
    none = np.full((128, 128), NEG, np.float32)
    dm = np.zeros((3, 128, 2, 2, 128), np.float32)
    for m in range(2):
        dm[0, :, m, 0] = tri
        dm[0, :, m, 1] = none
        dm[1, :, m, 0] = full
        dm[1, :, m, 1] = none
        dm[2, :, m, 0] = full
        dm[2, :, m, 1] = tri
    dmask = dm.reshape(3, 128, 512)
    return ident, smask, dmask


def make_in_maps(inputs):
    f = lambda a: np.ascontiguousarray(np.asarray(a, dtype=np.float32))
    x = f(inputs["x"])
    c = f(inputs["c"])
    ident, smask, dmask = _tables()
    pk = lambda v, n: np.ascontiguousarray(v.reshape(n, 128).T)
    gpre_pk = np.concatenate([pk(f(inputs[k])[0], 8) for k in ("ffn1_g_pre", "mix_g_pre", "ffn2_g_pre")], axis=1)
    gpost_row = np.stack([f(inputs[k])[0] for k in ("ffn1_g_post", "mix_g_post", "ffn2_g_post")], axis=0)
    lam_in = np.stack([f(inputs[k])[0] for k in ("lam_q1", "lam_k1", "lam_q2", "lam_k2")], axis=0)
    common = {
        "b_pk": pk(f(inputs["b_ada"])[0], 72),
        "b_row": f(inputs["b_ada"]),
        "w_ada": f(inputs["w_ada"])[0],
        "gpre_pk": np.ascontiguousarray(gpre_pk),
        "gpost_row": np.ascontiguousarray(gpost_row),
        "f1_wg": f(inputs["ffn1_w_gate"])[0], "f1_wu": f(inputs["ffn1_w_up"])[0], "f1_wd": f(inputs["ffn1_w_down"])[0],
        "f2_wg": f(inputs["ffn2_w_gate"])[0], "f2_wu": f(inputs["ffn2_w_up"])[0], "f2_wd": f(inputs["ffn2_w_down"])[0],
        "w_in": f(inputs["w_in"])[0], "w_out": f(inputs["w_out"])[0],
        "lam_in": np.ascontiguousarray(lam_in),
        "subln_row": f(inputs["diff_subln"]),
        "sbbeta_pk": pk(f(inputs["sb_beta"])[0], 4),
        "ident_d": ident, "smask_d": smask, "dmask_d": dmask,
    }
    pos = np.arange(S)
    qpos = (np.arange(S // 2) // 128) * 256 + (np.arange(S // 2) % 128)
    qx = np.zeros((4, 4, S // 2), np.float32)
    for h in range(4):
        sl = 2.0 ** (-2.0 * (h + 1))
        qx[h, 0] = sl * (qpos % 128)
        qx[h, 1] = sl * 128.0 * (qpos // 128)
        qx[h, 2] = -sl
        qx[h, 3] = -sl
    in_maps = []
    for core in range(8):
        b, hh = core // 2, core % 2
        xr = x[b, ::-1]
        if hh == 0:
            xin = np.ascontiguousarray(xr)
        else:
            xin = np.concatenate([xr[128:], np.zeros((128, D), np.float32)], axis=0)
        kx = np.zeros((4, S), np.float32)
        kx[0] = 1.0
        kx[1] = 1.0
        kx[2] = pos % 128
        kx[3] = 128.0 * (pos // 128)
        vflag = np.ones((128, NBLK), np.float32)
        if hh == 1:
            kx[3, S - 128:] = 2.0 ** 20
            vflag[:, NBLK - 1] = 0.0
        m = dict(common)
        m.update({"xin": xin, "c_pk": pk(c[b], 8), "kx_d": kx, "qx_d": qx, "vflag_d": vflag})
        in_maps.append(m)
    return in_maps


def assemble(outs):
    res = np.zeros((4, S, D), np.float32)
    for core in range(8):
        b, hh = core // 2, core % 2
        o = np.asarray(outs[core]).reshape(32, 128, D)
        for i in range(32):
            p0 = (2 * i + hh) * 128
            res[b, S - 1 - p0 - 127:S - p0] = o[i][::-1]
    return res


_NC_CACHE = {}


def kernel(**inputs):
    if "nc" not in _NC_CACHE:
        _NC_CACHE["nc"] = build_program()
    nc = _NC_CACHE["nc"]
    in_maps = make_in_maps(inputs)
    r = run_bass_kernel_spmd(nc, in_maps, core_ids=list(range(8)))
    return assemble([r.results[i]["out"] for i in range(8)])
```

```python
import numpy as np
import concourse.bass as bass
import concourse.mybir as mybir
from concourse.bass_utils import run_bass_kernel_spmd

F32 = mybir.dt.float32
BF16 = mybir.dt.bfloat16
AF = mybir.ActivationFunctionType
ALU = mybir.AluOpType
AX = mybir.AxisListType

ENGS = ["pe", "act", "dve", "pool", "sp"]
NDSEM = 8


class Buf:
    def __init__(self, name):
        self.name = name
        self.last_write = None
        self.reads = []


class Op:
    __slots__ = ("eng", "fn", "deps", "needed", "sem", "count", "dma", "phase", "prewait")

    def __init__(self, eng, fn, deps, dma, phase):
        self.eng = eng
        self.fn = fn
        self.deps = deps
        self.needed = False
        self.sem = None
        self.count = None
        self.dma = dma
        self.phase = phase
        self.prewait = None


class Prog:
    def __init__(self, nc):
        self.nc = nc
        self.streams = {e: [] for e in ENGS}
        self.phase = 0
        self.phase_ops = {}

    def op(self, eng, fn, reads=(), writes=(), dma=False):
        deps = []
        seen = set()

        def add(d):
            if d is not None and id(d) not in seen:
                seen.add(id(d))
                deps.append(d)

        for b in reads:
            add(b.last_write)
        for b in writes:
            add(b.last_write)
            for r in b.reads:
                add(r)
        o = Op(eng, fn, deps, dma, self.phase)
        for d in deps:
            if d.eng == "pe" and eng == "pe" and not d.dma and not dma:
                continue
            d.needed = True
        self.streams[eng].append(o)
        for b in reads:
            b.reads.append(o)
        for b in writes:
            b.last_write = o
            b.reads = []
        return o

    def barrier(self):
        deps = []
        for e in ENGS:
            st = self.streams[e]
            for o in reversed(st):
                if not o.dma and o.fn is not None:
                    deps.append(o)
                    break
            nd = 0
            for o in reversed(st):
                if o.dma:
                    deps.append(o)
                    nd += 1
                    if nd >= NDSEM:
                        break
        for d in deps:
            d.needed = True
        for e in ENGS:
            o = Op(e, None, list(deps), False, self.phase)
            self.streams[e].append(o)

    def next_phase(self):
        self.barrier()
        self.phase += 1

    def emit(self):
        nc = self.nc
        nph = self.phase + 1
        self._cms = []
        sems = {}
        for ph in range(nph):
            for e in ENGS:
                cm = nc.semaphore(f"s_{e}_{ph}")
                sems[(e, ph)] = cm.__enter__()
                self._cms.append(cm)
        dsems = {}
        for e in ENGS:
            for i in range(NDSEM):
                cm = nc.semaphore(f"d_{e}_{i}")
                dsems[(e, i)] = cm.__enter__()
                self._cms.append(cm)
        final = {}
        for e in ENGS:
            cnt = {}
            nd = 0
            for o in self.streams[e]:
                if o.dma:
                    slot = nd % NDSEM
                    o.sem = dsems[(e, slot)]
                    o.count = 16 * (nd // NDSEM + 1)
                    o.prewait = (o.sem, 16 * (nd // NDSEM)) if nd >= NDSEM else None
                    nd += 1
                elif o.needed:
                    c = cnt.get(o.phase, 0) + 1
                    cnt[o.phase] = c
                    o.sem = sems[(e, o.phase)]
                    o.count = c
                    assert c < 30000, (e, o.phase, c)
            for ph, c in cnt.items():
                final[(e, ph)] = c
        self.final = final
        engobj = {"pe": nc.tensor, "act": nc.scalar, "dve": nc.vector, "pool": nc.gpsimd, "sp": nc.sync}
        streams = self.streams

        def run(ename, eng):
            known = {}
            for o in streams[ename]:
                waits = {}
                for d in o.deps:
                    if d.eng == "pe" and ename == "pe" and not d.dma and not o.dma and o.fn is not None:
                        continue
                    assert d.sem is not None, (d.eng, ename)
                    k = id(d.sem)
                    if known.get(k, 0) >= d.count:
                        continue
                    if k not in waits or waits[k][1] < d.count:
                        waits[k] = (d.sem, d.count)
                if o.prewait is not None:
                    k = id(o.prewait[0])
                    if known.get(k, 0) < o.prewait[1]:
                        if k not in waits or waits[k][1] < o.prewait[1]:
                            waits[k] = o.prewait
                for k, (s, c) in waits.items():
                    eng.wait_ge(s, c)
                    known[k] = c
                if o.fn is None:
                    continue
                inst = o.fn(eng)
                if o.dma:
                    inst.then_inc(o.sem, 16)
                elif o.needed:
                    inst.then_inc(o.sem, 1)

        with nc.Block() as block:
            @block.tensor
            def _(e):
                run("pe", e)

            @block.scalar
            def _(e):
                run("act", e)

            @block.vector
            def _(e):
                run("dve", e)

            @block.gpsimd
            def _(e):
                run("pool", e)

            @block.sync
            def _(e):
                run("sp", e)
        for cm in reversed(self._cms):
            cm.__exit__(None, None, None)


D = 1024
S = 8192
NBLK = 64
DFF = 2816
NFF = 22
EPS = 1e-6
NEG = -32768.0
LAMBDA_INIT = 0.2


def rr(i, n):
    return i % n


def build_program(debug=False, stop_after=99):
    nc = bass.Bass("TRN2", target_bir_lowering=False)
    P = Prog(nc)

    def din(name, shape, dt=F32):
        return nc.dram_tensor(name, list(shape), dt, kind="ExternalInput").ap()

    xin = din("xin", [S, D])
    c_pk = din("c_pk", [128, 8])
    b_pk = din("b_pk", [128, 72])
    b_row = din("b_row", [1, 9 * D])
    w_ada = din("w_ada", [D, 9 * D])
    gpre_pk = din("gpre_pk", [128, 24])
    gpost_row = din("gpost_row", [3, D])
    ffw = []
    for f in (1, 2):
        ffw.append((din(f"f{f}_wg", [D, DFF]), din(f"f{f}_wu", [D, DFF]), din(f"f{f}_wd", [DFF, D])))
    w_in = din("w_in", [D, 3 * D])
    w_out = din("w_out", [D, D])
    lam_in = din("lam_in", [4, 64])
    subln_row = din("subln_row", [1, 128])
    sbbeta_pk = din("sbbeta_pk", [128, 4])
    ident_d = din("ident_d", [128, 128])
    smask_d = din("smask_d", [128, 128])
    dmask_d = din("dmask_d", [3, 128, 512])
    kx_d = din("kx_d", [4, S])
    qx_d = din("qx_d", [4, 4, S // 2])
    vflag_d = din("vflag_d", [128, NBLK])

    out = nc.dram_tensor("out", [S // 2, D], F32, kind="ExternalOutput").ap()
    skind = "ExternalOutput" if debug else "Internal"

    def dscr(name, shape, dt, dbg=False):
        return nc.dram_tensor(name, list(shape), dt, kind=(skind if dbg else "Internal")).ap()

    DBG = dscr("DBG", [4, 128, 512], F32, True)
    DBGB = dscr("DBGB", [8, 256], BF16, True)
    X1 = dscr("X1", [S, D], F32, True)
    CAT = dscr("CAT", [S // 2, D], F32, True)
    X2 = dscr("X2", [S // 2, D], F32, True)
    GREP = dscr("GREP", [3, 128, D], F32)
    KTD = dscr("KTD", [4, 128, S], BF16)
    KTS = dscr("KTS", [4, 128, S], BF16)
    QTD = dscr("QTD", [4, 128, S // 2], BF16)
    QTS = dscr("QTS", [4, 128, S // 2], BF16)
    VD = dscr("VD", [S, 512], BF16)
    VS = dscr("VS", [S, 512], BF16)

    import contextlib
    es = contextlib.ExitStack()

    def sb(name, shape, dt=F32):
        return es.enter_context(nc.sbuf_tensor(name, list(shape), dt))

    def ps(name, shape, dt=F32):
        return es.enter_context(nc.psum_tensor(name, list(shape), dt))

    with es:
        identf = sb("identf", [128, 128], F32)
        identb = sb("identb", [128, 128], BF16)
        modT = sb("modT", [128, 72], F32)
        Aco = sb("Aco", [128, 24], F32)
        gpre = sb("gpre", [128, 24], F32)
        mhalf = sb("mhalf", [128, 8], F32)
        lamneg = sb("lamneg", [128, 1], F32)
        sublnw = sb("sublnw", [128, 128], F32)
        sbbeta = sb("sbbeta", [128, 4], F32)
        vflag = sb("vflag", [128, NBLK], F32)
        B_const = Buf("const")
        B_modT = Buf("modT")

        P.op("sp", lambda e: e.dma_start(out=identf[:], in_=ident_d), writes=[B_const], dma=True)
        P.op("pool", lambda e: e.dma_start(out=identb[:], in_=ident_d), writes=[B_const], dma=True)
        P.op("sp", lambda e: e.dma_start(out=gpre[:], in_=gpre_pk), writes=[B_const], dma=True)
        P.op("sp", lambda e: e.dma_start(out=sbbeta[:], in_=sbbeta_pk), writes=[B_const], dma=True)
        P.op("sp", lambda e: e.dma_start(out=vflag[:], in_=vflag_d), writes=[B_const], dma=True)
        P.op("sp", lambda e: e.dma_start(out=sublnw[:], in_=subln_row.partition_broadcast(128)), writes=[B_const], dma=True)
        P.op("pool", lambda e: e.memset(mhalf[:], -0.5), writes=[B_const])

        with contextlib.ExitStack() as es2:
            def sb2(name, shape, dt=F32):
                return es2.enter_context(nc.sbuf_tensor(name, list(shape), dt))

            def ps2(name, shape, dt=F32):
                return es2.enter_context(nc.psum_tensor(name, list(shape), dt))
            ct = sb2("ct", [128, 8])
            sfl = sb2("sfl", [128, 8])
            sbf = sb2("sbf", [128, 8], BF16)
            onesb = sb2("onesb", [128, 128], BF16)
            srep = sb2("srep", [128, 8, 128], BF16)
            bpk = sb2("bpk", [128, 72])
            wa = [sb2(f"wa{i}", [128, 8, D], BF16) for i in range(2)]
            brow = sb2("brow", [128, D])
            gprow = sb2("gprow", [128, D])
            grep = sb2("grep", [128, D])
            lamt = sb2("lamt", [128, 4, 64])
            lamp = sb2("lamp", [128, 2, 64])
            lams = sb2("lams", [128, 2])
            pmod = ps2("pmod", [128, 512])
            pgr = [ps2(f"pgr{i}", [128, 512]) for i in range(2)]
            Bc, Bs, Bsrep, Bbpk, Bpm = Buf("c"), Buf("s"), Buf("srep"), Buf("bpk"), Buf("pm")
            Bwa = [Buf("wa0"), Buf("wa1")]
            Bbrow, Bgprow, Bgrep, Bpgr = Buf("brow"), Buf("gprow"), Buf("grep"), [Buf("pgr0"), Buf("pgr1")]
            Blam = Buf("lam")
            P.op("sp", lambda e: e.dma_start(out=ct[:], in_=c_pk), writes=[Bc], dma=True)
            P.op("sp", lambda e: e.dma_start(out=bpk[:], in_=b_pk), writes=[Bbpk], dma=True)
            P.op("act", lambda e: e.activation(out=sfl[:], in_=ct[:], func=AF.Silu), reads=[Bc], writes=[Bs])
            P.op("dve", lambda e: e.tensor_copy(out=sbf[:], in_=sfl[:]), reads=[Bs], writes=[Bs])
            P.op("dve", lambda e: e.memset(onesb[:], 1.0), writes=[Bsrep])
            for k in range(8):
                P.op("dve", lambda e, k=k: e.tensor_scalar(out=srep[:, k, :], in0=onesb[:], scalar1=sfl[:, k:k + 1], scalar2=None, op0=ALU.mult),
                     reads=[Bs, Bsrep], writes=[Bsrep])
            P.op("sp", lambda e: e.dma_start(out=lamt[:].rearrange("p a b -> p (a b)"), in_=lam_in.rearrange("a b -> (a b)").rearrange("(o n) -> o n", o=1).partition_broadcast(128)), writes=[Blam], dma=True)
            P.op("dve", lambda e: e.tensor_tensor(out=lamp[:, 0, :], in0=lamt[:, 0, :], in1=lamt[:, 1, :], op=ALU.mult), reads=[Blam], writes=[Blam])
            P.op("dve", lambda e: e.tensor_tensor(out=lamp[:, 1, :], in0=lamt[:, 2, :], in1=lamt[:, 3, :], op=ALU.mult), reads=[Blam], writes=[Blam])
            P.op("dve", lambda e: e.reduce_sum(out=lams[:], in_=lamp[:], axis=AX.X), reads=[Blam], writes=[Blam])
            P.op("act", lambda e: e.activation(out=lams[:], in_=lams[:], func=AF.Exp), reads=[Blam], writes=[Blam])
            P.op("dve", lambda e: e.tensor_tensor(out=lamneg[:], in0=lams[:, 1:2], in1=lams[:, 0:1], op=ALU.subtract), reads=[Blam], writes=[Blam])
            P.op("dve", lambda e: e.tensor_scalar(out=lamneg[:], in0=lamneg[:], scalar1=-LAMBDA_INIT, scalar2=None, op0=ALU.add), reads=[Blam], writes=[B_const, Blam])
            P.op("dve", lambda e: e.tensor_scalar(out=sublnw[:], in0=sublnw[:], scalar1=1.0 - LAMBDA_INIT, scalar2=None, op0=ALU.mult), reads=[B_const], writes=[B_const])

            for v in range(9):
                w = wa[v % 2]
                Bw = Bwa[v % 2]
                for k in range(8):
                    P.op("pool", lambda e, k=k, v=v, w=w: e.dma_start(out=w[:, k, :], in_=w_ada[k * 128:(k + 1) * 128, v * D:(v + 1) * D]),
                         writes=[Bw], dma=True)
                for jc in range(8):
                    j = v * 8 + jc
                    for k in range(8):
                        P.op("pe", lambda e, j=j, jc=jc, k=k, w=w: e.matmul(pmod[:, j:j + 1], lhsT=w[:, k, jc * 128:(jc + 1) * 128], rhs=sbf[:, k:k + 1],
                                                                         start=(k == 0), stop=(k == 7)),
                             reads=[Bw, Bs], writes=[Bpm])
                if v % 3 == 2:
                    gi = v // 3
                    resw = 1.0 if gi == 1 else 0.5
                    P.op("sp", lambda e, v=v: e.dma_start(out=brow[:], in_=b_row[0:1, v * D:(v + 1) * D].partition_broadcast(128)), writes=[Bbrow], dma=True)
                    P.op("sp", lambda e, gi=gi: e.dma_start(out=gprow[:], in_=gpost_row[gi:gi + 1, :].partition_broadcast(128)), writes=[Bgprow], dma=True)
                    for hf in range(2):
                        pg = pgr[hf]
                        for k in range(8):
                            P.op("pe", lambda e, k=k, hf=hf, pg=pg, w=w: e.matmul(pg[:], lhsT=srep[:, k, :], rhs=w[:, k, hf * 512:(hf + 1) * 512],
                                                                              start=(k == 0), stop=(k == 7)),
                                 reads=[Bw, Bsrep], writes=[Bpgr[hf]])
                        P.op("dve", lambda e, hf=hf, pg=pg: e.tensor_tensor(out=grep[:, hf * 512:(hf + 1) * 512], in0=pg[:], in1=brow[:, hf * 512:(hf + 1) * 512], op=ALU.add),
                             reads=[Bpgr[hf], Bbrow], writes=[Bgrep])
                    P.op("dve", lambda e, resw=resw: e.scalar_tensor_tensor(out=grep[:], in0=grep[:], scalar=resw, in1=gprow[:], op0=ALU.mult, op1=ALU.mult),
                         reads=[Bgrep, Bgprow], writes=[Bgrep])
                    P.op("sp", lambda e, gi=gi: e.dma_start(out=GREP[gi], in_=grep[:]), reads=[Bgrep], dma=True)
            P.op("dve", lambda e: e.tensor_tensor(out=modT[:], in0=pmod[:, 0:72], in1=bpk[:], op=ALU.add), reads=[Bpm, Bbpk], writes=[B_modT])
            for i in range(3):
                P.op("dve", lambda e, i=i: e.scalar_tensor_tensor(out=Aco[:, i * 8:(i + 1) * 8], in0=modT[:, (3 * i + 1) * 8:(3 * i + 2) * 8], scalar=1.0,
                                                                 in1=gpre[:, i * 8:(i + 1) * 8], op0=ALU.add, op1=ALU.mult),
                     reads=[B_modT, B_const], writes=[B_modT])
            P.next_phase()

        def Bsh(i):
            return modT[:, (3 * i) * 8:(3 * i) * 8 + 8]

        def ffn_phase(tag, src_rows, dst_rows, ngroups, li, wts):
            with contextlib.ExitStack() as es2:
                def sb2(name, shape, dt=F32):
                    return es2.enter_context(nc.sbuf_tensor(f"{tag}s_{name}", list(shape), dt))

                def ps2(name, shape, dt=F32):
                    return es2.enter_context(nc.psum_tensor(f"{tag}p_{name}", list(shape), dt))
                wg = sb2("wg", [128, 8, DFF], BF16)
                wu = sb2("wu", [128, 8, DFF], BF16)
                wd = sb2("wd", [128, NFF, D], BF16)
                G = sb2("G", [128, D])
                xt = [sb2(f"xt{i}", [128, 2, D]) for i in range(3)]
                hT = [sb2(f"hT{i}", [128, 8, 256], BF16) for i in range(2)]
                junk = sb2("junk", [128, D], BF16)
                ss = [sb2(f"ss{i}", [128, 2]) for i in range(2)]
                rstd = [sb2(f"rstd{i}", [128, 2]) for i in range(2)]
                sd = [sb2(f"sd{i}", [128, 2]) for i in range(2)]
                ss2 = [sb2(f"ss2{i}", [128, 4]) for i in range(2)]
                rstd2 = [sb2(f"rstd2{i}", [128, 2]) for i in range(2)]
                sg = [sb2(f"sg{i}", [128, 256]) for i in range(2)]
                mj = [sb2(f"mj{i}", [128, 256], BF16) for i in range(3)]
                tmp = [sb2(f"tmp{i}", [128, 512]) for i in range(2)]
                ptr = [ps2(f"ptr{i}", [128, 2, 256]) for i in range(2)]
                pgu = [ps2(f"pgu{i}", [128, 512]) for i in range(2)]
                py = [[ps2(f"py{i}{h}", [128, 512]) for h in range(2)] for i in range(2)]
                Bw = Buf("w")
                BG = Buf("G")
                Bxt = [Buf("xt0"), Buf("xt1"), Buf("xt2")]
                BhT = [Buf("hT0"), Buf("hT1")]
                Bjunk = Buf("junk")
                Bst = [Buf("st0"), Buf("st1")]
                Bst2 = [Buf("st20"), Buf("st21")]
                Bsg = [Buf("sg0"), Buf("sg1")]
                Bmj = [Buf(f"mj{i}") for i in range(3)]
                Btmp = [Buf("tmp0"), Buf("tmp1")]
                Bptr = [Buf("ptr0"), Buf("ptr1")]
                Bpgu = [Buf("pgu0"), Buf("pgu1")]
                Bpy = [[Buf(f"py{i}{h}") for h in range(2)] for i in range(2)]
                w_g, w_u, w_d = wts
                for k in range(8):
                    for hf in range(2):
                        P.op("pool", lambda e, k=k, hf=hf: e.dma_start(out=wg[:, k, hf * 1408:(hf + 1) * 1408], in_=w_g[k * 128:(k + 1) * 128, hf * 1408:(hf + 1) * 1408]), writes=[Bw], dma=True)
                        P.op("pool", lambda e, k=k, hf=hf: e.dma_start(out=wu[:, k, hf * 1408:(hf + 1) * 1408], in_=w_u[k * 128:(k + 1) * 128, hf * 1408:(hf + 1) * 1408]), writes=[Bw], dma=True)
                for j in range(NFF):
                    P.op("pool", lambda e, j=j: e.dma_start(out=wd[:, j, :], in_=w_d[j * 128:(j + 1) * 128, :]), writes=[Bw], dma=True)
                P.op("sp", lambda e: e.dma_start(out=G[:], in_=GREP[li]), writes=[BG], dma=True)
                A = Aco[:, li * 8:(li + 1) * 8]
                Bv = Bsh(li)

                def stage_load(g):
                    x3 = g % 3
                    P.op("sp", lambda e: e.dma_start(out=xt[x3][:], in_=src_rows(g).rearrange("(i p) d -> p i d", p=128)), writes=[Bxt[x3]], dma=True)

                def stage_pre(g):
                    b = g % 2
                    x3 = g % 3
                    for i in range(2):
                        P.op("act", lambda e, i=i: e.activation(out=junk[:], in_=xt[x3][:, i, :], func=AF.Square, accum_out=ss[b][:, i:i + 1]),
                             reads=[Bxt[x3]], writes=[Bjunk, Bst[b]])
                    P.op("dve", lambda e: e.tensor_scalar(out=sd[b][:], in0=ss[b][:], scalar1=1.0 / D, scalar2=EPS, op0=ALU.mult, op1=ALU.add),
                         reads=[Bst[b]], writes=[Bst[b]])
                    P.op("pool", lambda e: e.tensor_tensor(out=rstd[b][:], in0=sd[b][:], in1=mhalf[:, 0:2], op=ALU.pow), reads=[Bst[b], B_const], writes=[Bst[b]])
                    P.op("dve", lambda e: e.reciprocal(out=sd[b][:], in_=rstd[b][:]), reads=[Bst[b]], writes=[Bst[b]])
                    for i in range(2):
                        P.op("dve", lambda e, i=i: e.tensor_scalar(out=xt[x3][:, i, :], in0=xt[x3][:, i, :], scalar1=rstd[b][:, i:i + 1], scalar2=None, op0=ALU.mult),
                             reads=[Bst[b], Bxt[x3]], writes=[Bxt[x3]])
                    for r in range(4):
                        pt = ptr[r % 2]
                        for cc in range(2):
                            c = 2 * r + cc
                            for i in range(2):
                                P.op("pe", lambda e, c=c, cc=cc, i=i, pt=pt: e.transpose(out=pt[:, cc, i * 128:(i + 1) * 128], in_=xt[x3][:, i, c * 128:(c + 1) * 128], identity=identf[:]),
                                     reads=[Bxt[x3], B_const], writes=[Bptr[r % 2]])
                        for cc in range(2):
                            c = 2 * r + cc
                            P.op("dve", lambda e, c=c, cc=cc, pt=pt: e.tensor_scalar(out=hT[b][:, c, :], in0=pt[:, cc, :], scalar1=A[:, c:c + 1], scalar2=Bv[:, c:c + 1], op0=ALU.mult, op1=ALU.add),
                                 reads=[Bptr[r % 2], B_modT], writes=[BhT[b]])

                def gu(g, j):
                    b = g % 2
                    pg = pgu[j % 2]
                    for k in range(8):
                        P.op("pe", lambda e, k=k: e.matmul(pg[:, 0:256], lhsT=wg[:, k, j * 128:(j + 1) * 128], rhs=hT[b][:, k, :], start=(k == 0), stop=False, skip_group_check=True),
                             reads=[Bw, BhT[b]], writes=[Bpgu[j % 2]])
                    for k in range(8):
                        P.op("pe", lambda e, k=k: e.matmul(pg[:, 256:512], lhsT=wu[:, k, j * 128:(j + 1) * 128], rhs=hT[b][:, k, :], start=False, stop=(k == 7), skip_group_check=True),
                             reads=[Bw, BhT[b]], writes=[Bpgu[j % 2]])
                    P.op("act", lambda e: e.activation(out=sg[j % 2][:], in_=pg[:, 0:256], func=AF.Silu), reads=[Bpgu[j % 2]], writes=[Bsg[j % 2]])
                    P.op("dve", lambda e: e.tensor_tensor(out=mj[j % 3][:], in0=sg[j % 2][:], in1=pg[:, 256:512], op=ALU.mult),
                         reads=[Bsg[j % 2], Bpgu[j % 2]], writes=[Bmj[j % 3]])

                def down(g, j):
                    for i in range(2):
                        for hf in range(2):
                            P.op("pe", lambda e, i=i, hf=hf: e.matmul(py[i][hf][:], lhsT=mj[j % 3][:, i * 128:(i + 1) * 128], rhs=wd[:, j, hf * 512:(hf + 1) * 512],
                                                                   start=(j == 0), stop=(j == NFF - 1)),
                                 reads=[Bmj[j % 3], Bw], writes=[Bpy[i][hf]])

                def stage_main(g, hook=None):
                    b = g % 2
                    x3 = g % 3
                    gu(g, 0)
                    for j in range(NFF):
                        if j + 1 < NFF:
                            gu(g, j + 1)
                        down(g, j)
                        if j == 9 and hook is not None:
                            hook()
                    for i in range(2):
                        for hf in range(2):
                            P.op("act", lambda e, i=i, hf=hf: e.activation(out=junk[:, 0:512], in_=py[i][hf][:], func=AF.Square, accum_out=ss2[b][:, 2 * i + hf:2 * i + hf + 1]),
                                 reads=[Bpy[i][hf]], writes=[Bjunk, Bst2[b]])
                    P.op("dve", lambda e: e.tensor_tensor(out=rstd2[b][:], in0=ss2[b][:, 0:4:2], in1=ss2[b][:, 1:4:2], op=ALU.add), reads=[Bst2[b]], writes=[Bst2[b]])
                    P.op("dve", lambda e: e.tensor_scalar(out=rstd2[b][:], in0=rstd2[b][:], scalar1=1.0 / D, scalar2=EPS, op0=ALU.mult, op1=ALU.add), reads=[Bst2[b]], writes=[Bst2[b]])
                    P.op("pool", lambda e: e.tensor_tensor(out=rstd2[b][:], in0=rstd2[b][:], in1=mhalf[:, 0:2], op=ALU.pow), reads=[Bst2[b], B_const], writes=[Bst2[b]])
                    n = 0
                    for i in range(2):
                        for hf in range(2):
                            t = tmp[n % 2]
                            Bt = Btmp[n % 2]
                            n += 1
                            P.op("dve", lambda e, i=i, hf=hf, t=t: e.scalar_tensor_tensor(out=t[:], in0=py[i][hf][:], scalar=rstd2[b][:, i:i + 1], in1=G[:, hf * 512:(hf + 1) * 512], op0=ALU.mult, op1=ALU.mult),
                                 reads=[Bpy[i][hf], Bst2[b], BG], writes=[Bt])
                            P.op("dve", lambda e, i=i, hf=hf, t=t: e.scalar_tensor_tensor(out=xt[x3][:, i, hf * 512:(hf + 1) * 512], in0=xt[x3][:, i, hf * 512:(hf + 1) * 512], scalar=sd[b][:, i:i + 1], in1=t[:], op0=ALU.mult, op1=ALU.add),
                                 reads=[Bt, Bst[b], Bxt[x3]], writes=[Bxt[x3]])
                    P.op("sp", lambda e: e.dma_start(out=dst_rows(g).rearrange("(i p) d -> p i d", p=128), in_=xt[x3][:]), reads=[Bxt[x3]], dma=True)

                for t in range(ngroups + 2):
                    if t < ngroups:
                        stage_load(t)
                    has_pre = 0 <= t - 1 < ngroups
                    if 0 <= t - 2 < ngroups:
                        stage_main(t - 2, hook=(lambda t=t: stage_pre(t - 1)) if has_pre else None)
                    elif has_pre:
                        stage_pre(t - 1)
                P.next_phase()

        if stop_after >= 1:
            ng1 = S // 256
            ffn_phase("f1", lambda g: xin[g * 256:(g + 1) * 256, :], lambda g: X1[g * 256:(g + 1) * 256, :], ng1, 0, ffw[0])
        def proj_phase():
            with contextlib.ExitStack() as es2:
                def sb2(name, shape, dt=F32):
                    return es2.enter_context(nc.sbuf_tensor(f"pj_{name}", list(shape), dt))

                def ps2(name, shape, dt=F32):
                    return es2.enter_context(nc.psum_tensor(f"pjp_{name}", list(shape), dt))
                win = sb2("win", [128, 8, 3 * D], BF16)
                xt = [sb2(f"xt{i}", [128, 4, D]) for i in range(2)]
                hT = [sb2(f"hT{i}", [128, 8, 512], BF16) for i in range(2)]
                junk = sb2("junk", [128, D], BF16)
                ss = [sb2(f"ss{i}", [128, 4]) for i in range(2)]
                rstd = [sb2(f"rstd{i}", [128, 4]) for i in range(2)]
                ev = [sb2(f"ev{i}", [128, 512], BF16) for i in range(4)]
                ptr = [ps2(f"ptr{i}", [128, 512]) for i in range(2)]
                po = [ps2(f"po{i}", [128, 512]) for i in range(4)]
                Bw = Buf("win")
                Bxt = [Buf("xt0"), Buf("xt1")]
                BhT = [Buf("hT0"), Buf("hT1")]
                Bjunk = Buf("junk")
                Bst = [Buf("st0"), Buf("st1")]
                Bev = [Buf(f"ev{i}") for i in range(4)]
                Bptr = [Buf("ptr0"), Buf("ptr1")]
                Bpo = [Buf(f"po{i}") for i in range(4)]
                for k in range(8):
                    for t3 in range(3):
                        P.op("pool", lambda e, k=k, t3=t3: e.dma_start(out=win[:, k, t3 * D:(t3 + 1) * D], in_=w_in[k * 128:(k + 1) * 128, t3 * D:(t3 + 1) * D]), writes=[Bw], dma=True)
                A = Aco[:, 8:16]
                Bv = Bsh(1)
                NG = S // 512
                cnt = [0]

                def load(g):
                    b = g % 2
                    P.op("sp", lambda e: e.dma_start(out=xt[b][:], in_=X1[g * 512:(g + 1) * 512, :].rearrange("(i p) d -> p i d", p=128)), writes=[Bxt[b]], dma=True)

                def pre(g):
                    b = g % 2
                    for i in range(4):
                        P.op("act", lambda e, i=i: e.activation(out=junk[:], in_=xt[b][:, i, :], func=AF.Square, accum_out=ss[b][:, i:i + 1]),
                             reads=[Bxt[b]], writes=[Bjunk, Bst[b]])
                    P.op("dve", lambda e: e.tensor_scalar(out=ss[b][:], in0=ss[b][:], scalar1=1.0 / D, scalar2=EPS, op0=ALU.mult, op1=ALU.add), reads=[Bst[b]], writes=[Bst[b]])
                    P.op("pool", lambda e: e.tensor_tensor(out=rstd[b][:], in0=ss[b][:], in1=mhalf[:, 0:4], op=ALU.pow), reads=[Bst[b], B_const], writes=[Bst[b]])
                    for i in range(4):
                        P.op("dve", lambda e, i=i: e.tensor_scalar(out=xt[b][:, i, :], in0=xt[b][:, i, :], scalar1=rstd[b][:, i:i + 1], scalar2=None, op0=ALU.mult),
                             reads=[Bst[b], Bxt[b]], writes=[Bxt[b]])
                    for c in range(8):
                        pt = ptr[c % 2]
                        for i in range(4):
                            P.op("pe", lambda e, c=c, i=i, pt=pt: e.transpose(out=pt[:, i * 128:(i + 1) * 128], in_=xt[b][:, i, c * 128:(c + 1) * 128], identity=identf[:]),
                                 reads=[Bxt[b], B_const], writes=[Bptr[c % 2]])
                        P.op("dve", lambda e, c=c, pt=pt: e.tensor_scalar(out=hT[b][:, c, :], in0=pt[:], scalar1=A[:, c:c + 1], scalar2=Bv[:, c:c + 1], op0=ALU.mult, op1=ALU.add),
                             reads=[Bptr[c % 2], B_modT], writes=[BhT[b]])

                def main(g):
                    b = g % 2
                    for cc in range(8):
                        col = 512 + cc * 128 if cc < 4 else 2048 + (cc - 4) * 128
                        n = cnt[0] % 4
                        cnt[0] += 1
                        for k in range(8):
                            P.op("pe", lambda e, k=k, col=col, n=n: e.matmul(po[n][:], lhsT=win[:, k, col:col + 128], rhs=hT[b][:, k, :], start=(k == 0), stop=(k == 7)),
                                 reads=[Bw, BhT[b]], writes=[Bpo[n]])
                        P.op("act", lambda e, n=n: e.activation(out=ev[n][:], in_=po[n][:], func=AF.Copy), reads=[Bpo[n]], writes=[Bev[n]])
                        dst = (KTD[cc] if cc < 4 else KTS[cc - 4])[:, g * 512:(g + 1) * 512]
                        P.op("sp", lambda e, n=n, dst=dst: e.dma_start(out=dst, in_=ev[n][:]), reads=[Bev[n]], dma=True)
                    for i in range(4):
                        for sec, dstT in ((1024, VD), (2560, VS)):
                            n = cnt[0] % 4
                            cnt[0] += 1
                            for k in range(8):
                                P.op("pe", lambda e, k=k, sec=sec, n=n, i=i: e.matmul(po[n][:], lhsT=hT[b][:, k, i * 128:(i + 1) * 128], rhs=win[:, k, sec:sec + 512], start=(k == 0), stop=(k == 7)),
                                     reads=[Bw, BhT[b]], writes=[Bpo[n]])
                            blk = 4 * g + i
                            P.op("dve", lambda e, n=n, blk=blk: e.tensor_scalar(out=ev[n][:], in0=po[n][:], scalar1=vflag[:, blk:blk + 1], scalar2=None, op0=ALU.mult),
                                 reads=[Bpo[n], B_const], writes=[Bev[n]])
                            dst = dstT[blk * 128:(blk + 1) * 128, :]
                            P.op("sp", lambda e, n=n, dst=dst: e.dma_start(out=dst, in_=ev[n][:]), reads=[Bev[n]], dma=True)
                    for cc in range(8):
                        col = cc * 128 if cc < 4 else 1536 + (cc - 4) * 128
                        n = cnt[0] % 4
                        cnt[0] += 1
                        for s in range(2):
                            for k in range(8):
                                P.op("pe", lambda e, k=k, col=col, n=n, s=s: e.matmul(po[n][:, s * 128:(s + 1) * 128], lhsT=win[:, k, col:col + 128], rhs=hT[b][:, k, s * 256:s * 256 + 128],
                                                                                start=(k == 0 and s == 0), stop=(k == 7), skip_group_check=True),
                                     reads=[Bw, BhT[b]], writes=[Bpo[n]])
                        P.op("act", lambda e, n=n: e.activation(out=ev[n][:, 0:256], in_=po[n][:, 0:256], func=AF.Copy, scale=0.125), reads=[Bpo[n]], writes=[Bev[n]])
                        dst = (QTD[cc] if cc < 4 else QTS[cc - 4])[:, g * 256:(g + 1) * 256]
                        P.op("sp", lambda e, n=n, dst=dst: e.dma_start(out=dst, in_=ev[n][:, 0:256]), reads=[Bev[n]], dma=True)

                load(0)
                for t in range(NG + 1):
                    if t + 1 < NG:
                        load(t + 1)
                    if t < NG:
                        pre(t)
                    if t >= 1:
                        main(t - 1)
                P.next_phase()

        def diff_phase():
            with contextlib.ExitStack() as es2:
                def sb2(name, shape, dt=F32):
                    return es2.enter_context(nc.sbuf_tensor(f"da_{name}", list(shape), dt))

                def ps2(name, shape, dt=F32):
                    return es2.enter_context(nc.psum_tensor(f"dap_{name}", list(shape), dt))
                KTa = [[sb2(f"KT{hb}{c}", [128, S], BF16) for c in range(2)] for hb in range(2)]
                QTa = [[sb2(f"QT{hb}{c}", [128, S // 2], BF16) for c in range(2)] for hb in range(2)]
                Va = [sb2(f"Va{hb}", [128, NBLK, 130], BF16) for hb in range(2)]
                dmb = sb2("dmb", [128, 3, 512], BF16)
                pT = [sb2(f"pT{i}", [128, 512], BF16) for i in range(4)]
                od = [sb2(f"od{i}", [128, 2, 128]) for i in range(3)]
                rs = [sb2(f"rs{i}", [128, 8]) for i in range(2)]
                junk = sb2("junk", [128, 128], BF16)
                pS = [ps2(f"pS{i}", [128, 512]) for i in range(3)]
                pO = [[ps2(f"pO{st}{i}", [128, 512]) for i in range(2)] for st in range(2)]
                Bop = [Buf("op0"), Buf("op1")]
                Bva1 = [Buf("va1_0"), Buf("va1_1")]
                Bdm = Buf("dm")
                BpT = [Buf(f"pT{i}") for i in range(4)]
                Bod = [Buf(f"od{i}") for i in range(3)]
                Brs = [Buf("rs0"), Buf("rs1")]
                Bjunk = Buf("junk")
                BpS = [Buf(f"pS{i}") for i in range(3)]
                BpO = [[Buf(f"pO{st}{i}") for i in range(2)] for st in range(2)]
                P.op("pool", lambda e: e.dma_start(out=dmb[:], in_=dmask_d.rearrange("j p n -> p j n")), writes=[Bdm], dma=True)
                for hb in range(2):
                    P.op("pool", lambda e, hb=hb: e.memset(Va[hb][:, :, 128:130], 1.0), writes=[Bva1[hb]])

                def load_head(hd):
                    hb = hd % 2
                    for c in range(2):
                        P.op("sp", lambda e, c=c: e.dma_start(out=KTa[hb][c][0:64, :], in_=KTD[hd][c * 64:(c + 1) * 64, :]), writes=[Bop[hb]], dma=True)
                        P.op("pool", lambda e, c=c: e.dma_start(out=KTa[hb][c][64:68, :].rearrange("r (a b) -> r a b", b=2048), in_=kx_d.rearrange("r (a b) -> r a b", b=2048)), writes=[Bop[hb]], dma=True)
                        P.op("sp", lambda e, c=c: e.dma_start(out=QTa[hb][c][0:64, :], in_=QTD[hd][c * 64:(c + 1) * 64, :]), writes=[Bop[hb]], dma=True)
                        P.op("pool", lambda e, c=c: e.dma_start(out=QTa[hb][c][64:68, :].rearrange("r (a b) -> r a b", b=2048), in_=qx_d[hd].rearrange("r (a b) -> r a b", b=2048)), writes=[Bop[hb]], dma=True)
                    for kq in range(8):
                        P.op("sp", lambda e, kq=kq: e.dma_start(out=Va[hb][:, kq * 8:(kq + 1) * 8, 0:128], in_=VD[kq * 1024:(kq + 1) * 1024, hd * 128:(hd + 1) * 128].rearrange("(k p) e -> p k e", p=128)), writes=[Bop[hb]], dma=True)

                qn = [0]
                gn = [0]

                def compute_head(hd):
                    hb = hd % 2
                    K0, K1 = KTa[hb]
                    Q0, Q1 = QTa[hb]
                    V = Va[hb]
                    rdeps = [Bop[hb], Bva1[hb]]
                    tiles = [(j, kb) for j in range(16) for kb in range(4 * j, NBLK)]
                    N = len(tiles)
                    base = qn[0]
                    qn[0] += N

                    def A(n):
                        j, kb = tiles[n]
                        q = (base + n) % 3
                        t4 = (base + n) % 4
                        psq = pS[q]
                        rel = kb - 4 * j
                        P.op("pe", lambda e: e.matmul(psq[:, 0:256], lhsT=K0[0:68, kb * 128:(kb + 1) * 128], rhs=Q0[0:68, j * 256:(j + 1) * 256], start=True, stop=False, skip_group_check=True),
                             reads=rdeps, writes=[BpS[q]])
                        P.op("pe", lambda e: e.matmul(psq[:, 256:512], lhsT=K1[0:68, kb * 128:(kb + 1) * 128], rhs=Q1[0:68, j * 256:(j + 1) * 256], start=False, stop=(rel >= 3), skip_group_check=True),
                             reads=rdeps, writes=[BpS[q]])
                        if rel < 3:
                            P.op("pe", lambda e: e.matmul(psq[:, 0:512], lhsT=identb[:], rhs=dmb[:, rel, :], start=False, stop=True, skip_group_check=True),
                                 reads=[Bdm, B_const], writes=[BpS[q]])
                        P.op("act", lambda e: e.activation(out=pT[t4][:], in_=psq[:], func=AF.Exp), reads=[BpS[q]], writes=[BpT[t4]])

                    def B(n):
                        j, kb = tiles[n]
                        t4 = (base + n) % 4
                        rel = kb - 4 * j
                        st = j % 2
                        for m in range(2):
                            pOm = pO[st][m]
                            for s in range(2):
                                if s == 1 and rel < 2:
                                    continue
                                P.op("pe", lambda e, m=m, s=s, pOm=pOm: e.matmul(pOm[:, s * 130:s * 130 + 129], lhsT=pT[t4][:, m * 256 + s * 128:m * 256 + (s + 1) * 128], rhs=V[:, kb, 0:129],
                                                                             start=(rel == 0 and s == 0), stop=(kb == NBLK - 1), skip_group_check=True),
                                     reads=[BpT[t4]] + rdeps, writes=[BpO[st][m]])
                        if kb == NBLK - 1:
                            norm(j, st)

                    def norm(j, st):
                        g3 = gn[0] % 3
                        g2 = gn[0] % 2
                        gn[0] += 1
                        o = od[g3]
                        r = rs[g2]
                        pO0, pO1 = pO[st]
                        for m in range(2):
                            for s in range(2):
                                P.op("dve", lambda e, m=m, s=s: e.reciprocal(out=r[:, 2 * m + s:2 * m + s + 1], in_=pO[st][m][:, s * 130 + 128:s * 130 + 129]), reads=[BpO[st][m]], writes=[Brs[g2]])
                        P.op("dve", lambda e: e.tensor_scalar(out=r[:, 4:6], in0=r[:, 2:4], scalar1=lamneg[:, 0:1], scalar2=None, op0=ALU.mult), reads=[Brs[g2], B_const], writes=[Brs[g2]])
                        for s in range(2):
                            P.op("dve", lambda e, s=s: e.tensor_scalar(out=o[:, s, :], in0=pO0[:, s * 130:s * 130 + 128], scalar1=r[:, s:s + 1], scalar2=None, op0=ALU.mult),
                                 reads=[BpO[st][0], Brs[g2]], writes=[Bod[g3]])
                            P.op("dve", lambda e, s=s: e.scalar_tensor_tensor(out=o[:, s, :], in0=pO1[:, s * 130:s * 130 + 128], scalar=r[:, 4 + s:5 + s], in1=o[:, s, :], op0=ALU.mult, op1=ALU.add),
                                 reads=[BpO[st][1], Brs[g2], Bod[g3]], writes=[Bod[g3]])
                            P.op("act", lambda e, s=s: e.activation(out=junk[:], in_=o[:, s, :], func=AF.Square, accum_out=r[:, 6 + s:7 + s]), reads=[Bod[g3]], writes=[Bjunk, Brs[g2]])
                        P.op("dve", lambda e: e.tensor_scalar(out=r[:, 6:8], in0=r[:, 6:8], scalar1=1.0 / 128, scalar2=EPS, op0=ALU.mult, op1=ALU.add), reads=[Brs[g2]], writes=[Brs[g2]])
                        P.op("pool", lambda e: e.tensor_tensor(out=r[:, 6:8], in0=r[:, 6:8], in1=mhalf[:, 0:2], op=ALU.pow), reads=[Brs[g2], B_const], writes=[Brs[g2]])
                        for s in range(2):
                            P.op("dve", lambda e, s=s: e.scalar_tensor_tensor(out=o[:, s, :], in0=o[:, s, :], scalar=r[:, 6 + s:7 + s], in1=sublnw[:], op0=ALU.mult, op1=ALU.mult),
                                 reads=[Bod[g3], Brs[g2], B_const], writes=[Bod[g3]])
                        dst = CAT[j * 256:(j + 1) * 256, hd * 128:(hd + 1) * 128].rearrange("(s p) e -> p s e", p=128)
                        P.op("sp", lambda e: e.dma_start(out=dst, in_=o[:]), reads=[Bod[g3]], dma=True)

                    for n in range(N + 2):
                        if n < N:
                            A(n)
                        if 0 <= n - 2 < N:
                            B(n - 2)

                load_head(0)
                for hd in range(4):
                    if hd + 1 < 4:
                        load_head(hd + 1)
                    compute_head(hd)
                P.next_phase()

        def sb_phase():
            with contextlib.ExitStack() as es2:
                def sb2(name, shape, dt=F32):
                    return es2.enter_context(nc.sbuf_tensor(f"sa_{name}", list(shape), dt))

                def ps2(name, shape, dt=F32):
                    return es2.enter_context(nc.psum_tensor(f"sap_{name}", list(shape), dt))
                KTs = [sb2(f"KT{hb}", [128, S], BF16) for hb in range(2)]
                Qz = [[sb2(f"Qz{hb}{h2}", [128, S // 2], BF16) for h2 in range(2)] for hb in range(2)]
                Vs = [sb2(f"Vs{hb}", [128, NBLK, 128], BF16) for hb in range(2)]
                osb = sb2("osb", [128, 32, 128])
                smb = sb2("smb", [128, 128], BF16)
                om = [sb2(f"om{i}", [128, 512]) for i in range(3)]
                cp = [sb2(f"cp{i}", [128, 513]) for i in range(4)]
                att = [sb2(f"att{i}", [128, 512], BF16) for i in range(4)]
                attT = [sb2(f"attT{i}", [128, 512], BF16) for i in range(3)]
                pz = [ps2(f"pz{i}", [128, 512]) for i in range(3)]
                pTt = [ps2(f"pTt{i}", [128, 1024], BF16) for i in range(2)]
                pOs = [ps2(f"pOs{i}", [128, 512]) for i in range(2)]
                Bop = [Buf("op0"), Buf("op1")]
                Bz = [Buf("z0"), Buf("z1")]
                Bosb = Buf("osb")
                Bsm = Buf("sm")
                Bom = [Buf(f"om{i}") for i in range(3)]
                Bcp = [Buf(f"cp{i}") for i in range(4)]
                Bc0 = [Buf(f"c0{i}") for i in range(4)]
                Batt = [Buf(f"att{i}") for i in range(4)]
                BattT = [Buf(f"attT{i}") for i in range(3)]
                Bpz = [Buf(f"pz{i}") for i in range(3)]
                BpTt = [Buf(f"pTt{i}") for i in range(2)]
                BpOs = [Buf(f"pOs{i}") for i in range(2)]
                P.op("pool", lambda e: e.dma_start(out=smb[:], in_=smask_d), writes=[Bsm], dma=True)
                for hb in range(2):
                    P.op("pool", lambda e, hb=hb: e.memset(Qz[hb][0][64:128, :], 0.0), writes=[Bz[hb]])
                    P.op("pool", lambda e, hb=hb: e.memset(Qz[hb][1][0:64, :], 0.0), writes=[Bz[hb]])

                def load_pair(ch):
                    hb = ch % 2
                    P.op("sp", lambda e: e.dma_start(out=KTs[hb][:], in_=KTS[ch]), writes=[Bop[hb]], dma=True)
                    P.op("sp", lambda e: e.dma_start(out=Qz[hb][0][0:64, :], in_=QTS[ch][0:64, :]), writes=[Bop[hb]], dma=True)
                    P.op("sp", lambda e: e.dma_start(out=Qz[hb][1][64:128, :], in_=QTS[ch][64:128, :]), writes=[Bop[hb]], dma=True)
                    for kq in range(8):
                        P.op("sp", lambda e, kq=kq: e.dma_start(out=Vs[hb][:, kq * 8:(kq + 1) * 8, :], in_=VS[kq * 1024:(kq + 1) * 1024, ch * 128:(ch + 1) * 128].rearrange("(k p) e -> p k e", p=128)), writes=[Bop[hb]], dma=True)

                state = {"n": 0, "slot": 0}

                def compute_pair(ch):
                    hb = ch % 2
                    KT = KTs[hb]
                    V = Vs[hb]
                    rdeps = [Bop[hb], Bz[hb]]
                    tasks = []
                    for h2 in range(2):
                        for i in range(32):
                            b0 = 2 * i
                            first = True
                            if i % 2 == 1:
                                tasks.append((h2, i, b0, 2, True, b0 + 2 >= NBLK))
                                b0 += 2
                                first = False
                            while b0 < NBLK:
                                tasks.append((h2, i, b0, 4, first, b0 + 4 >= NBLK))
                                first = False
                                b0 += 4
                    N = len(tasks)
                    base = state["n"]
                    slotmap = {}

                    def A(n):
                        h2, i, b0, nb, first, last = tasks[n]
                        gnn = base + n
                        W = nb * 128
                        z = pz[gnn % 3]
                        Q = Qz[hb][h2]
                        P.op("pe", lambda e: e.matmul(z[:, 0:W], lhsT=Q[:, i * 128:(i + 1) * 128], rhs=KT[:, b0 * 128:b0 * 128 + W], start=True, stop=(not first), skip_group_check=True),
                             reads=rdeps, writes=[Bpz[gnn % 3]])
                        if first:
                            P.op("pe", lambda e: e.matmul(z[:, 0:128], lhsT=identb[:], rhs=smb[:], start=False, stop=True, skip_group_check=True),
                                 reads=[Bsm, B_const], writes=[Bpz[gnn % 3]])
                        o = om[gnn % 3]
                        P.op("act", lambda e: e.activation(out=o[:, 0:W], in_=z[:, 0:W], func=AF.Sigmoid, scale=-1.0), reads=[Bpz[gnn % 3]], writes=[Bom[gnn % 3]])
                        c = cp[gnn % 4]
                        if first:
                            P.op("dve", lambda e: e.tensor_tensor_scan(out=c[:, 1:W + 1], data0=o[:, 0:W], data1=z[:, 0:W], initial=1.0, op0=ALU.mult, op1=ALU.bypass),
                                 reads=[Bom[gnn % 3], Bpz[gnn % 3]], writes=[Bcp[gnn % 4]])
                        else:
                            pc = cp[(gnn - 1) % 4]
                            pW = tasks[n - 1][3] * 128
                            P.op("dve", lambda e: e.tensor_tensor_scan(out=c[:, 1:W + 1], data0=o[:, 0:W], data1=z[:, 0:W], initial=pc[:, pW:pW + 1], op0=ALU.mult, op1=ALU.bypass),
                                 reads=[Bom[gnn % 3], Bpz[gnn % 3], Bcp[(gnn - 1) % 4]], writes=[Bcp[gnn % 4]])

                    def A2(n):
                        h2, i, b0, nb, first, last = tasks[n]
                        gnn = base + n
                        W = nb * 128
                        c = cp[gnn % 4]
                        if first:
                            P.op("pool", lambda e: e.memset(c[:, 0:1], 1.0), writes=[Bc0[gnn % 4]])
                        else:
                            pc = cp[(gnn - 1) % 4]
                            pW = tasks[n - 1][3] * 128
                            P.op("act", lambda e: e.activation(out=c[:, 0:1], in_=pc[:, pW:pW + 1], func=AF.Copy), reads=[Bcp[(gnn - 1) % 4]], writes=[Bc0[gnn % 4]])
                        a = att[gnn % 4]
                        P.op("pool", lambda e: e.tensor_tensor(out=a[:, 0:W], in0=c[:, 0:W], in1=c[:, 1:W + 1], op=ALU.subtract), reads=[Bcp[gnn % 4], Bc0[gnn % 4]], writes=[Batt[gnn % 4]])

                    def Bst(n):
                        h2, i, b0, nb, first, last = tasks[n]
                        gnn = base + n
                        W = nb * 128
                        a = att[gnn % 4]
                        t = pTt[gnn % 2]
                        for b in range(nb):
                            P.op("pe", lambda e, b=b: e.transpose(out=t[:, b * 128:(b + 1) * 128], in_=a[:, b * 128:(b + 1) * 128], identity=identb[:]),
                                 reads=[Batt[gnn % 4], B_const], writes=[BpTt[gnn % 2]])
                        aT = attT[gnn % 3]
                        P.op("act", lambda e: e.activation(out=aT[:, 0:W], in_=t[:, 0:W], func=AF.Copy), reads=[BpTt[gnn % 2]], writes=[BattT[gnn % 3]])

                    def C(n):
                        h2, i, b0, nb, first, last = tasks[n]
                        gnn = base + n
                        aT = attT[gnn % 3]
                        if first:
                            slotmap[(h2, i)] = state["slot"] % 2
                            state["slot"] += 1
                        sl = slotmap[(h2, i)]
                        for b in range(nb):
                            P.op("pe", lambda e, b=b: e.matmul(pOs[sl][:, 0:64], lhsT=aT[:, b * 128:(b + 1) * 128], rhs=V[:, b0 + b, h2 * 64:(h2 + 1) * 64], start=(first and b == 0), stop=(last and b == nb - 1)),
                                 reads=[BattT[gnn % 3]] + rdeps, writes=[BpOs[sl]])
                        if last:
                            P.op("dve", lambda e: e.tensor_copy(out=osb[:, i, h2 * 64:(h2 + 1) * 64], in_=pOs[sl][:, 0:64]), reads=[BpOs[sl]], writes=[Bosb])

                    for n in range(N + 5):
                        if n < N:
                            A(n)
                        if 0 <= n - 1 < N:
                            A2(n - 1)
                        if 0 <= n - 4 < N:
                            Bst(n - 4)
                        if 0 <= n - 5 < N:
                            C(n - 5)
                    state["n"] += N
                    for iq in range(4):
                        dst = CAT[iq * 1024:(iq + 1) * 1024, 512 + ch * 128:512 + (ch + 1) * 128].rearrange("(i p) e -> p i e", p=128)
                        P.op("sp", lambda e, iq=iq, dst=dst: e.dma_start(out=dst, in_=osb[:, iq * 8:(iq + 1) * 8, :]), reads=[Bosb], dma=True)

                load_pair(0)
                for ch in range(4):
                    if ch + 1 < 4:
                        load_pair(ch + 1)
                    compute_pair(ch)
                P.next_phase()

        def out_phase():
            with contextlib.ExitStack() as es2:
                def sb2(name, shape, dt=F32):
                    return es2.enter_context(nc.sbuf_tensor(f"op_{name}", list(shape), dt))

                def ps2(name, shape, dt=F32):
                    return es2.enter_context(nc.psum_tensor(f"opp_{name}", list(shape), dt))
                wout = sb2("wout", [128, 8, D], BF16)
                G = sb2("G", [128, D])
                ct = [sb2(f"ct{i}", [128, 4, D]) for i in range(2)]
                x1t = [sb2(f"x1t{i}", [128, 4, D]) for i in range(2)]
                cT = [sb2(f"cT{i}", [128, 8, 512], BF16) for i in range(2)]
                junk = sb2("junk", [128, 512], BF16)
                ss = [sb2(f"ss{i}", [128, 4]) for i in range(2)]
                st2 = [sb2(f"st2{i}", [128, 4]) for i in range(2)]
                tmp = [sb2(f"tmp{i}", [128, 512]) for i in range(2)]
                ptr = [ps2(f"ptr{i}", [128, 512]) for i in range(2)]
                po = [ps2(f"po{i}", [128, 512]) for i in range(4)]
                Bw, BG = Buf("w"), Buf("G")
                Bct = [Buf("ct0"), Buf("ct1")]
                Bx1 = [Buf("x10"), Buf("x11")]
                BcT = [Buf("cT0"), Buf("cT1")]
                Bjunk = Buf("junk")
                Bss = [Buf("ss0"), Buf("ss1")]
                Bst2 = [Buf("st20"), Buf("st21")]
                Btmp = [Buf("tmp0"), Buf("tmp1")]
                Bptr = [Buf("ptr0"), Buf("ptr1")]
                Bpo = [Buf(f"po{i}") for i in range(4)]
                for k in range(8):
                    P.op("pool", lambda e, k=k: e.dma_start(out=wout[:, k, :], in_=w_out[k * 128:(k + 1) * 128, :]), writes=[Bw], dma=True)
                P.op("sp", lambda e: e.dma_start(out=G[:], in_=GREP[1]), writes=[BG], dma=True)
                NG = 8
                X1v = X1.rearrange("(blk two p) d -> p blk two d", two=2, p=128)
                cnt = [0, 0, 0]

                def load(g):
                    b = g % 2
                    P.op("sp", lambda e: e.dma_start(out=ct[b][:], in_=CAT[g * 512:(g + 1) * 512, :].rearrange("(i p) d -> p i d", p=128)), writes=[Bct[b]], dma=True)
                    P.op("sp", lambda e: e.dma_start(out=x1t[b][:], in_=X1v[:, 4 * g:4 * g + 4, 0, :]), writes=[Bx1[b]], dma=True)

                def pre(g):
                    b = g % 2
                    for i in range(4):
                        P.op("act", lambda e, i=i: e.activation(out=junk[:], in_=ct[b][:, i, 512:1024], func=AF.Square, accum_out=ss[b][:, i:i + 1]),
                             reads=[Bct[b]], writes=[Bjunk, Bss[b]])
                    P.op("dve", lambda e: e.tensor_scalar(out=ss[b][:], in0=ss[b][:], scalar1=1.0 / 512, scalar2=EPS, op0=ALU.mult, op1=ALU.add), reads=[Bss[b]], writes=[Bss[b]])
                    P.op("pool", lambda e: e.tensor_tensor(out=ss[b][:], in0=ss[b][:], in1=mhalf[:, 0:4], op=ALU.pow), reads=[Bss[b], B_const], writes=[Bss[b]])
                    for i in range(4):
                        P.op("dve", lambda e, i=i: e.tensor_scalar(out=ct[b][:, i, 512:1024], in0=ct[b][:, i, 512:1024], scalar1=ss[b][:, i:i + 1], scalar2=None, op0=ALU.mult),
                             reads=[Bss[b], Bct[b]], writes=[Bct[b]])
                    for c in range(8):
                        pt = ptr[c % 2]
                        for i in range(4):
                            P.op("pe", lambda e, c=c, i=i, pt=pt: e.transpose(out=pt[:, i * 128:(i + 1) * 128], in_=ct[b][:, i, c * 128:(c + 1) * 128], identity=identf[:]),
                                 reads=[Bct[b], B_const], writes=[Bptr[c % 2]])
                        if c < 4:
                            P.op("act", lambda e, c=c, pt=pt: e.activation(out=cT[b][:, c, :], in_=pt[:], func=AF.Copy), reads=[Bptr[c % 2]], writes=[BcT[b]])
                        else:
                            P.op("dve", lambda e, c=c, pt=pt: e.tensor_scalar(out=cT[b][:, c, :], in0=pt[:], scalar1=sbbeta[:, c - 4:c - 3], scalar2=None, op0=ALU.mult),
                                 reads=[Bptr[c % 2], B_const], writes=[BcT[b]])

                def main(g):
                    b = g % 2
                    for i in range(4):
                        pp = []
                        for hf in range(2):
                            n = cnt[0] % 4
                            cnt[0] += 1
                            pp.append(n)
                            for c in range(8):
                                P.op("pe", lambda e, c=c, n=n, hf=hf, i=i: e.matmul(po[n][:], lhsT=cT[b][:, c, i * 128:(i + 1) * 128], rhs=wout[:, c, hf * 512:(hf + 1) * 512], start=(c == 0), stop=(c == 7)),
                                     reads=[BcT[b], Bw], writes=[Bpo[n]])
                        s2 = cnt[1] % 2
                        cnt[1] += 1
                        r = st2[s2]
                        for hf in range(2):
                            P.op("act", lambda e, r=r, hf=hf, n=pp[hf]: e.activation(out=junk[:], in_=po[n][:], func=AF.Square, accum_out=r[:, hf:hf + 1]), reads=[Bpo[pp[hf]]], writes=[Bjunk, Bst2[s2]])
                        P.op("dve", lambda e, r=r: e.tensor_tensor(out=r[:, 2:3], in0=r[:, 0:1], in1=r[:, 1:2], op=ALU.add), reads=[Bst2[s2]], writes=[Bst2[s2]])
                        P.op("dve", lambda e, r=r: e.tensor_scalar(out=r[:, 2:3], in0=r[:, 2:3], scalar1=1.0 / D, scalar2=EPS, op0=ALU.mult, op1=ALU.add), reads=[Bst2[s2]], writes=[Bst2[s2]])
                        P.op("pool", lambda e, r=r: e.tensor_tensor(out=r[:, 3:4], in0=r[:, 2:3], in1=mhalf[:, 0:1], op=ALU.pow), reads=[Bst2[s2], B_const], writes=[Bst2[s2]])
                        for hf in range(2):
                            tn = cnt[2] % 2
                            cnt[2] += 1
                            t = tmp[tn]
                            P.op("dve", lambda e, r=r, hf=hf, n=pp[hf], t=t: e.scalar_tensor_tensor(out=t[:], in0=po[n][:], scalar=r[:, 3:4], in1=G[:, hf * 512:(hf + 1) * 512], op0=ALU.mult, op1=ALU.mult),
                                 reads=[Bpo[pp[hf]], Bst2[s2], BG], writes=[Btmp[tn]])
                            P.op("dve", lambda e, r=r, hf=hf, t=t, i=i: e.tensor_tensor(out=x1t[b][:, i, hf * 512:(hf + 1) * 512], in0=x1t[b][:, i, hf * 512:(hf + 1) * 512], in1=t[:], op=ALU.add),
                                 reads=[Btmp[tn], Bx1[b]], writes=[Bx1[b]])
                    P.op("sp", lambda e: e.dma_start(out=X2[g * 512:(g + 1) * 512, :].rearrange("(i p) d -> p i d", p=128), in_=x1t[b][:]), reads=[Bx1[b]], dma=True)

                load(0)
                for t in range(NG + 1):
                    if t < NG:
                        pre(t)
                    if t >= 1:
                        main(t - 1)
                    if t + 1 < NG:
                        load(t + 1)
                P.next_phase()

        if stop_after >= 2:
            proj_phase()
        if stop_after >= 3:
            diff_phase()
        if stop_after >= 4:
            sb_phase()
        if stop_after >= 5:
            out_phase()
        if stop_after >= 6:
            ffn_phase("f2", lambda g: X2[g * 256:(g + 1) * 256, :], lambda g: out[g * 256:(g + 1) * 256, :], S // 2 // 256, 2, ffw[1])
        P.emit()
    return nc


def _tables():
    ident = np.eye(128, dtype=np.float32)
    r = np.arange(128)
    smask = np.where(r[None, :] <= r[:, None], NEG, 0.0).astype(np.float32)
    kk = r[:, None]
    qq = r[None, :]
    tri = np.where(kk >= qq, 0.0, NEG).astype(np.float32)
    full = np.zeros((128, 128), np.float32)
    none = np.full((128, 128), NEG, np.float32)
    dm = np.zeros((3, 128, 2, 2, 128), np.float32)
    for m in range(2):
        dm[0, :, m, 0] = tri
        dm[0, :, m, 1] = none
        dm[1, :, m, 0] = full
        dm[1, :, m, 1] = none
        dm[2, :, m, 0] = full
        dm[2, :, m, 1] = tri
    dmask = dm.reshape(3, 128, 512)
    return ident, smask, dmask


def make_in_maps(inputs):
    f = lambda a: np.ascontiguousarray(np.asarray(a, dtype=np.float32))
    x = f(inputs["x"])
    c = f(inputs["c"])
    ident, smask, dmask = _tables()
    pk = lambda v, n: np.ascontiguousarray(v.reshape(n, 128).T)
    gpre_pk = np.concatenate([pk(f(inputs[k])[0], 8) for k in ("ffn1_g_pre", "mix_g_pre", "ffn2_g_pre")], axis=1)
    gpost_row = np.stack([f(inputs[k])[0] for k in ("ffn1_g_post", "mix_g_post", "ffn2_g_post")], axis=0)
    lam_in = np.stack([f(inputs[k])[0] for k in ("lam_q1", "lam_k1", "lam_q2", "lam_k2")], axis=0)
    common = {
        "b_pk": pk(f(inputs["b_ada"])[0], 72),
        "b_row": f(inputs["b_ada"]),
        "w_ada": f(inputs["w_ada"])[0],
        "gpre_pk": np.ascontiguousarray(gpre_pk),
        "gpost_row": np.ascontiguousarray(gpost_row),
        "f1_wg": f(inputs["ffn1_w_gate"])[0], "f1_wu": f(inputs["ffn1_w_up"])[0], "f1_wd": f(inputs["ffn1_w_down"])[0],
        "f2_wg": f(inputs["ffn2_w_gate"])[0], "f2_wu": f(inputs["ffn2_w_up"])[0], "f2_wd": f(inputs["ffn2_w_down"])[0],
        "w_in": f(inputs["w_in"])[0], "w_out": f(inputs["w_out"])[0],
        "lam_in": np.ascontiguousarray(lam_in),
        "subln_row": f(inputs["diff_subln"]),
        "sbbeta_pk": pk(f(inputs["sb_beta"])[0], 4),
        "ident_d": ident, "smask_d": smask, "dmask_d": dmask,
    }
    pos = np.arange(S)
    qpos = (np.arange(S // 2) // 128) * 256 + (np.arange(S // 2) % 128)
    qx = np.zeros((4, 4, S // 2), np.float32)
    for h in range(4):
        sl = 2.0 ** (-2.0 * (h + 1))
        qx[h, 0] = sl * (qpos % 128)
        qx[h, 1] = sl * 128.0 * (qpos // 128)
        qx[h, 2] = -sl
        qx[h, 3] = -sl
    in_maps = []
    for core in range(8):
        b, hh = core // 2, core % 2
        xr = x[b, ::-1]
        if hh == 0:
            xin = np.ascontiguousarray(xr)
        else:
            xin = np.concatenate([xr[128:], np.zeros((128, D), np.float32)], axis=0)
        kx = np.zeros((4, S), np.float32)
        kx[0] = 1.0
        kx[1] = 1.0
        kx[2] = pos % 128
        kx[3] = 128.0 * (pos // 128)
        vflag = np.ones((128, NBLK), np.float32)
        if hh == 1:
            kx[3, S - 128:] = 2.0 ** 20
            vflag[:, NBLK - 1] = 0.0
        m = dict(common)
        m.update({"xin": xin, "c_pk": pk(c[b], 8), "kx_d": kx, "qx_d": qx, "vflag_d": vflag})
        in_maps.append(m)
    return in_maps


def assemble(outs):
    res = np.zeros((4, S, D), np.float32)
    for core in range(8):
        b, hh = core // 2, core % 2
        o = np.asarray(outs[core]).reshape(32, 128, D)
        for i in range(32):
            p0 = (2 * i + hh) * 128
            res[b, S - 1 - p0 - 127:S - p0] = o[i][::-1]
    return res


_NC_CACHE = {}


def kernel(**inputs):
    if "nc" not in _NC_CACHE:
        _NC_CACHE["nc"] = build_program()
    nc = _NC_CACHE["nc"]
    in_maps = make_in_maps(inputs)
    r = run_bass_kernel_spmd(nc, in_maps, core_ids=list(range(8)))
    return assemble([r.results[i]["out"] for i in range(8)])
```

```python
import numpy as np
import concourse.bass as bass
import concourse.mybir as mybir
from concourse.bass_utils import run_bass_kernel_spmd

F32 = mybir.dt.float32
BF16 = mybir.dt.bfloat16
AF = mybir.ActivationFunctionType
ALU = mybir.AluOpType
AX = mybir.AxisListType

ENGS = ["pe", "act", "dve", "pool", "sp"]
NDSEM = 8


class Buf:
    def __init__(self, name):
        self.name = name
        self.last_write = None
        self.reads = []


class Op:
    __slots__ = ("eng", "fn", "deps", "needed", "sem", "count", "dma", "phase", "prewait")

    def __init__(self, eng, fn, deps, dma, phase):
        self.eng = eng
        self.fn = fn
        self.deps = deps
        self.needed = False
        self.sem = None
        self.count = None
        self.dma = dma
        self.phase = phase
        self.prewait = None


class Prog:
    def __init__(self, nc):
        self.nc = nc
        self.streams = {e: [] for e in ENGS}
        self.phase = 0
        self.phase_ops = {}

    def op(self, eng, fn, reads=(), writes=(), dma=False):
        deps = []
        seen = set()

        def add(d):
            if d is not None and id(d) not in seen:
                seen.add(id(d))
                deps.append(d)

        for b in reads:
            add(b.last_write)
        for b in writes:
            add(b.last_write)
            for r in b.reads:
                add(r)
        o = Op(eng, fn, deps, dma, self.phase)
        for d in deps:
            if d.eng == "pe" and eng == "pe" and not d.dma and not dma:
                continue
            d.needed = True
        self.streams[eng].append(o)
        for b in reads:
            b.reads.append(o)
        for b in writes:
            b.last_write = o
            b.reads = []
        return o

    def barrier(self):
        deps = []
        for e in ENGS:
            st = self.streams[e]
            for o in reversed(st):
                if not o.dma and o.fn is not None:
                    deps.append(o)
                    break
            nd = 0
            for o in reversed(st):
                if o.dma:
                    deps.append(o)
                    nd += 1
                    if nd >= NDSEM:
                        break
        for d in deps:
            d.needed = True
        for e in ENGS:
            o = Op(e, None, list(deps), False, self.phase)
            self.streams[e].append(o)

    def next_phase(self):
        self.barrier()
        self.phase += 1

    def emit(self):
        nc = self.nc
        nph = self.phase + 1
        self._cms = []
        sems = {}
        for ph in range(nph):
            for e in ENGS:
                cm = nc.semaphore(f"s_{e}_{ph}")
                sems[(e, ph)] = cm.__enter__()
                self._cms.append(cm)
        dsems = {}
        for e in ENGS:
            for i in range(NDSEM):
                cm = nc.semaphore(f"d_{e}_{i}")
                dsems[(e, i)] = cm.__enter__()
                self._cms.append(cm)
        final = {}
        for e in ENGS:
            cnt = {}
            nd = 0
            for o in self.streams[e]:
                if o.dma:
                    slot = nd % NDSEM
                    o.sem = dsems[(e, slot)]
                    o.count = 16 * (nd // NDSEM + 1)
                    o.prewait = (o.sem, 16 * (nd // NDSEM)) if nd >= NDSEM else None
                    nd += 1
                elif o.needed:
                    c = cnt.get(o.phase, 0) + 1
                    cnt[o.phase] = c
                    o.sem = sems[(e, o.phase)]
                    o.count = c
                    assert c < 30000, (e, o.phase, c)
            for ph, c in cnt.items():
                final[(e, ph)] = c
        self.final = final
        engobj = {"pe": nc.tensor, "act": nc.scalar, "dve": nc.vector, "pool": nc.gpsimd, "sp": nc.sync}
        streams = self.streams

        def run(ename, eng):
            known = {}
            for o in streams[ename]:
                waits = {}
                for d in o.deps:
                    if d.eng == "pe" and ename == "pe" and not d.dma and not o.dma and o.fn is not None:
                        continue
                    assert d.sem is not None, (d.eng, ename)
                    k = id(d.sem)
                    if known.get(k, 0) >= d.count:
                        continue
                    if k not in waits or waits[k][1] < d.count:
                        waits[k] = (d.sem, d.count)
                if o.prewait is not None:
                    k = id(o.prewait[0])
                    if known.get(k, 0) < o.prewait[1]:
                        if k not in waits or waits[k][1] < o.prewait[1]:
                            waits[k] = o.prewait
                for k, (s, c) in waits.items():
                    eng.wait_ge(s, c)
                    known[k] = c
                if o.fn is None:
                    continue
                inst = o.fn(eng)
                if o.dma:
                    inst.then_inc(o.sem, 16)
                elif o.needed:
                    inst.then_inc(o.sem, 1)

        with nc.Block() as block:
            @block.tensor
            def _(e):
                run("pe", e)

            @block.scalar
            def _(e):
                run("act", e)

            @block.vector
            def _(e):
                run("dve", e)

            @block.gpsimd
            def _(e):
                run("pool", e)

            @block.sync
            def _(e):
                run("sp", e)
        for cm in reversed(self._cms):
            cm.__exit__(None, None, None)


D = 1024
S = 8192
NBLK = 64
DFF = 2816
NFF = 22
EPS = 1e-6
NEG = -32768.0
LAMBDA_INIT = 0.2


def rr(i, n):
    return i % n


def build_program(debug=False, stop_after=99):
    nc = bass.Bass("TRN2", target_bir_lowering=False)
    P = Prog(nc)

    def din(name, shape, dt=F32):
        return nc.dram_tensor(name, list(shape), dt, kind="ExternalInput").ap()

    xin = din("xin", [S, D])
    c_pk = din("c_pk", [128, 8])
    b_pk = din("b_pk", [128, 72])
    b_row = din("b_row", [1, 9 * D])
    w_ada = din("w_ada", [D, 9 * D])
    gpre_pk = din("gpre_pk", [128, 24])
    gpost_row = din("gpost_row", [3, D])
    ffw = []
    for f in (1, 2):
        ffw.append((din(f"f{f}_wg", [D, DFF]), din(f"f{f}_wu", [D, DFF]), din(f"f{f}_wd", [DFF, D])))
    w_in = din("w_in", [D, 3 * D])
    w_out = din("w_out", [D, D])
    lam_in = din("lam_in", [4, 64])
    subln_row = din("subln_row", [1, 128])
    sbbeta_pk = din("sbbeta_pk", [128, 4])
    ident_d = din("ident_d", [128, 128])
    smask_d = din("smask_d", [128, 128])
    dmask_d = din("dmask_d", [3, 128, 512])
    kx_d = din("kx_d", [4, S])
    qx_d = din("qx_d", [4, 4, S // 2])
    vflag_d = din("vflag_d", [128, NBLK])

    out = nc.dram_tensor("out", [S // 2, D], F32, kind="ExternalOutput").ap()
    skind = "ExternalOutput" if debug else "Internal"

    def dscr(name, shape, dt, dbg=False):
        return nc.dram_tensor(name, list(shape), dt, kind=(skind if dbg else "Internal")).ap()

    DBG = dscr("DBG", [4, 128, 512], F32, True)
    DBGB = dscr("DBGB", [8, 256], BF16, True)
    X1 = dscr("X1", [S, D], F32, True)
    CAT = dscr("CAT", [S // 2, D], F32, True)
    X2 = dscr("X2", [S // 2, D], F32, True)
    GREP = dscr("GREP", [3, 128, D], F32)
    KTD = dscr("KTD", [4, 128, S], BF16)
    KTS = dscr("KTS", [4, 128, S], BF16)
    QTD = dscr("QTD", [4, 128, S // 2], BF16)
    QTS = dscr("QTS", [4, 128, S // 2], BF16)
    VD = dscr("VD", [S, 512], BF16)
    VS = dscr("VS", [S, 512], BF16)

    import contextlib
    es = contextlib.ExitStack()

    def sb(name, shape, dt=F32):
        return es.enter_context(nc.sbuf_tensor(name, list(shape), dt))

    def ps(name, shape, dt=F32):
        return es.enter_context(nc.psum_tensor(name, list(shape), dt))

    with es:
        identf = sb("identf", [128, 128], F32)
        identb = sb("identb", [128, 128], BF16)
        modT = sb("modT", [128, 72], F32)
        Aco = sb("Aco", [128, 24], F32)
        gpre = sb("gpre", [128, 24], F32)
        mhalf = sb("mhalf", [128, 8], F32)
        lamneg = sb("lamneg", [128, 1], F32)
        sublnw = sb("sublnw", [128, 128], F32)
        sbbeta = sb("sbbeta", [128, 4], F32)
        vflag = sb("vflag", [128, NBLK], F32)
        B_const = Buf("const")
        B_modT = Buf("modT")

        P.op("sp", lambda e: e.dma_start(out=identf[:], in_=ident_d), writes=[B_const], dma=True)
        P.op("pool", lambda e: e.dma_start(out=identb[:], in_=ident_d), writes=[B_const], dma=True)
        P.op("sp", lambda e: e.dma_start(out=gpre[:], in_=gpre_pk), writes=[B_const], dma=True)
        P.op("sp", lambda e: e.dma_start(out=sbbeta[:], in_=sbbeta_pk), writes=[B_const], dma=True)
        P.op("sp", lambda e: e.dma_start(out=vflag[:], in_=vflag_d), writes=[B_const], dma=True)
        P.op("sp", lambda e: e.dma_start(out=sublnw[:], in_=subln_row.partition_broadcast(128)), writes=[B_const], dma=True)
        P.op("pool", lambda e: e.memset(mhalf[:], -0.5), writes=[B_const])

        with contextlib.ExitStack() as es2:
            def sb2(name, shape, dt=F32):
                return es2.enter_context(nc.sbuf_tensor(name, list(shape), dt))

            def ps2(name, shape, dt=F32):
                return es2.enter_context(nc.psum_tensor(name, list(shape), dt))
            ct = sb2("ct", [128, 8])
            sfl = sb2("sfl", [128, 8])
            sbf = sb2("sbf", [128, 8], BF16)
            onesb = sb2("onesb", [128, 128], BF16)
            srep = sb2("srep", [128, 8, 128], BF16)
            bpk = sb2("bpk", [128, 72])
            wa = [sb2(f"wa{i}", [128, 8, D], BF16) for i in range(2)]
            brow = sb2("brow", [128, D])
            gprow = sb2("gprow", [128, D])
            grep = sb2("grep", [128, D])
            lamt = sb2("lamt", [128, 4, 64])
            lamp = sb2("lamp", [128, 2, 64])
            lams = sb2("lams", [128, 2])
            pmod = ps2("pmod", [128, 512])
            pgr = [ps2(f"pgr{i}", [128, 512]) for i in range(2)]
            Bc, Bs, Bsrep, Bbpk, Bpm = Buf("c"), Buf("s"), Buf("srep"), Buf("bpk"), Buf("pm")
            Bwa = [Buf("wa0"), Buf("wa1")]
            Bbrow, Bgprow, Bgrep, Bpgr = Buf("brow"), Buf("gprow"), Buf("grep"), [Buf("pgr0"), Buf("pgr1")]
            Blam = Buf("lam")
            P.op("sp", lambda e: e.dma_start(out=ct[:], in_=c_pk), writes=[Bc], dma=True)
            P.op("sp", lambda e: e.dma_start(out=bpk[:], in_=b_pk), writes=[Bbpk], dma=True)
            P.op("act", lambda e: e.activation(out=sfl[:], in_=ct[:], func=AF.Silu), reads=[Bc], writes=[Bs])
            P.op("dve", lambda e: e.tensor_copy(out=sbf[:], in_=sfl[:]), reads=[Bs], writes=[Bs])
            P.op("dve", lambda e: e.memset(onesb[:], 1.0), writes=[Bsrep])
            for k in range(8):
                P.op("dve", lambda e, k=k: e.tensor_scalar(out=srep[:, k, :], in0=onesb[:], scalar1=sfl[:, k:k + 1], scalar2=None, op0=ALU.mult),
                     reads=[Bs, Bsrep], writes=[Bsrep])
            P.op("sp", lambda e: e.dma_start(out=lamt[:].rearrange("p a b -> p (a b)"), in_=lam_in.rearrange("a b -> (a b)").rearrange("(o n) -> o n", o=1).partition_broadcast(128)), writes=[Blam], dma=True)
            P.op("dve", lambda e: e.tensor_tensor(out=lamp[:, 0, :], in0=lamt[:, 0, :], in1=lamt[:, 1, :], op=ALU.mult), reads=[Blam], writes=[Blam])
            P.op("dve", lambda e: e.tensor_tensor(out=lamp[:, 1, :], in0=lamt[:, 2, :], in1=lamt[:, 3, :], op=ALU.mult), reads=[Blam], writes=[Blam])
            P.op("dve", lambda e: e.reduce_sum(out=lams[:], in_=lamp[:], axis=AX.X), reads=[Blam], writes=[Blam])
            P.op("act", lambda e: e.activation(out=lams[:], in_=lams[:], func=AF.Exp), reads=[Blam], writes=[Blam])
            P.op("dve", lambda e: e.tensor_tensor(out=lamneg[:], in0=lams[:, 1:2], in1=lams[:, 0:1], op=ALU.subtract), reads=[Blam], writes=[Blam])
            P.op("dve", lambda e: e.tensor_scalar(out=lamneg[:], in0=lamneg[:], scalar1=-LAMBDA_INIT, scalar2=None, op0=ALU.add), reads=[Blam], writes=[B_const, Blam])
            P.op("dve", lambda e: e.tensor_scalar(out=sublnw[:], in0=sublnw[:], scalar1=1.0 - LAMBDA_INIT, scalar2=None, op0=ALU.mult), reads=[B_const], writes=[B_const])

            for v in range(9):
                w = wa[v % 2]
                Bw = Bwa[v % 2]
                for k in range(8):
                    P.op("pool", lambda e, k=k, v=v, w=w: e.dma_start(out=w[:, k, :], in_=w_ada[k * 128:(k + 1) * 128, v * D:(v + 1) * D]),
                         writes=[Bw], dma=True)
                for jc in range(8):
                    j = v * 8 + jc
                    for k in range(8):
                        P.op("pe", lambda e, j=j, jc=jc, k=k, w=w: e.matmul(pmod[:, j:j + 1], lhsT=w[:, k, jc * 128:(jc + 1) * 128], rhs=sbf[:, k:k + 1],
                                                                         start=(k == 0), stop=(k == 7)),
                             reads=[Bw, Bs], writes=[Bpm])
                if v % 3 == 2:
                    gi = v // 3
                    resw = 1.0 if gi == 1 else 0.5
                    P.op("sp", lambda e, v=v: e.dma_start(out=brow[:], in_=b_row[0:1, v * D:(v + 1) * D].partition_broadcast(128)), writes=[Bbrow], dma=True)
                    P.op("sp", lambda e, gi=gi: e.dma_start(out=gprow[:], in_=gpost_row[gi:gi + 1, :].partition_broadcast(128)), writes=[Bgprow], dma=True)
                    for hf in range(2):
                        pg = pgr[hf]
                        for k in range(8):
                            P.op("pe", lambda e, k=k, hf=hf, pg=pg, w=w: e.matmul(pg[:], lhsT=srep[:, k, :], rhs=w[:, k, hf * 512:(hf + 1) * 512],
                                                                              start=(k == 0), stop=(k == 7)),
                                 reads=[Bw, Bsrep], writes=[Bpgr[hf]])
                        P.op("dve", lambda e, hf=hf, pg=pg: e.tensor_tensor(out=grep[:, hf * 512:(hf + 1) * 512], in0=pg[:], in1=brow[:, hf * 512:(hf + 1) * 512], op=ALU.add),
                             reads=[Bpgr[hf], Bbrow], writes=[Bgrep])
                    P.op("dve", lambda e, resw=resw: e.scalar_tensor_tensor(out=grep[:], in0=grep[:], scalar=resw, in1=gprow[:], op0=ALU.mult, op1=ALU.mult),
                         reads=[Bgrep, Bgprow], writes=[Bgrep])
                    P.op("sp", lambda e, gi=gi: e.dma_start(out=GREP[gi], in_=grep[:]), reads=[Bgrep], dma=True)
            P.op("dve", lambda e: e.tensor_tensor(out=modT[:], in0=pmod[:, 0:72], in1=bpk[:], op=ALU.add), reads=[Bpm, Bbpk], writes=[B_modT])
            for i in range(3):
                P.op("dve", lambda e, i=i: e.scalar_tensor_tensor(out=Aco[:, i * 8:(i + 1) * 8], in0=modT[:, (3 * i + 1) * 8:(3 * i + 2) * 8], scalar=1.0,
                                                                 in1=gpre[:, i * 8:(i + 1) * 8], op0=ALU.add, op1=ALU.mult),
                     reads=[B_modT, B_const], writes=[B_modT])
            P.next_phase()

        def Bsh(i):
            return modT[:, (3 * i) * 8:(3 * i) * 8 + 8]

        def ffn_phase(tag, src_rows, dst_rows, ngroups, li, wts):
            with contextlib.ExitStack() as es2:
                def sb2(name, shape, dt=F32):
                    return es2.enter_context(nc.sbuf_tensor(f"{tag}s_{name}", list(shape), dt))

                def ps2(name, shape, dt=F32):
                    return es2.enter_context(nc.psum_tensor(f"{tag}p_{name}", list(shape), dt))
                wg = sb2("wg", [128, 8, DFF], BF16)
                wu = sb2("wu", [128, 8, DFF], BF16)
                wd = sb2("wd", [128, NFF, D], BF16)
                G = sb2("G", [128, D])
                xt = [sb2(f"xt{i}", [128, 2, D]) for i in range(3)]
                hT = [sb2(f"hT{i}", [128, 8, 256], BF16) for i in range(2)]
                junk = sb2("junk", [128, D], BF16)
                ss = [sb2(f"ss{i}", [128, 2]) for i in range(2)]
                rstd = [sb2(f"rstd{i}", [128, 2]) for i in range(2)]
                sd = [sb2(f"sd{i}", [128, 2]) for i in range(2)]
                ss2 = [sb2(f"ss2{i}", [128, 4]) for i in range(2)]
                rstd2 = [sb2(f"rstd2{i}", [128, 2]) for i in range(2)]
                sg = [sb2(f"sg{i}", [128, 256]) for i in range(2)]
                mj = [sb2(f"mj{i}", [128, 256], BF16) for i in range(3)]
                tmp = [sb2(f"tmp{i}", [128, 512]) for i in range(2)]
                ptr = [ps2(f"ptr{i}", [128, 2, 256]) for i in range(2)]
                pgu = [ps2(f"pgu{i}", [128, 512]) for i in range(2)]
                py = [[ps2(f"py{i}{h}", [128, 512]) for h in range(2)] for i in range(2)]
                Bw = Buf("w")
                BG = Buf("G")
                Bxt = [Buf("xt0"), Buf("xt1"), Buf("xt2")]
                BhT = [Buf("hT0"), Buf("hT1")]
                Bjunk = Buf("junk")
                Bst = [Buf("st0"), Buf("st1")]
                Bst2 = [Buf("st20"), Buf("st21")]
                Bsg = [Buf("sg0"), Buf("sg1")]
                Bmj = [Buf(f"mj{i}") for i in range(3)]
                Btmp = [Buf("tmp0"), Buf("tmp1")]
                Bptr = [Buf("ptr0"), Buf("ptr1")]
                Bpgu = [Buf("pgu0"), Buf("pgu1")]
                Bpy = [[Buf(f"py{i}{h}") for h in range(2)] for i in range(2)]
                w_g, w_u, w_d = wts
                for k in range(8):
                    for hf in range(2):
                        P.op("pool", lambda e, k=k, hf=hf: e.dma_start(out=wg[:, k, hf * 1408:(hf + 1) * 1408], in_=w_g[k * 128:(k + 1) * 128, hf * 1408:(hf + 1) * 1408]), writes=[Bw], dma=True)
                        P.op("pool", lambda e, k=k, hf=hf: e.dma_start(out=wu[:, k, hf * 1408:(hf + 1) * 1408], in_=w_u[k * 128:(k + 1) * 128, hf * 1408:(hf + 1) * 1408]), writes=[Bw], dma=True)
                for j in range(NFF):
                    P.op("pool", lambda e, j=j: e.dma_start(out=wd[:, j, :], in_=w_d[j * 128:(j + 1) * 128, :]), writes=[Bw], dma=True)
                P.op("sp", lambda e: e.dma_start(out=G[:], in_=GREP[li]), writes=[BG], dma=True)
                A = Aco[:, li * 8:(li + 1) * 8]
                Bv = Bsh(li)

                def stage_load(g):
                    x3 = g % 3
                    P.op("sp", lambda e: e.dma_start(out=xt[x3][:], in_=src_rows(g).rearrange("(i p) d -> p i d", p=128)), writes=[Bxt[x3]], dma=True)

                def stage_pre(g):
                    b = g % 2
                    x3 = g % 3
                    for i in range(2):
                        P.op("act", lambda e, i=i: e.activation(out=junk[:], in_=xt[x3][:, i, :], func=AF.Square, accum_out=ss[b][:, i:i + 1]),
                             reads=[Bxt[x3]], writes=[Bjunk, Bst[b]])
                    P.op("dve", lambda e: e.tensor_scalar(out=sd[b][:], in0=ss[b][:], scalar1=1.0 / D, scalar2=EPS, op0=ALU.mult, op1=ALU.add),
                         reads=[Bst[b]], writes=[Bst[b]])
                    P.op("pool", lambda e: e.tensor_tensor(out=rstd[b][:], in0=sd[b][:], in1=mhalf[:, 0:2], op=ALU.pow), reads=[Bst[b], B_const], writes=[Bst[b]])
                    P.op("dve", lambda e: e.reciprocal(out=sd[b][:], in_=rstd[b][:]), reads=[Bst[b]], writes=[Bst[b]])
                    for i in range(2):
                        P.op("dve", lambda e, i=i: e.tensor_scalar(out=xt[x3][:, i, :], in0=xt[x3][:, i, :], scalar1=rstd[b][:, i:i + 1], scalar2=None, op0=ALU.mult),
                             reads=[Bst[b], Bxt[x3]], writes=[Bxt[x3]])
                def stage_pre_b(g):
                    b = g % 2
                    x3 = g % 3
                    for r in range(4):
                        pt = ptr[r % 2]
                        for cc in range(2):
                            c = 2 * r + cc
                            for i in range(2):
                                P.op("pe", lambda e, c=c, cc=cc, i=i, pt=pt: e.transpose(out=pt[:, cc, i * 128:(i + 1) * 128], in_=xt[x3][:, i, c * 128:(c + 1) * 128], identity=identf[:]),
                                     reads=[Bxt[x3], B_const], writes=[Bptr[r % 2]])
                        for cc in range(2):
                            c = 2 * r + cc
                            P.op("dve", lambda e, c=c, cc=cc, pt=pt: e.tensor_scalar(out=hT[b][:, c, :], in0=pt[:, cc, :], scalar1=A[:, c:c + 1], scalar2=Bv[:, c:c + 1], op0=ALU.mult, op1=ALU.add),
                                 reads=[Bptr[r % 2], B_modT], writes=[BhT[b]])

                def gu(g, j):
                    b = g % 2
                    pg = pgu[j % 2]
                    for k in range(8):
                        P.op("pe", lambda e, k=k: e.matmul(pg[:, 0:256], lhsT=wg[:, k, j * 128:(j + 1) * 128], rhs=hT[b][:, k, :], start=(k == 0), stop=False, skip_group_check=True),
                             reads=[Bw, BhT[b]], writes=[Bpgu[j % 2]])
                    for k in range(8):
                        P.op("pe", lambda e, k=k: e.matmul(pg[:, 256:512], lhsT=wu[:, k, j * 128:(j + 1) * 128], rhs=hT[b][:, k, :], start=False, stop=(k == 7), skip_group_check=True),
                             reads=[Bw, BhT[b]], writes=[Bpgu[j % 2]])
                    P.op("act", lambda e: e.activation(out=sg[j % 2][:], in_=pg[:, 0:256], func=AF.Silu), reads=[Bpgu[j % 2]], writes=[Bsg[j % 2]])
                    P.op("dve", lambda e: e.tensor_tensor(out=mj[j % 3][:], in0=sg[j % 2][:], in1=pg[:, 256:512], op=ALU.mult),
                         reads=[Bsg[j % 2], Bpgu[j % 2]], writes=[Bmj[j % 3]])

                def down(g, j):
                    for i in range(2):
                        for hf in range(2):
                            P.op("pe", lambda e, i=i, hf=hf: e.matmul(py[i][hf][:], lhsT=mj[j % 3][:, i * 128:(i + 1) * 128], rhs=wd[:, j, hf * 512:(hf + 1) * 512],
                                                                   start=(j == 0), stop=(j == NFF - 1)),
                                 reads=[Bmj[j % 3], Bw], writes=[Bpy[i][hf]])

                def stage_main(g, hook=None):
                    b = g % 2
                    x3 = g % 3
                    gu(g, 0)
                    for j in range(NFF):
                        if j + 1 < NFF:
                            gu(g, j + 1)
                        down(g, j)
                        if hook is not None and j == 3:
                            hook[0]()
                        if hook is not None and j == 12:
                            hook[1]()
                    for i in range(2):
                        for hf in range(2):
                            P.op("act", lambda e, i=i, hf=hf: e.activation(out=junk[:, 0:512], in_=py[i][hf][:], func=AF.Square, accum_out=ss2[b][:, 2 * i + hf:2 * i + hf + 1]),
                                 reads=[Bpy[i][hf]], writes=[Bjunk, Bst2[b]])
                    P.op("dve", lambda e: e.tensor_tensor(out=rstd2[b][:], in0=ss2[b][:, 0:4:2], in1=ss2[b][:, 1:4:2], op=ALU.add), reads=[Bst2[b]], writes=[Bst2[b]])
                    P.op("dve", lambda e: e.tensor_scalar(out=rstd2[b][:], in0=rstd2[b][:], scalar1=1.0 / D, scalar2=EPS, op0=ALU.mult, op1=ALU.add), reads=[Bst2[b]], writes=[Bst2[b]])
                    P.op("pool", lambda e: e.tensor_tensor(out=rstd2[b][:], in0=rstd2[b][:], in1=mhalf[:, 0:2], op=ALU.pow), reads=[Bst2[b], B_const], writes=[Bst2[b]])
                    n = 0
                    for i in range(2):
                        for hf in range(2):
                            t = tmp[n % 2]
                            Bt = Btmp[n % 2]
                            n += 1
                            P.op("dve", lambda e, i=i, hf=hf, t=t: e.scalar_tensor_tensor(out=t[:], in0=py[i][hf][:], scalar=rstd2[b][:, i:i + 1], in1=G[:, hf * 512:(hf + 1) * 512], op0=ALU.mult, op1=ALU.mult),
                                 reads=[Bpy[i][hf], Bst2[b], BG], writes=[Bt])
                            P.op("dve", lambda e, i=i, hf=hf, t=t: e.scalar_tensor_tensor(out=xt[x3][:, i, hf * 512:(hf + 1) * 512], in0=xt[x3][:, i, hf * 512:(hf + 1) * 512], scalar=sd[b][:, i:i + 1], in1=t[:], op0=ALU.mult, op1=ALU.add),
                                 reads=[Bt, Bst[b], Bxt[x3]], writes=[Bxt[x3]])
                    P.op("sp", lambda e: e.dma_start(out=dst_rows(g).rearrange("(i p) d -> p i d", p=128), in_=xt[x3][:]), reads=[Bxt[x3]], dma=True)

                for t in range(ngroups + 2):
                    if t < ngroups:
                        stage_load(t)
                    has_pre = 0 <= t - 1 < ngroups
                    if 0 <= t - 2 < ngroups:
                        stage_main(t - 2, hook=((lambda t=t: stage_pre(t - 1)), (lambda t=t: stage_pre_b(t - 1))) if has_pre else None)
                    elif has_pre:
                        stage_pre(t - 1)
                        stage_pre_b(t - 1)
                P.next_phase()

        if stop_after >= 1:
            ng1 = S // 256
            ffn_phase("f1", lambda g: xin[g * 256:(g + 1) * 256, :], lambda g: X1[g * 256:(g + 1) * 256, :], ng1, 0, ffw[0])
        def proj_phase():
            with contextlib.ExitStack() as es2:
                def sb2(name, shape, dt=F32):
                    return es2.enter_context(nc.sbuf_tensor(f"pj_{name}", list(shape), dt))

                def ps2(name, shape, dt=F32):
                    return es2.enter_context(nc.psum_tensor(f"pjp_{name}", list(shape), dt))
                win = sb2("win", [128, 8, 3 * D], BF16)
                xt = [sb2(f"xt{i}", [128, 4, D]) for i in range(2)]
                hT = [sb2(f"hT{i}", [128, 8, 512], BF16) for i in range(2)]
                junk = sb2("junk", [128, D], BF16)
                ss = [sb2(f"ss{i}", [128, 4]) for i in range(2)]
                rstd = [sb2(f"rstd{i}", [128, 4]) for i in range(2)]
                ev = [sb2(f"ev{i}", [128, 512], BF16) for i in range(4)]
                ptr = [ps2(f"ptr{i}", [128, 512]) for i in range(2)]
                po = [ps2(f"po{i}", [128, 512]) for i in range(4)]
                Bw = Buf("win")
                Bxt = [Buf("xt0"), Buf("xt1")]
                BhT = [Buf("hT0"), Buf("hT1")]
                Bjunk = Buf("junk")
                Bst = [Buf("st0"), Buf("st1")]
                Bev = [Buf(f"ev{i}") for i in range(4)]
                Bptr = [Buf("ptr0"), Buf("ptr1")]
                Bpo = [Buf(f"po{i}") for i in range(4)]
                for k in range(8):
                    for t3 in range(3):
                        P.op("pool", lambda e, k=k, t3=t3: e.dma_start(out=win[:, k, t3 * D:(t3 + 1) * D], in_=w_in[k * 128:(k + 1) * 128, t3 * D:(t3 + 1) * D]), writes=[Bw], dma=True)
                A = Aco[:, 8:16]
                Bv = Bsh(1)
                NG = S // 512
                cnt = [0]

                def load(g):
                    b = g % 2
                    P.op("sp", lambda e: e.dma_start(out=xt[b][:], in_=X1[g * 512:(g + 1) * 512, :].rearrange("(i p) d -> p i d", p=128)), writes=[Bxt[b]], dma=True)

                def pre(g):
                    b = g % 2
                    for i in range(4):
                        P.op("act", lambda e, i=i: e.activation(out=junk[:], in_=xt[b][:, i, :], func=AF.Square, accum_out=ss[b][:, i:i + 1]),
                             reads=[Bxt[b]], writes=[Bjunk, Bst[b]])
                    P.op("dve", lambda e: e.tensor_scalar(out=ss[b][:], in0=ss[b][:], scalar1=1.0 / D, scalar2=EPS, op0=ALU.mult, op1=ALU.add), reads=[Bst[b]], writes=[Bst[b]])
                    P.op("pool", lambda e: e.tensor_tensor(out=rstd[b][:], in0=ss[b][:], in1=mhalf[:, 0:4], op=ALU.pow), reads=[Bst[b], B_const], writes=[Bst[b]])
                    for i in range(4):
                        P.op("dve", lambda e, i=i: e.tensor_scalar(out=xt[b][:, i, :], in0=xt[b][:, i, :], scalar1=rstd[b][:, i:i + 1], scalar2=None, op0=ALU.mult),
                             reads=[Bst[b], Bxt[b]], writes=[Bxt[b]])
                    for c in range(8):
                        pt = ptr[c % 2]
                        for i in range(4):
                            P.op("pe", lambda e, c=c, i=i, pt=pt: e.transpose(out=pt[:, i * 128:(i + 1) * 128], in_=xt[b][:, i, c * 128:(c + 1) * 128], identity=identf[:]),
                                 reads=[Bxt[b], B_const], writes=[Bptr[c % 2]])
                        P.op("dve", lambda e, c=c, pt=pt: e.tensor_scalar(out=hT[b][:, c, :], in0=pt[:], scalar1=A[:, c:c + 1], scalar2=Bv[:, c:c + 1], op0=ALU.mult, op1=ALU.add),
                             reads=[Bptr[c % 2], B_modT], writes=[BhT[b]])

                def main(g):
                    b = g % 2
                    for cc in range(8):
                        col = 512 + cc * 128 if cc < 4 else 2048 + (cc - 4) * 128
                        n = cnt[0] % 4
                        cnt[0] += 1
                        for k in range(8):
                            P.op("pe", lambda e, k=k, col=col, n=n: e.matmul(po[n][:], lhsT=win[:, k, col:col + 128], rhs=hT[b][:, k, :], start=(k == 0), stop=(k == 7)),
                                 reads=[Bw, BhT[b]], writes=[Bpo[n]])
                        P.op("act", lambda e, n=n: e.activation(out=ev[n][:], in_=po[n][:], func=AF.Copy), reads=[Bpo[n]], writes=[Bev[n]])
                        dst = (KTD[cc] if cc < 4 else KTS[cc - 4])[:, g * 512:(g + 1) * 512]
                        P.op("sp", lambda e, n=n, dst=dst: e.dma_start(out=dst, in_=ev[n][:]), reads=[Bev[n]], dma=True)
                    for i in range(4):
                        for sec, dstT in ((1024, VD), (2560, VS)):
                            n = cnt[0] % 4
                            cnt[0] += 1
                            for k in range(8):
                                P.op("pe", lambda e, k=k, sec=sec, n=n, i=i: e.matmul(po[n][:], lhsT=hT[b][:, k, i * 128:(i + 1) * 128], rhs=win[:, k, sec:sec + 512], start=(k == 0), stop=(k == 7)),
                                     reads=[Bw, BhT[b]], writes=[Bpo[n]])
                            blk = 4 * g + i
                            P.op("dve", lambda e, n=n, blk=blk: e.tensor_scalar(out=ev[n][:], in0=po[n][:], scalar1=vflag[:, blk:blk + 1], scalar2=None, op0=ALU.mult),
                                 reads=[Bpo[n], B_const], writes=[Bev[n]])
                            dst = dstT[blk * 128:(blk + 1) * 128, :]
                            P.op("sp", lambda e, n=n, dst=dst: e.dma_start(out=dst, in_=ev[n][:]), reads=[Bev[n]], dma=True)
                    for cc in range(8):
                        col = cc * 128 if cc < 4 else 1536 + (cc - 4) * 128
                        n = cnt[0] % 4
                        cnt[0] += 1
                        for s in range(2):
                            for k in range(8):
                                P.op("pe", lambda e, k=k, col=col, n=n, s=s: e.matmul(po[n][:, s * 128:(s + 1) * 128], lhsT=win[:, k, col:col + 128], rhs=hT[b][:, k, s * 256:s * 256 + 128],
                                                                                start=(k == 0 and s == 0), stop=(k == 7), skip_group_check=True),
                                     reads=[Bw, BhT[b]], writes=[Bpo[n]])
                        P.op("act", lambda e, n=n: e.activation(out=ev[n][:, 0:256], in_=po[n][:, 0:256], func=AF.Copy, scale=0.125), reads=[Bpo[n]], writes=[Bev[n]])
                        dst = (QTD[cc] if cc < 4 else QTS[cc - 4])[:, g * 256:(g + 1) * 256]
                        P.op("sp", lambda e, n=n, dst=dst: e.dma_start(out=dst, in_=ev[n][:, 0:256]), reads=[Bev[n]], dma=True)

                load(0)
                for t in range(NG + 1):
                    if t + 1 < NG:
                        load(t + 1)
                    if t < NG:
                        pre(t)
                    if t >= 1:
                        main(t - 1)
                P.next_phase()

        def diff_phase():
            with contextlib.ExitStack() as es2:
                def sb2(name, shape, dt=F32):
                    return es2.enter_context(nc.sbuf_tensor(f"da_{name}", list(shape), dt))

                def ps2(name, shape, dt=F32):
                    return es2.enter_context(nc.psum_tensor(f"dap_{name}", list(shape), dt))
                KTa = [[sb2(f"KT{hb}{c}", [128, S], BF16) for c in range(2)] for hb in range(2)]
                QTa = [[sb2(f"QT{hb}{c}", [128, S // 2], BF16) for c in range(2)] for hb in range(2)]
                Va = [sb2(f"Va{hb}", [128, NBLK, 130], BF16) for hb in range(2)]
                dmb = sb2("dmb", [128, 3, 512], BF16)
                pT = [sb2(f"pT{i}", [128, 512], BF16) for i in range(4)]
                od = [sb2(f"od{i}", [128, 2, 128]) for i in range(3)]
                rs = [sb2(f"rs{i}", [128, 8]) for i in range(2)]
                junk = sb2("junk", [128, 128], BF16)
                pS = [ps2(f"pS{i}", [128, 512]) for i in range(3)]
                pO = [[ps2(f"pO{st}{i}", [128, 512]) for i in range(2)] for st in range(2)]
                Bop = [Buf("op0"), Buf("op1")]
                Bva1 = [Buf("va1_0"), Buf("va1_1")]
                Bdm = Buf("dm")
                BpT = [Buf(f"pT{i}") for i in range(4)]
                Bod = [Buf(f"od{i}") for i in range(3)]
                Brs = [Buf("rs0"), Buf("rs1")]
                Bjunk = Buf("junk")
                BpS = [Buf(f"pS{i}") for i in range(3)]
                BpO = [[Buf(f"pO{st}{i}") for i in range(2)] for st in range(2)]
                P.op("pool", lambda e: e.dma_start(out=dmb[:], in_=dmask_d.rearrange("j p n -> p j n")), writes=[Bdm], dma=True)
                for hb in range(2):
                    P.op("pool", lambda e, hb=hb: e.memset(Va[hb][:, :, 128:130], 1.0), writes=[Bva1[hb]])

                def load_head(hd):
                    hb = hd % 2
                    for c in range(2):
                        P.op("sp", lambda e, c=c: e.dma_start(out=KTa[hb][c][0:64, :], in_=KTD[hd][c * 64:(c + 1) * 64, :]), writes=[Bop[hb]], dma=True)
                        P.op("pool", lambda e, c=c: e.dma_start(out=KTa[hb][c][64:68, :].rearrange("r (a b) -> r a b", b=2048), in_=kx_d.rearrange("r (a b) -> r a b", b=2048)), writes=[Bop[hb]], dma=True)
                        P.op("sp", lambda e, c=c: e.dma_start(out=QTa[hb][c][0:64, :], in_=QTD[hd][c * 64:(c + 1) * 64, :]), writes=[Bop[hb]], dma=True)
                        P.op("pool", lambda e, c=c: e.dma_start(out=QTa[hb][c][64:68, :].rearrange("r (a b) -> r a b", b=2048), in_=qx_d[hd].rearrange("r (a b) -> r a b", b=2048)), writes=[Bop[hb]], dma=True)
                    for kq in range(8):
                        P.op("sp", lambda e, kq=kq: e.dma_start(out=Va[hb][:, kq * 8:(kq + 1) * 8, 0:128], in_=VD[kq * 1024:(kq + 1) * 1024, hd * 128:(hd + 1) * 128].rearrange("(k p) e -> p k e", p=128)), writes=[Bop[hb]], dma=True)

                qn = [0]
                gn = [0]

                def compute_head(hd):
                    hb = hd % 2
                    K0, K1 = KTa[hb]
                    Q0, Q1 = QTa[hb]
                    V = Va[hb]
                    rdeps = [Bop[hb], Bva1[hb]]
                    tiles = [(j, kb) for j in range(16) for kb in range(4 * j, NBLK)]
                    N = len(tiles)
                    base = qn[0]
                    qn[0] += N

                    def A(n):
                        j, kb = tiles[n]
                        q = (base + n) % 3
                        t4 = (base + n) % 4
                        psq = pS[q]
                        rel = kb - 4 * j
                        P.op("pe", lambda e: e.matmul(psq[:, 0:256], lhsT=K0[0:68, kb * 128:(kb + 1) * 128], rhs=Q0[0:68, j * 256:(j + 1) * 256], start=True, stop=False, skip_group_check=True),
                             reads=rdeps, writes=[BpS[q]])
                        P.op("pe", lambda e: e.matmul(psq[:, 256:512], lhsT=K1[0:68, kb * 128:(kb + 1) * 128], rhs=Q1[0:68, j * 256:(j + 1) * 256], start=False, stop=(rel >= 3), skip_group_check=True),
                             reads=rdeps, writes=[BpS[q]])
                        if rel < 3:
                            P.op("pe", lambda e: e.matmul(psq[:, 0:512], lhsT=identb[:], rhs=dmb[:, rel, :], start=False, stop=True, skip_group_check=True),
                                 reads=[Bdm, B_const], writes=[BpS[q]])
                        P.op("act", lambda e: e.activation(out=pT[t4][:], in_=psq[:], func=AF.Exp), reads=[BpS[q]], writes=[BpT[t4]])

                    def B(n):
                        j, kb = tiles[n]
                        t4 = (base + n) % 4
                        rel = kb - 4 * j
                        st = j % 2
                        for m in range(2):
                            pOm = pO[st][m]
                            for s in range(2):
                                if s == 1 and rel < 2:
                                    continue
                                P.op("pe", lambda e, m=m, s=s, pOm=pOm: e.matmul(pOm[:, s * 130:s * 130 + 129], lhsT=pT[t4][:, m * 256 + s * 128:m * 256 + (s + 1) * 128], rhs=V[:, kb, 0:129],
                                                                             start=(rel == 0 and s == 0), stop=(kb == NBLK - 1), skip_group_check=True),
                                     reads=[BpT[t4]] + rdeps, writes=[BpO[st][m]])
                        if kb == NBLK - 1:
                            norm(j, st)

                    def norm(j, st):
                        g3 = gn[0] % 3
                        g2 = gn[0] % 2
                        gn[0] += 1
                        o = od[g3]
                        r = rs[g2]
                        pO0, pO1 = pO[st]
                        for m in range(2):
                            for s in range(2):
                                P.op("dve", lambda e, m=m, s=s: e.reciprocal(out=r[:, 2 * m + s:2 * m + s + 1], in_=pO[st][m][:, s * 130 + 128:s * 130 + 129]), reads=[BpO[st][m]], writes=[Brs[g2]])
                        P.op("dve", lambda e: e.tensor_scalar(out=r[:, 4:6], in0=r[:, 2:4], scalar1=lamneg[:, 0:1], scalar2=None, op0=ALU.mult), reads=[Brs[g2], B_const], writes=[Brs[g2]])
                        for s in range(2):
                            P.op("dve", lambda e, s=s: e.tensor_scalar(out=o[:, s, :], in0=pO0[:, s * 130:s * 130 + 128], scalar1=r[:, s:s + 1], scalar2=None, op0=ALU.mult),
                                 reads=[BpO[st][0], Brs[g2]], writes=[Bod[g3]])
                            P.op("dve", lambda e, s=s: e.scalar_tensor_tensor(out=o[:, s, :], in0=pO1[:, s * 130:s * 130 + 128], scalar=r[:, 4 + s:5 + s], in1=o[:, s, :], op0=ALU.mult, op1=ALU.add),
                                 reads=[BpO[st][1], Brs[g2], Bod[g3]], writes=[Bod[g3]])
                            P.op("act", lambda e, s=s: e.activation(out=junk[:], in_=o[:, s, :], func=AF.Square, accum_out=r[:, 6 + s:7 + s]), reads=[Bod[g3]], writes=[Bjunk, Brs[g2]])
                        P.op("dve", lambda e: e.tensor_scalar(out=r[:, 6:8], in0=r[:, 6:8], scalar1=1.0 / 128, scalar2=EPS, op0=ALU.mult, op1=ALU.add), reads=[Brs[g2]], writes=[Brs[g2]])
                        P.op("pool", lambda e: e.tensor_tensor(out=r[:, 6:8], in0=r[:, 6:8], in1=mhalf[:, 0:2], op=ALU.pow), reads=[Brs[g2], B_const], writes=[Brs[g2]])
                        for s in range(2):
                            P.op("dve", lambda e, s=s: e.scalar_tensor_tensor(out=o[:, s, :], in0=o[:, s, :], scalar=r[:, 6 + s:7 + s], in1=sublnw[:], op0=ALU.mult, op1=ALU.mult),
                                 reads=[Bod[g3], Brs[g2], B_const], writes=[Bod[g3]])
                        dst = CAT[j * 256:(j + 1) * 256, hd * 128:(hd + 1) * 128].rearrange("(s p) e -> p s e", p=128)
                        P.op("sp", lambda e: e.dma_start(out=dst, in_=o[:]), reads=[Bod[g3]], dma=True)

                    for n in range(N + 2):
                        if n < N:
                            A(n)
                        if 0 <= n - 2 < N:
                            B(n - 2)

                load_head(0)
                for hd in range(4):
                    if hd + 1 < 4:
                        load_head(hd + 1)
                    compute_head(hd)
                P.next_phase()

        def sb_phase():
            with contextlib.ExitStack() as es2:
                def sb2(name, shape, dt=F32):
                    return es2.enter_context(nc.sbuf_tensor(f"sa_{name}", list(shape), dt))

                def ps2(name, shape, dt=F32):
                    return es2.enter_context(nc.psum_tensor(f"sap_{name}", list(shape), dt))
                KTs = [sb2(f"KT{hb}", [128, S], BF16) for hb in range(2)]
                Qz = [[sb2(f"Qz{hb}{h2}", [128, S // 2], BF16) for h2 in range(2)] for hb in range(2)]
                Vs = [sb2(f"Vs{hb}", [128, NBLK, 128], BF16) for hb in range(2)]
                osb = sb2("osb", [128, 32, 128])
                smb = sb2("smb", [128, 128], BF16)
                om = [sb2(f"om{i}", [128, 512]) for i in range(3)]
                cp = [sb2(f"cp{i}", [128, 513]) for i in range(4)]
                att = [sb2(f"att{i}", [128, 512], BF16) for i in range(4)]
                attT = [sb2(f"attT{i}", [128, 512], BF16) for i in range(3)]
                pz = [ps2(f"pz{i}", [128, 512]) for i in range(3)]
                pTt = [ps2(f"pTt{i}", [128, 1024], BF16) for i in range(2)]
                pOs = [ps2(f"pOs{i}", [128, 512]) for i in range(2)]
                Bop = [Buf("op0"), Buf("op1")]
                Bz = [Buf("z0"), Buf("z1")]
                Bosb = Buf("osb")
                Bsm = Buf("sm")
                Bom = [Buf(f"om{i}") for i in range(3)]
                Bcp = [Buf(f"cp{i}") for i in range(4)]
                Bc0 = [Buf(f"c0{i}") for i in range(4)]
                Batt = [Buf(f"att{i}") for i in range(4)]
                BattT = [Buf(f"attT{i}") for i in range(3)]
                Bpz = [Buf(f"pz{i}") for i in range(3)]
                BpTt = [Buf(f"pTt{i}") for i in range(2)]
                BpOs = [Buf(f"pOs{i}") for i in range(2)]
                P.op("pool", lambda e: e.dma_start(out=smb[:], in_=smask_d), writes=[Bsm], dma=True)
                for hb in range(2):
                    P.op("pool", lambda e, hb=hb: e.memset(Qz[hb][0][64:128, :], 0.0), writes=[Bz[hb]])
                    P.op("pool", lambda e, hb=hb: e.memset(Qz[hb][1][0:64, :], 0.0), writes=[Bz[hb]])

                def load_pair(ch):
                    hb = ch % 2
                    P.op("sp", lambda e: e.dma_start(out=KTs[hb][:], in_=KTS[ch]), writes=[Bop[hb]], dma=True)
                    P.op("sp", lambda e: e.dma_start(out=Qz[hb][0][0:64, :], in_=QTS[ch][0:64, :]), writes=[Bop[hb]], dma=True)
                    P.op("sp", lambda e: e.dma_start(out=Qz[hb][1][64:128, :], in_=QTS[ch][64:128, :]), writes=[Bop[hb]], dma=True)
                    for kq in range(8):
                        P.op("sp", lambda e, kq=kq: e.dma_start(out=Vs[hb][:, kq * 8:(kq + 1) * 8, :], in_=VS[kq * 1024:(kq + 1) * 1024, ch * 128:(ch + 1) * 128].rearrange("(k p) e -> p k e", p=128)), writes=[Bop[hb]], dma=True)

                state = {"n": 0, "slot": 0}

                def compute_pair(ch):
                    hb = ch % 2
                    KT = KTs[hb]
                    V = Vs[hb]
                    rdeps = [Bop[hb], Bz[hb]]
                    tasks = []
                    for h2 in range(2):
                        for i in range(32):
                            b0 = 2 * i
                            first = True
                            if i % 2 == 1:
                                tasks.append((h2, i, b0, 2, True, b0 + 2 >= NBLK))
                                b0 += 2
                                first = False
                            while b0 < NBLK:
                                tasks.append((h2, i, b0, 4, first, b0 + 4 >= NBLK))
                                first = False
                                b0 += 4
                    N = len(tasks)
                    base = state["n"]
                    slotmap = {}

                    def A(n):
                        h2, i, b0, nb, first, last = tasks[n]
                        gnn = base + n
                        W = nb * 128
                        z = pz[gnn % 3]
                        Q = Qz[hb][h2]
                        P.op("pe", lambda e: e.matmul(z[:, 0:W], lhsT=Q[:, i * 128:(i + 1) * 128], rhs=KT[:, b0 * 128:b0 * 128 + W], start=True, stop=(not first), skip_group_check=True),
                             reads=rdeps, writes=[Bpz[gnn % 3]])
                        if first:
                            P.op("pe", lambda e: e.matmul(z[:, 0:128], lhsT=identb[:], rhs=smb[:], start=False, stop=True, skip_group_check=True),
                                 reads=[Bsm, B_const], writes=[Bpz[gnn % 3]])
                        o = om[gnn % 3]
                        P.op("act", lambda e: e.activation(out=o[:, 0:W], in_=z[:, 0:W], func=AF.Sigmoid, scale=-1.0), reads=[Bpz[gnn % 3]], writes=[Bom[gnn % 3]])
                        c = cp[gnn % 4]
                        if first:
                            P.op("dve", lambda e: e.tensor_tensor_scan(out=c[:, 1:W + 1], data0=o[:, 0:W], data1=z[:, 0:W], initial=1.0, op0=ALU.mult, op1=ALU.bypass),
                                 reads=[Bom[gnn % 3], Bpz[gnn % 3]], writes=[Bcp[gnn % 4]])
                        else:
                            pc = cp[(gnn - 1) % 4]
                            pW = tasks[n - 1][3] * 128
                            P.op("dve", lambda e: e.tensor_tensor_scan(out=c[:, 1:W + 1], data0=o[:, 0:W], data1=z[:, 0:W], initial=pc[:, pW:pW + 1], op0=ALU.mult, op1=ALU.bypass),
                                 reads=[Bom[gnn % 3], Bpz[gnn % 3], Bcp[(gnn - 1) % 4]], writes=[Bcp[gnn % 4]])

                    def A2(n):
                        h2, i, b0, nb, first, last = tasks[n]
                        gnn = base + n
                        W = nb * 128
                        c = cp[gnn % 4]
                        if first:
                            P.op("pool", lambda e: e.memset(c[:, 0:1], 1.0), writes=[Bc0[gnn % 4]])
                        else:
                            pc = cp[(gnn - 1) % 4]
                            pW = tasks[n - 1][3] * 128
                            P.op("act", lambda e: e.activation(out=c[:, 0:1], in_=pc[:, pW:pW + 1], func=AF.Copy), reads=[Bcp[(gnn - 1) % 4]], writes=[Bc0[gnn % 4]])
                        a = att[gnn % 4]
                        P.op("pool", lambda e: e.tensor_tensor(out=a[:, 0:W], in0=c[:, 0:W], in1=c[:, 1:W + 1], op=ALU.subtract), reads=[Bcp[gnn % 4], Bc0[gnn % 4]], writes=[Batt[gnn % 4]])

                    def Bst(n):
                        h2, i, b0, nb, first, last = tasks[n]
                        gnn = base + n
                        W = nb * 128
                        a = att[gnn % 4]
                        t = pTt[gnn % 2]
                        for b in range(nb):
                            P.op("pe", lambda e, b=b: e.transpose(out=t[:, b * 128:(b + 1) * 128], in_=a[:, b * 128:(b + 1) * 128], identity=identb[:]),
                                 reads=[Batt[gnn % 4], B_const], writes=[BpTt[gnn % 2]])
                        aT = attT[gnn % 3]
                        P.op("act", lambda e: e.activation(out=aT[:, 0:W], in_=t[:, 0:W], func=AF.Copy), reads=[BpTt[gnn % 2]], writes=[BattT[gnn % 3]])

                    def C(n):
                        h2, i, b0, nb, first, last = tasks[n]
                        gnn = base + n
                        aT = attT[gnn % 3]
                        if first:
                            slotmap[(h2, i)] = state["slot"] % 2
                            state["slot"] += 1
                        sl = slotmap[(h2, i)]
                        for b in range(nb):
                            P.op("pe", lambda e, b=b: e.matmul(pOs[sl][:, 0:64], lhsT=aT[:, b * 128:(b + 1) * 128], rhs=V[:, b0 + b, h2 * 64:(h2 + 1) * 64], start=(first and b == 0), stop=(last and b == nb - 1)),
                                 reads=[BattT[gnn % 3]] + rdeps, writes=[BpOs[sl]])
                        if last:
                            P.op("dve", lambda e: e.tensor_copy(out=osb[:, i, h2 * 64:(h2 + 1) * 64], in_=pOs[sl][:, 0:64]), reads=[BpOs[sl]], writes=[Bosb])

                    for n in range(N + 5):
                        if n < N:
                            A(n)
                        if 0 <= n - 1 < N:
                            A2(n - 1)
                        if 0 <= n - 4 < N:
                            Bst(n - 4)
                        if 0 <= n - 5 < N:
                            C(n - 5)
                    state["n"] += N
                    for iq in range(4):
                        dst = CAT[iq * 1024:(iq + 1) * 1024, 512 + ch * 128:512 + (ch + 1) * 128].rearrange("(i p) e -> p i e", p=128)
                        P.op("sp", lambda e, iq=iq, dst=dst: e.dma_start(out=dst, in_=osb[:, iq * 8:(iq + 1) * 8, :]), reads=[Bosb], dma=True)

                load_pair(0)
                for ch in range(4):
                    if ch + 1 < 4:
                        load_pair(ch + 1)
                    compute_pair(ch)
                P.next_phase()

        def out_phase():
            with contextlib.ExitStack() as es2:
                def sb2(name, shape, dt=F32):
                    return es2.enter_context(nc.sbuf_tensor(f"op_{name}", list(shape), dt))

                def ps2(name, shape, dt=F32):
                    return es2.enter_context(nc.psum_tensor(f"opp_{name}", list(shape), dt))
                wout = sb2("wout", [128, 8, D], BF16)
                G = sb2("G", [128, D])
                ct = [sb2(f"ct{i}", [128, 4, D]) for i in range(2)]
                x1t = [sb2(f"x1t{i}", [128, 4, D]) for i in range(2)]
                cT = [sb2(f"cT{i}", [128, 8, 512], BF16) for i in range(2)]
                junk = sb2("junk", [128, 512], BF16)
                ss = [sb2(f"ss{i}", [128, 4]) for i in range(2)]
                st2 = [sb2(f"st2{i}", [128, 4]) for i in range(2)]
                tmp = [sb2(f"tmp{i}", [128, 512]) for i in range(2)]
                ptr = [ps2(f"ptr{i}", [128, 512]) for i in range(2)]
                po = [ps2(f"po{i}", [128, 512]) for i in range(4)]
                Bw, BG = Buf("w"), Buf("G")
                Bct = [Buf("ct0"), Buf("ct1")]
                Bx1 = [Buf("x10"), Buf("x11")]
                BcT = [Buf("cT0"), Buf("cT1")]
                Bjunk = Buf("junk")
                Bss = [Buf("ss0"), Buf("ss1")]
                Bst2 = [Buf("st20"), Buf("st21")]
                Btmp = [Buf("tmp0"), Buf("tmp1")]
                Bptr = [Buf("ptr0"), Buf("ptr1")]
                Bpo = [Buf(f"po{i}") for i in range(4)]
                for k in range(8):
                    P.op("pool", lambda e, k=k: e.dma_start(out=wout[:, k, :], in_=w_out[k * 128:(k + 1) * 128, :]), writes=[Bw], dma=True)
                P.op("sp", lambda e: e.dma_start(out=G[:], in_=GREP[1]), writes=[BG], dma=True)
                NG = 8
                X1v = X1.rearrange("(blk two p) d -> p blk two d", two=2, p=128)
                cnt = [0, 0, 0]

                def load(g):
                    b = g % 2
                    P.op("sp", lambda e: e.dma_start(out=ct[b][:], in_=CAT[g * 512:(g + 1) * 512, :].rearrange("(i p) d -> p i d", p=128)), writes=[Bct[b]], dma=True)
                    P.op("sp", lambda e: e.dma_start(out=x1t[b][:], in_=X1v[:, 4 * g:4 * g + 4, 0, :]), writes=[Bx1[b]], dma=True)

                def pre(g):
                    b = g % 2
                    for i in range(4):
                        P.op("act", lambda e, i=i: e.activation(out=junk[:], in_=ct[b][:, i, 512:1024], func=AF.Square, accum_out=ss[b][:, i:i + 1]),
                             reads=[Bct[b]], writes=[Bjunk, Bss[b]])
                    P.op("dve", lambda e: e.tensor_scalar(out=ss[b][:], in0=ss[b][:], scalar1=1.0 / 512, scalar2=EPS, op0=ALU.mult, op1=ALU.add), reads=[Bss[b]], writes=[Bss[b]])
                    P.op("pool", lambda e: e.tensor_tensor(out=ss[b][:], in0=ss[b][:], in1=mhalf[:, 0:4], op=ALU.pow), reads=[Bss[b], B_const], writes=[Bss[b]])
                    for i in range(4):
                        P.op("dve", lambda e, i=i: e.tensor_scalar(out=ct[b][:, i, 512:1024], in0=ct[b][:, i, 512:1024], scalar1=ss[b][:, i:i + 1], scalar2=None, op0=ALU.mult),
                             reads=[Bss[b], Bct[b]], writes=[Bct[b]])
                    for c in range(8):
                        pt = ptr[c % 2]
                        for i in range(4):
                            P.op("pe", lambda e, c=c, i=i, pt=pt: e.transpose(out=pt[:, i * 128:(i + 1) * 128], in_=ct[b][:, i, c * 128:(c + 1) * 128], identity=identf[:]),
                                 reads=[Bct[b], B_const], writes=[Bptr[c % 2]])
                        if c < 4:
                            P.op("act", lambda e, c=c, pt=pt: e.activation(out=cT[b][:, c, :], in_=pt[:], func=AF.Copy), reads=[Bptr[c % 2]], writes=[BcT[b]])
                        else:
                            P.op("dve", lambda e, c=c, pt=pt: e.tensor_scalar(out=cT[b][:, c, :], in0=pt[:], scalar1=sbbeta[:, c - 4:c - 3], scalar2=None, op0=ALU.mult),
                                 reads=[Bptr[c % 2], B_const], writes=[BcT[b]])

                def main(g):
                    b = g % 2
                    for i in range(4):
                        pp = []
                        for hf in range(2):
                            n = cnt[0] % 4
                            cnt[0] += 1
                            pp.append(n)
                            for c in range(8):
                                P.op("pe", lambda e, c=c, n=n, hf=hf, i=i: e.matmul(po[n][:], lhsT=cT[b][:, c, i * 128:(i + 1) * 128], rhs=wout[:, c, hf * 512:(hf + 1) * 512], start=(c == 0), stop=(c == 7)),
                                     reads=[BcT[b], Bw], writes=[Bpo[n]])
                        s2 = cnt[1] % 2
                        cnt[1] += 1
                        r = st2[s2]
                        for hf in range(2):
                            P.op("act", lambda e, r=r, hf=hf, n=pp[hf]: e.activation(out=junk[:], in_=po[n][:], func=AF.Square, accum_out=r[:, hf:hf + 1]), reads=[Bpo[pp[hf]]], writes=[Bjunk, Bst2[s2]])
                        P.op("dve", lambda e, r=r: e.tensor_tensor(out=r[:, 2:3], in0=r[:, 0:1], in1=r[:, 1:2], op=ALU.add), reads=[Bst2[s2]], writes=[Bst2[s2]])
                        P.op("dve", lambda e, r=r: e.tensor_scalar(out=r[:, 2:3], in0=r[:, 2:3], scalar1=1.0 / D, scalar2=EPS, op0=ALU.mult, op1=ALU.add), reads=[Bst2[s2]], writes=[Bst2[s2]])
                        P.op("pool", lambda e, r=r: e.tensor_tensor(out=r[:, 3:4], in0=r[:, 2:3], in1=mhalf[:, 0:1], op=ALU.pow), reads=[Bst2[s2], B_const], writes=[Bst2[s2]])
                        for hf in range(2):
                            tn = cnt[2] % 2
                            cnt[2] += 1
                            t = tmp[tn]
                            P.op("dve", lambda e, r=r, hf=hf, n=pp[hf], t=t: e.scalar_tensor_tensor(out=t[:], in0=po[n][:], scalar=r[:, 3:4], in1=G[:, hf * 512:(hf + 1) * 512], op0=ALU.mult, op1=ALU.mult),
                                 reads=[Bpo[pp[hf]], Bst2[s2], BG], writes=[Btmp[tn]])
                            P.op("dve", lambda e, r=r, hf=hf, t=t, i=i: e.tensor_tensor(out=x1t[b][:, i, hf * 512:(hf + 1) * 512], in0=x1t[b][:, i, hf * 512:(hf + 1) * 512], in1=t[:], op=ALU.add),
                                 reads=[Btmp[tn], Bx1[b]], writes=[Bx1[b]])
                    P.op("sp", lambda e: e.dma_start(out=X2[g * 512:(g + 1) * 512, :].rearrange("(i p) d -> p i d", p=128), in_=x1t[b][:]), reads=[Bx1[b]], dma=True)

                load(0)
                for t in range(NG + 1):
                    if t < NG:
                        pre(t)
                    if t >= 1:
                        main(t - 1)
                    if t + 1 < NG:
                        load(t + 1)
                P.next_phase()

        if stop_after >= 2:
            proj_phase()
        if stop_after >= 3:
            diff_phase()
        if stop_after >= 4:
            sb_phase()
        if stop_after >= 5:
            out_phase()
        if stop_after >= 6:
            ffn_phase("f2", lambda g: X2[g * 256:(g + 1) * 256, :], lambda g: out[g * 256:(g + 1) * 256, :], S // 2 // 256, 2, ffw[1])
        P.emit()
    return nc


def _tables():
    ident = np.eye(128, dtype=np.float32)
    r = np.arange(128)
    smask = np.where(r[None, :] <= r[:, None], NEG, 0.0).astype(np.float32)
    kk = r[:, None]
    qq = r[None, :]
    tri = np.where(kk >= qq, 0.0, NEG).astype(np.float32)
    full = np.zeros((128, 128), np.float32)
    none = np.full((128, 128), NEG, np.float32)
    dm = np.zeros((3, 128, 2, 2, 128), np.float32)
    for m in range(2):
        dm[0, :, m, 0] = tri
        dm[0, :, m, 1] = none
        dm[1, :, m, 0] = full
        dm[1, :, m, 1] = none
        dm[2, :, m, 0] = full
        dm[2, :, m, 1] = tri
    dmask = dm.reshape(3, 128, 512)
    return ident, smask, dmask


def make_in_maps(inputs):
    f = lambda a: np.ascontiguousarray(np.asarray(a, dtype=np.float32))
    x = f(inputs["x"])
    c = f(inputs["c"])
    ident, smask, dmask = _tables()
    pk = lambda v, n: np.ascontiguousarray(v.reshape(n, 128).T)
    gpre_pk = np.concatenate([pk(f(inputs[k])[0], 8) for k in ("ffn1_g_pre", "mix_g_pre", "ffn2_g_pre")], axis=1)
    gpost_row = np.stack([f(inputs[k])[0] for k in ("ffn1_g_post", "mix_g_post", "ffn2_g_post")], axis=0)
    lam_in = np.stack([f(inputs[k])[0] for k in ("lam_q1", "lam_k1", "lam_q2", "lam_k2")], axis=0)
    common = {
        "b_pk": pk(f(inputs["b_ada"])[0], 72),
        "b_row": f(inputs["b_ada"]),
        "w_ada": f(inputs["w_ada"])[0],
        "gpre_pk": np.ascontiguousarray(gpre_pk),
        "gpost_row": np.ascontiguousarray(gpost_row),
        "f1_wg": f(inputs["ffn1_w_gate"])[0], "f1_wu": f(inputs["ffn1_w_up"])[0], "f1_wd": f(inputs["ffn1_w_down"])[0],
        "f2_wg": f(inputs["ffn2_w_gate"])[0], "f2_wu": f(inputs["ffn2_w_up"])[0], "f2_wd": f(inputs["ffn2_w_down"])[0],
        "w_in": f(inputs["w_in"])[0], "w_out": f(inputs["w_out"])[0],
        "lam_in": np.ascontiguousarray(lam_in),
        "subln_row": f(inputs["diff_subln"]),
        "sbbeta_pk": pk(f(inputs["sb_beta"])[0], 4),
        "ident_d": ident, "smask_d": smask, "dmask_d": dmask,
    }
    pos = np.arange(S)
    qpos = (np.arange(S // 2) // 128) * 256 + (np.arange(S // 2) % 128)
    qx = np.zeros((4, 4, S // 2), np.float32)
    for h in range(4):
        sl = 2.0 ** (-2.0 * (h + 1))
        qx[h, 0] = sl * (qpos % 128)
        qx[h, 1] = sl * 128.0 * (qpos // 128)
        qx[h, 2] = -sl
        qx[h, 3] = -sl
    in_maps = []
    for core in range(8):
        b, hh = core // 2, core % 2
        xr = x[b, ::-1]
        if hh == 0:
            xin = np.ascontiguousarray(xr)
        else:
            xin = np.concatenate([xr[128:], np.zeros((128, D), np.float32)], axis=0)
        kx = np.zeros((4, S), np.float32)
        kx[0] = 1.0
        kx[1] = 1.0
        kx[2] = pos % 128
        kx[3] = 128.0 * (pos // 128)
        vflag = np.ones((128, NBLK), np.float32)
        if hh == 1:
            kx[3, S - 128:] = 2.0 ** 20
            vflag[:, NBLK - 1] = 0.0
        m = dict(common)
        m.update({"xin": xin, "c_pk": pk(c[b], 8), "kx_d": kx, "qx_d": qx, "vflag_d": vflag})
        in_maps.append(m)
    return in_maps


def assemble(outs):
    res = np.zeros((4, S, D), np.float32)
    for core in range(8):
        b, hh = core // 2, core % 2
        o = np.asarray(outs[core]).reshape(32, 128, D)
        for i in range(32):
            p0 = (2 * i + hh) * 128
            res[b, S - 1 - p0 - 127:S - p0] = o[i][::-1]
    return res


_NC_CACHE = {}


def kernel(**inputs):
    if "nc" not in _NC_CACHE:
        _NC_CACHE["nc"] = build_program()
    nc = _NC_CACHE["nc"]
    in_maps = make_in_maps(inputs)
    r = run_bass_kernel_spmd(nc, in_maps, core_ids=list(range(8)))
    return assemble([r.results[i]["out"] for i in range(8)])
```

```python
import numpy as np
import concourse.bass as bass
import concourse.mybir as mybir
from concourse.bass_utils import run_bass_kernel_spmd

F32 = mybir.dt.float32
BF16 = mybir.dt.bfloat16
AF = mybir.ActivationFunctionType
ALU = mybir.AluOpType
AX = mybir.AxisListType

ENGS = ["pe", "act", "dve", "pool", "sp"]
NDSEM = 8


class Buf:
    def __init__(self, name):
        self.name = name
        self.last_write = None
        self.reads = []


class Op:
    __slots__ = ("eng", "fn", "deps", "needed", "sem", "count", "dma", "phase", "prewait")

    def __init__(self, eng, fn, deps, dma, phase):
        self.eng = eng
        self.fn = fn
        self.deps = deps
        self.needed = False
        self.sem = None
        self.count = None
        self.dma = dma
        self.phase = phase
        self.prewait = None


class Prog:
    def __init__(self, nc):
        self.nc = nc
        self.streams = {e: [] for e in ENGS}
        self.phase = 0
        self.phase_ops = {}

    def op(self, eng, fn, reads=(), writes=(), dma=False):
        deps = []
        seen = set()

        def add(d):
            if d is not None and id(d) not in seen:
                seen.add(id(d))
                deps.append(d)

        for b in reads:
            add(b.last_write)
        for b in writes:
            add(b.last_write)
            for r in b.reads:
                add(r)
        o = Op(eng, fn, deps, dma, self.phase)
        for d in deps:
            if d.eng == "pe" and eng == "pe" and not d.dma and not dma:
                continue
            d.needed = True
        self.streams[eng].append(o)
        for b in reads:
            b.reads.append(o)
        for b in writes:
            b.last_write = o
            b.reads = []
        return o

    def barrier(self):
        deps = []
        for e in ENGS:
            st = self.streams[e]
            for o in reversed(st):
                if not o.dma and o.fn is not None:
                    deps.append(o)
                    break
            nd = 0
            for o in reversed(st):
                if o.dma:
                    deps.append(o)
                    nd += 1
                    if nd >= NDSEM:
                        break
        for d in deps:
            d.needed = True
        for e in ENGS:
            o = Op(e, None, list(deps), False, self.phase)
            self.streams[e].append(o)

    def next_phase(self):
        self.barrier()
        self.phase += 1

    def emit(self):
        nc = self.nc
        nph = self.phase + 1
        self._cms = []
        sems = {}
        for ph in range(nph):
            for e in ENGS:
                cm = nc.semaphore(f"s_{e}_{ph}")
                sems[(e, ph)] = cm.__enter__()
                self._cms.append(cm)
        dsems = {}
        for e in ENGS:
            for i in range(NDSEM):
                cm = nc.semaphore(f"d_{e}_{i}")
                dsems[(e, i)] = cm.__enter__()
                self._cms.append(cm)
        final = {}
        for e in ENGS:
            cnt = {}
            nd = 0
            for o in self.streams[e]:
                if o.dma:
                    slot = nd % NDSEM
                    o.sem = dsems[(e, slot)]
                    o.count = 16 * (nd // NDSEM + 1)
                    o.prewait = (o.sem, 16 * (nd // NDSEM)) if nd >= NDSEM else None
                    nd += 1
                elif o.needed:
                    c = cnt.get(o.phase, 0) + 1
                    cnt[o.phase] = c
                    o.sem = sems[(e, o.phase)]
                    o.count = c
                    assert c < 30000, (e, o.phase, c)
            for ph, c in cnt.items():
                final[(e, ph)] = c
        self.final = final
        engobj = {"pe": nc.tensor, "act": nc.scalar, "dve": nc.vector, "pool": nc.gpsimd, "sp": nc.sync}
        streams = self.streams

        def run(ename, eng):
            known = {}
            for o in streams[ename]:
                waits = {}
                for d in o.deps:
                    if d.eng == "pe" and ename == "pe" and not d.dma and not o.dma and o.fn is not None:
                        continue
                    assert d.sem is not None, (d.eng, ename)
                    k = id(d.sem)
                    if known.get(k, 0) >= d.count:
                        continue
                    if k not in waits or waits[k][1] < d.count:
                        waits[k] = (d.sem, d.count)
                if o.prewait is not None:
                    k = id(o.prewait[0])
                    if known.get(k, 0) < o.prewait[1]:
                        if k not in waits or waits[k][1] < o.prewait[1]:
                            waits[k] = o.prewait
                for k, (s, c) in waits.items():
                    eng.wait_ge(s, c)
                    known[k] = c
                if o.fn is None:
                    continue
                inst = o.fn(eng)
                if o.dma:
                    inst.then_inc(o.sem, 16)
                elif o.needed:
                    inst.then_inc(o.sem, 1)

        with nc.Block() as block:
            @block.tensor
            def _(e):
                run("pe", e)

            @block.scalar
            def _(e):
                run("act", e)

            @block.vector
            def _(e):
                run("dve", e)

            @block.gpsimd
            def _(e):
                run("pool", e)

            @block.sync
            def _(e):
                run("sp", e)
        for cm in reversed(self._cms):
            cm.__exit__(None, None, None)


D = 1024
S = 8192
NBLK = 64
DFF = 2816
NFF = 22
EPS = 1e-6
NEG = -32768.0
LAMBDA_INIT = 0.2


def rr(i, n):
    return i % n


def build_program(debug=False, stop_after=99):
    nc = bass.Bass("TRN2", target_bir_lowering=False)
    P = Prog(nc)

    def din(name, shape, dt=F32):
        return nc.dram_tensor(name, list(shape), dt, kind="ExternalInput").ap()

    xin = din("xin", [S, D])
    c_pk = din("c_pk", [128, 8])
    b_pk = din("b_pk", [128, 72])
    b_row = din("b_row", [1, 9 * D])
    w_ada = din("w_ada", [D, 9 * D])
    gpre_pk = din("gpre_pk", [128, 24])
    gpost_row = din("gpost_row", [3, D])
    ffw = []
    for f in (1, 2):
        ffw.append((din(f"f{f}_wg", [D, DFF]), din(f"f{f}_wu", [D, DFF]), din(f"f{f}_wd", [DFF, D])))
    w_in = din("w_in", [D, 3 * D])
    w_out = din("w_out", [D, D])
    lam_in = din("lam_in", [4, 64])
    subln_row = din("subln_row", [1, 128])
    sbbeta_pk = din("sbbeta_pk", [128, 4])
    ident_d = din("ident_d", [128, 128])
    smask_d = din("smask_d", [128, 128])
    dmask_d = din("dmask_d", [3, 128, 512])
    kx_d = din("kx_d", [4, S])
    qx_d = din("qx_d", [4, 4, S // 2])
    vflag_d = din("vflag_d", [128, NBLK])

    out = nc.dram_tensor("out", [S // 2, D], F32, kind="ExternalOutput").ap()
    skind = "ExternalOutput" if debug else "Internal"

    def dscr(name, shape, dt, dbg=False):
        return nc.dram_tensor(name, list(shape), dt, kind=(skind if dbg else "Internal")).ap()

    DBG = dscr("DBG", [4, 128, 512], F32, True)
    DBGB = dscr("DBGB", [8, 256], BF16, True)
    X1 = dscr("X1", [S, D], F32, True)
    CAT = dscr("CAT", [S // 2, D], F32, True)
    X2 = dscr("X2", [S // 2, D], F32, True)
    GREP = dscr("GREP", [3, 128, D], F32)
    KTD = dscr("KTD", [4, 128, S], BF16)
    KTS = dscr("KTS", [4, 128, S], BF16)
    QTD = dscr("QTD", [4, 128, S // 2], BF16)
    QTS = dscr("QTS", [4, 128, S // 2], BF16)
    VD = dscr("VD", [S, 512], BF16)
    VS = dscr("VS", [S, 512], BF16)

    import contextlib
    es = contextlib.ExitStack()

    def sb(name, shape, dt=F32):
        return es.enter_context(nc.sbuf_tensor(name, list(shape), dt))

    def ps(name, shape, dt=F32):
        return es.enter_context(nc.psum_tensor(name, list(shape), dt))

    with es:
        identf = sb("identf", [128, 128], F32)
        identb = sb("identb", [128, 128], BF16)
        modT = sb("modT", [128, 72], F32)
        Aco = sb("Aco", [128, 24], F32)
        gpre = sb("gpre", [128, 24], F32)
        mhalf = sb("mhalf", [128, 8], F32)
        lamneg = sb("lamneg", [128, 1], F32)
        sublnw = sb("sublnw", [128, 128], F32)
        sbbeta = sb("sbbeta", [128, 4], F32)
        vflag = sb("vflag", [128, NBLK], F32)
        B_const = Buf("const")
        B_modT = Buf("modT")

        P.op("sp", lambda e: e.dma_start(out=identf[:], in_=ident_d), writes=[B_const], dma=True)
        P.op("pool", lambda e: e.dma_start(out=identb[:], in_=ident_d), writes=[B_const], dma=True)
        P.op("sp", lambda e: e.dma_start(out=gpre[:], in_=gpre_pk), writes=[B_const], dma=True)
        P.op("sp", lambda e: e.dma_start(out=sbbeta[:], in_=sbbeta_pk), writes=[B_const], dma=True)
        P.op("sp", lambda e: e.dma_start(out=vflag[:], in_=vflag_d), writes=[B_const], dma=True)
        P.op("sp", lambda e: e.dma_start(out=sublnw[:], in_=subln_row.partition_broadcast(128)), writes=[B_const], dma=True)
        P.op("pool", lambda e: e.memset(mhalf[:], -0.5), writes=[B_const])

        with contextlib.ExitStack() as es2:
            def sb2(name, shape, dt=F32):
                return es2.enter_context(nc.sbuf_tensor(name, list(shape), dt))

            def ps2(name, shape, dt=F32):
                return es2.enter_context(nc.psum_tensor(name, list(shape), dt))
            ct = sb2("ct", [128, 8])
            sfl = sb2("sfl", [128, 8])
            sbf = sb2("sbf", [128, 8], BF16)
            onesb = sb2("onesb", [128, 128], BF16)
            srep = sb2("srep", [128, 8, 128], BF16)
            bpk = sb2("bpk", [128, 72])
            wa = [sb2(f"wa{i}", [128, 8, D], BF16) for i in range(2)]
            brow = sb2("brow", [128, D])
            gprow = sb2("gprow", [128, D])
            grep = sb2("grep", [128, D])
            lamt = sb2("lamt", [128, 4, 64])
            lamp = sb2("lamp", [128, 2, 64])
            lams = sb2("lams", [128, 2])
            pmod = ps2("pmod", [128, 512])
            pgr = [ps2(f"pgr{i}", [128, 512]) for i in range(2)]
            Bc, Bs, Bsrep, Bbpk, Bpm = Buf("c"), Buf("s"), Buf("srep"), Buf("bpk"), Buf("pm")
            Bwa = [Buf("wa0"), Buf("wa1")]
            Bbrow, Bgprow, Bgrep, Bpgr = Buf("brow"), Buf("gprow"), Buf("grep"), [Buf("pgr0"), Buf("pgr1")]
            Blam = Buf("lam")
            P.op("sp", lambda e: e.dma_start(out=ct[:], in_=c_pk), writes=[Bc], dma=True)
            P.op("sp", lambda e: e.dma_start(out=bpk[:], in_=b_pk), writes=[Bbpk], dma=True)
            P.op("act", lambda e: e.activation(out=sfl[:], in_=ct[:], func=AF.Silu), reads=[Bc], writes=[Bs])
            P.op("dve", lambda e: e.tensor_copy(out=sbf[:], in_=sfl[:]), reads=[Bs], writes=[Bs])
            P.op("dve", lambda e: e.memset(onesb[:], 1.0), writes=[Bsrep])
            for k in range(8):
                P.op("dve", lambda e, k=k: e.tensor_scalar(out=srep[:, k, :], in0=onesb[:], scalar1=sfl[:, k:k + 1], scalar2=None, op0=ALU.mult),
                     reads=[Bs, Bsrep], writes=[Bsrep])
            P.op("sp", lambda e: e.dma_start(out=lamt[:].rearrange("p a b -> p (a b)"), in_=lam_in.rearrange("a b -> (a b)").rearrange("(o n) -> o n", o=1).partition_broadcast(128)), writes=[Blam], dma=True)
            P.op("dve", lambda e: e.tensor_tensor(out=lamp[:, 0, :], in0=lamt[:, 0, :], in1=lamt[:, 1, :], op=ALU.mult), reads=[Blam], writes=[Blam])
            P.op("dve", lambda e: e.tensor_tensor(out=lamp[:, 1, :], in0=lamt[:, 2, :], in1=lamt[:, 3, :], op=ALU.mult), reads=[Blam], writes=[Blam])
            P.op("dve", lambda e: e.reduce_sum(out=lams[:], in_=lamp[:], axis=AX.X), reads=[Blam], writes=[Blam])
            P.op("act", lambda e: e.activation(out=lams[:], in_=lams[:], func=AF.Exp), reads=[Blam], writes=[Blam])
            P.op("dve", lambda e: e.tensor_tensor(out=lamneg[:], in0=lams[:, 1:2], in1=lams[:, 0:1], op=ALU.subtract), reads=[Blam], writes=[Blam])
            P.op("dve", lambda e: e.tensor_scalar(out=lamneg[:], in0=lamneg[:], scalar1=-LAMBDA_INIT, scalar2=None, op0=ALU.add), reads=[Blam], writes=[B_const, Blam])
            P.op("dve", lambda e: e.tensor_scalar(out=sublnw[:], in0=sublnw[:], scalar1=1.0 - LAMBDA_INIT, scalar2=None, op0=ALU.mult), reads=[B_const], writes=[B_const])

            for v in range(9):
                w = wa[v % 2]
                Bw = Bwa[v % 2]
                for k in range(8):
                    P.op("pool", lambda e, k=k, v=v, w=w: e.dma_start(out=w[:, k, :], in_=w_ada[k * 128:(k + 1) * 128, v * D:(v + 1) * D]),
                         writes=[Bw], dma=True)
                for jc in range(8):
                    j = v * 8 + jc
                    for k in range(8):
                        P.op("pe", lambda e, j=j, jc=jc, k=k, w=w: e.matmul(pmod[:, j:j + 1], lhsT=w[:, k, jc * 128:(jc + 1) * 128], rhs=sbf[:, k:k + 1],
                                                                         start=(k == 0), stop=(k == 7)),
                             reads=[Bw, Bs], writes=[Bpm])
                if v % 3 == 2:
                    gi = v // 3
                    resw = 1.0 if gi == 1 else 0.5
                    P.op("sp", lambda e, v=v: e.dma_start(out=brow[:], in_=b_row[0:1, v * D:(v + 1) * D].partition_broadcast(128)), writes=[Bbrow], dma=True)
                    P.op("sp", lambda e, gi=gi: e.dma_start(out=gprow[:], in_=gpost_row[gi:gi + 1, :].partition_broadcast(128)), writes=[Bgprow], dma=True)
                    for hf in range(2):
                        pg = pgr[hf]
                        for k in range(8):
                            P.op("pe", lambda e, k=k, hf=hf, pg=pg, w=w: e.matmul(pg[:], lhsT=srep[:, k, :], rhs=w[:, k, hf * 512:(hf + 1) * 512],
                                                                              start=(k == 0), stop=(k == 7)),
                                 reads=[Bw, Bsrep], writes=[Bpgr[hf]])
                        P.op("dve", lambda e, hf=hf, pg=pg: e.tensor_tensor(out=grep[:, hf * 512:(hf + 1) * 512], in0=pg[:], in1=brow[:, hf * 512:(hf + 1) * 512], op=ALU.add),
                             reads=[Bpgr[hf], Bbrow], writes=[Bgrep])
                    P.op("dve", lambda e, resw=resw: e.scalar_tensor_tensor(out=grep[:], in0=grep[:], scalar=resw, in1=gprow[:], op0=ALU.mult, op1=ALU.mult),
                         reads=[Bgrep, Bgprow], writes=[Bgrep])
                    P.op("sp", lambda e, gi=gi: e.dma_start(out=GREP[gi], in_=grep[:]), reads=[Bgrep], dma=True)
            P.op("dve", lambda e: e.tensor_tensor(out=modT[:], in0=pmod[:, 0:72], in1=bpk[:], op=ALU.add), reads=[Bpm, Bbpk], writes=[B_modT])
            for i in range(3):
                P.op("dve", lambda e, i=i: e.scalar_tensor_tensor(out=Aco[:, i * 8:(i + 1) * 8], in0=modT[:, (3 * i + 1) * 8:(3 * i + 2) * 8], scalar=1.0,
                                                                 in1=gpre[:, i * 8:(i + 1) * 8], op0=ALU.add, op1=ALU.mult),
                     reads=[B_modT, B_const], writes=[B_modT])
            P.next_phase()

        def Bsh(i):
            return modT[:, (3 * i) * 8:(3 * i) * 8 + 8]

        def ffn_phase(tag, src_rows, dst_rows, ngroups, li, wts):
            with contextlib.ExitStack() as es2:
                def sb2(name, shape, dt=F32):
                    return es2.enter_context(nc.sbuf_tensor(f"{tag}s_{name}", list(shape), dt))

                def ps2(name, shape, dt=F32):
                    return es2.enter_context(nc.psum_tensor(f"{tag}p_{name}", list(shape), dt))
                wg = sb2("wg", [128, 8, DFF], BF16)
                wu = sb2("wu", [128, 8, DFF], BF16)
                wd = sb2("wd", [128, NFF, D], BF16)
                G = sb2("G", [128, D])
                xt = [sb2(f"xt{i}", [128, 2, D]) for i in range(3)]
                hT = [sb2(f"hT{i}", [128, 8, 256], BF16) for i in range(2)]
                junk = sb2("junk", [128, D], BF16)
                ss = [sb2(f"ss{i}", [128, 2]) for i in range(2)]
                rstd = [sb2(f"rstd{i}", [128, 2]) for i in range(2)]
                sd = [sb2(f"sd{i}", [128, 2]) for i in range(2)]
                ss2 = [sb2(f"ss2{i}", [128, 4]) for i in range(2)]
                rstd2 = [sb2(f"rstd2{i}", [128, 2]) for i in range(2)]
                sg = [sb2(f"sg{i}", [128, 256]) for i in range(2)]
                mj = [sb2(f"mj{i}", [128, 256], BF16) for i in range(3)]
                tmp = [sb2(f"tmp{i}", [128, 512]) for i in range(4)]
                ptr = [ps2(f"ptr{i}", [128, 2, 256]) for i in range(2)]
                pgu = [ps2(f"pgu{i}", [128, 512]) for i in range(2)]
                py = [[ps2(f"py{i}{h}", [128, 512]) for h in range(2)] for i in range(2)]
                Bw = Buf("w")
                BG = Buf("G")
                Bxt = [Buf("xt0"), Buf("xt1"), Buf("xt2")]
                BhT = [Buf("hT0"), Buf("hT1")]
                Bjunk = Buf("junk")
                Bst = [Buf("st0"), Buf("st1")]
                Bst2 = [Buf("st20"), Buf("st21")]
                Bsg = [Buf("sg0"), Buf("sg1")]
                Bmj = [Buf(f"mj{i}") for i in range(3)]
                Btmp = [Buf(f"tmp{i}") for i in range(4)]
                Bptr = [Buf("ptr0"), Buf("ptr1")]
                Bpgu = [Buf("pgu0"), Buf("pgu1")]
                Bpy = [[Buf(f"py{i}{h}") for h in range(2)] for i in range(2)]
                w_g, w_u, w_d = wts
                for k in range(8):
                    for hf in range(2):
                        P.op("pool", lambda e, k=k, hf=hf: e.dma_start(out=wg[:, k, hf * 1408:(hf + 1) * 1408], in_=w_g[k * 128:(k + 1) * 128, hf * 1408:(hf + 1) * 1408]), writes=[Bw], dma=True)
                        P.op("pool", lambda e, k=k, hf=hf: e.dma_start(out=wu[:, k, hf * 1408:(hf + 1) * 1408], in_=w_u[k * 128:(k + 1) * 128, hf * 1408:(hf + 1) * 1408]), writes=[Bw], dma=True)
                for j in range(NFF):
                    P.op("pool", lambda e, j=j: e.dma_start(out=wd[:, j, :], in_=w_d[j * 128:(j + 1) * 128, :]), writes=[Bw], dma=True)
                P.op("sp", lambda e: e.dma_start(out=G[:], in_=GREP[li]), writes=[BG], dma=True)
                A = Aco[:, li * 8:(li + 1) * 8]
                Bv = Bsh(li)

                def stage_load(g):
                    x3 = g % 3
                    P.op("sp", lambda e: e.dma_start(out=xt[x3][:], in_=src_rows(g).rearrange("(i p) d -> p i d", p=128)), writes=[Bxt[x3]], dma=True)

                def stage_pre(g):
                    b = g % 2
                    x3 = g % 3
                    for i in range(2):
                        P.op("act", lambda e, i=i: e.activation(out=junk[:], in_=xt[x3][:, i, :], func=AF.Square, accum_out=ss[b][:, i:i + 1]),
                             reads=[Bxt[x3]], writes=[Bjunk, Bst[b]])
                    P.op("dve", lambda e: e.tensor_scalar(out=sd[b][:], in0=ss[b][:], scalar1=1.0 / D, scalar2=EPS, op0=ALU.mult, op1=ALU.add),
                         reads=[Bst[b]], writes=[Bst[b]])
                    P.op("pool", lambda e: e.tensor_tensor(out=rstd[b][:], in0=sd[b][:], in1=mhalf[:, 0:2], op=ALU.pow), reads=[Bst[b], B_const], writes=[Bst[b]])
                    P.op("dve", lambda e: e.reciprocal(out=sd[b][:], in_=rstd[b][:]), reads=[Bst[b]], writes=[Bst[b]])
                    for i in range(2):
                        P.op("dve", lambda e, i=i: e.tensor_scalar(out=xt[x3][:, i, :], in0=xt[x3][:, i, :], scalar1=rstd[b][:, i:i + 1], scalar2=None, op0=ALU.mult),
                             reads=[Bst[b], Bxt[x3]], writes=[Bxt[x3]])
                def stage_pre_b(g):
                    b = g % 2
                    x3 = g % 3
                    for r in range(4):
                        pt = ptr[r % 2]
                        for cc in range(2):
                            c = 2 * r + cc
                            for i in range(2):
                                P.op("pe", lambda e, c=c, cc=cc, i=i, pt=pt: e.transpose(out=pt[:, cc, i * 128:(i + 1) * 128], in_=xt[x3][:, i, c * 128:(c + 1) * 128], identity=identf[:]),
                                     reads=[Bxt[x3], B_const], writes=[Bptr[r % 2]])
                        for cc in range(2):
                            c = 2 * r + cc
                            P.op("dve", lambda e, c=c, cc=cc, pt=pt: e.tensor_scalar(out=hT[b][:, c, :], in0=pt[:, cc, :], scalar1=A[:, c:c + 1], scalar2=Bv[:, c:c + 1], op0=ALU.mult, op1=ALU.add),
                                 reads=[Bptr[r % 2], B_modT], writes=[BhT[b]])

                def gu(g, j):
                    b = g % 2
                    pg = pgu[j % 2]
                    for k in range(8):
                        P.op("pe", lambda e, k=k: e.matmul(pg[:, 0:256], lhsT=wg[:, k, j * 128:(j + 1) * 128], rhs=hT[b][:, k, :], start=(k == 0), stop=False, skip_group_check=True),
                             reads=[Bw, BhT[b]], writes=[Bpgu[j % 2]])
                    for k in range(8):
                        P.op("pe", lambda e, k=k: e.matmul(pg[:, 256:512], lhsT=wu[:, k, j * 128:(j + 1) * 128], rhs=hT[b][:, k, :], start=False, stop=(k == 7), skip_group_check=True),
                             reads=[Bw, BhT[b]], writes=[Bpgu[j % 2]])
                    P.op("act", lambda e: e.activation(out=sg[j % 2][:], in_=pg[:, 0:256], func=AF.Silu), reads=[Bpgu[j % 2]], writes=[Bsg[j % 2]])
                    P.op("dve", lambda e: e.tensor_tensor(out=mj[j % 3][:], in0=sg[j % 2][:], in1=pg[:, 256:512], op=ALU.mult),
                         reads=[Bsg[j % 2], Bpgu[j % 2]], writes=[Bmj[j % 3]])

                def down(g, j):
                    for i in range(2):
                        for hf in range(2):
                            P.op("pe", lambda e, i=i, hf=hf: e.matmul(py[i][hf][:], lhsT=mj[j % 3][:, i * 128:(i + 1) * 128], rhs=wd[:, j, hf * 512:(hf + 1) * 512],
                                                                   start=(j == 0), stop=(j == NFF - 1)),
                                 reads=[Bmj[j % 3], Bw], writes=[Bpy[i][hf]])

                def stage_main(g, hook=None):
                    b = g % 2
                    x3 = g % 3
                    gu(g, 0)
                    for j in range(NFF):
                        if j + 1 < NFF:
                            gu(g, j + 1)
                        down(g, j)
                        if hook is not None and j == 3:
                            hook[0]()
                        if hook is not None and j == 12:
                            hook[1]()
                    for i in range(2):
                        for hf in range(2):
                            P.op("act", lambda e, i=i, hf=hf: e.activation(out=junk[:, 0:512], in_=py[i][hf][:], func=AF.Square, accum_out=ss2[b][:, 2 * i + hf:2 * i + hf + 1]),
                                 reads=[Bpy[i][hf]], writes=[Bjunk, Bst2[b]])
                    P.op("dve", lambda e: e.tensor_tensor(out=rstd2[b][:], in0=ss2[b][:, 0:4:2], in1=ss2[b][:, 1:4:2], op=ALU.add), reads=[Bst2[b]], writes=[Bst2[b]])
                    P.op("dve", lambda e: e.tensor_scalar(out=rstd2[b][:], in0=rstd2[b][:], scalar1=1.0 / D, scalar2=EPS, op0=ALU.mult, op1=ALU.add), reads=[Bst2[b]], writes=[Bst2[b]])
                    P.op("pool", lambda e: e.tensor_tensor(out=rstd2[b][:], in0=rstd2[b][:], in1=mhalf[:, 0:2], op=ALU.pow), reads=[Bst2[b], B_const], writes=[Bst2[b]])
                    for i in range(2):
                        for hf in range(2):
                            t = tmp[2 * i + hf]
                            Bt = Btmp[2 * i + hf]
                            P.op("dve", lambda e, i=i, hf=hf, t=t: e.scalar_tensor_tensor(out=t[:], in0=py[i][hf][:], scalar=rstd2[b][:, i:i + 1], in1=G[:, hf * 512:(hf + 1) * 512], op0=ALU.mult, op1=ALU.mult),
                                 reads=[Bpy[i][hf], Bst2[b], BG], writes=[Bt])
                    for i in range(2):
                        for hf in range(2):
                            t = tmp[2 * i + hf]
                            Bt = Btmp[2 * i + hf]
                            P.op("dve", lambda e, i=i, hf=hf, t=t: e.scalar_tensor_tensor(out=xt[x3][:, i, hf * 512:(hf + 1) * 512], in0=xt[x3][:, i, hf * 512:(hf + 1) * 512], scalar=sd[b][:, i:i + 1], in1=t[:], op0=ALU.mult, op1=ALU.add),
                                 reads=[Bt, Bst[b], Bxt[x3]], writes=[Bxt[x3]])
                    P.op("sp", lambda e: e.dma_start(out=dst_rows(g).rearrange("(i p) d -> p i d", p=128), in_=xt[x3][:]), reads=[Bxt[x3]], dma=True)

                for t in range(ngroups + 2):
                    if t < ngroups:
                        stage_load(t)
                    has_pre = 0 <= t - 1 < ngroups
                    if 0 <= t - 2 < ngroups:
                        stage_main(t - 2, hook=((lambda t=t: stage_pre(t - 1)), (lambda t=t: stage_pre_b(t - 1))) if has_pre else None)
                    elif has_pre:
                        stage_pre(t - 1)
                        stage_pre_b(t - 1)
                P.next_phase()

        if stop_after >= 1:
            ng1 = S // 256
            ffn_phase("f1", lambda g: xin[g * 256:(g + 1) * 256, :], lambda g: X1[g * 256:(g + 1) * 256, :], ng1, 0, ffw[0])
        def proj_phase():
            with contextlib.ExitStack() as es2:
                def sb2(name, shape, dt=F32):
                    return es2.enter_context(nc.sbuf_tensor(f"pj_{name}", list(shape), dt))

                def ps2(name, shape, dt=F32):
                    return es2.enter_context(nc.psum_tensor(f"pjp_{name}", list(shape), dt))
                win = sb2("win", [128, 8, 3 * D], BF16)
                xt = [sb2(f"xt{i}", [128, 4, D]) for i in range(2)]
                hT = [sb2(f"hT{i}", [128, 8, 512], BF16) for i in range(2)]
                junk = sb2("junk", [128, D], BF16)
                ss = [sb2(f"ss{i}", [128, 4]) for i in range(2)]
                rstd = [sb2(f"rstd{i}", [128, 4]) for i in range(2)]
                ev = [sb2(f"ev{i}", [128, 512], BF16) for i in range(4)]
                ptr = [ps2(f"ptr{i}", [128, 512]) for i in range(2)]
                po = [ps2(f"po{i}", [128, 512]) for i in range(4)]
                Bw = Buf("win")
                Bxt = [Buf("xt0"), Buf("xt1")]
                BhT = [Buf("hT0"), Buf("hT1")]
                Bjunk = Buf("junk")
                Bst = [Buf("st0"), Buf("st1")]
                Bev = [Buf(f"ev{i}") for i in range(4)]
                Bptr = [Buf("ptr0"), Buf("ptr1")]
                Bpo = [Buf(f"po{i}") for i in range(4)]
                for k in range(8):
                    for t3 in range(3):
                        P.op("pool", lambda e, k=k, t3=t3: e.dma_start(out=win[:, k, t3 * D:(t3 + 1) * D], in_=w_in[k * 128:(k + 1) * 128, t3 * D:(t3 + 1) * D]), writes=[Bw], dma=True)
                A = Aco[:, 8:16]
                Bv = Bsh(1)
                NG = S // 512
                cnt = [0]

                def load(g):
                    b = g % 2
                    P.op("sp", lambda e: e.dma_start(out=xt[b][:], in_=X1[g * 512:(g + 1) * 512, :].rearrange("(i p) d -> p i d", p=128)), writes=[Bxt[b]], dma=True)

                def pre(g):
                    b = g % 2
                    for i in range(4):
                        P.op("act", lambda e, i=i: e.activation(out=junk[:], in_=xt[b][:, i, :], func=AF.Square, accum_out=ss[b][:, i:i + 1]),
                             reads=[Bxt[b]], writes=[Bjunk, Bst[b]])
                    P.op("dve", lambda e: e.tensor_scalar(out=ss[b][:], in0=ss[b][:], scalar1=1.0 / D, scalar2=EPS, op0=ALU.mult, op1=ALU.add), reads=[Bst[b]], writes=[Bst[b]])
                    P.op("pool", lambda e: e.tensor_tensor(out=rstd[b][:], in0=ss[b][:], in1=mhalf[:, 0:4], op=ALU.pow), reads=[Bst[b], B_const], writes=[Bst[b]])
                    for i in range(4):
                        P.op("dve", lambda e, i=i: e.tensor_scalar(out=xt[b][:, i, :], in0=xt[b][:, i, :], scalar1=rstd[b][:, i:i + 1], scalar2=None, op0=ALU.mult),
                             reads=[Bst[b], Bxt[b]], writes=[Bxt[b]])
                    for c in range(8):
                        pt = ptr[c % 2]
                        for i in range(4):
                            P.op("pe", lambda e, c=c, i=i, pt=pt: e.transpose(out=pt[:, i * 128:(i + 1) * 128], in_=xt[b][:, i, c * 128:(c + 1) * 128], identity=identf[:]),
                                 reads=[Bxt[b], B_const], writes=[Bptr[c % 2]])
                        P.op("dve", lambda e, c=c, pt=pt: e.tensor_scalar(out=hT[b][:, c, :], in0=pt[:], scalar1=A[:, c:c + 1], scalar2=Bv[:, c:c + 1], op0=ALU.mult, op1=ALU.add),
                             reads=[Bptr[c % 2], B_modT], writes=[BhT[b]])

                def main(g, hook=None):
                    b = g % 2
                    for cc in range(8):
                        col = 512 + cc * 128 if cc < 4 else 2048 + (cc - 4) * 128
                        n = cnt[0] % 4
                        cnt[0] += 1
                        for k in range(8):
                            P.op("pe", lambda e, k=k, col=col, n=n: e.matmul(po[n][:], lhsT=win[:, k, col:col + 128], rhs=hT[b][:, k, :], start=(k == 0), stop=(k == 7)),
                                 reads=[Bw, BhT[b]], writes=[Bpo[n]])
                        P.op("act", lambda e, n=n: e.activation(out=ev[n][:], in_=po[n][:], func=AF.Copy), reads=[Bpo[n]], writes=[Bev[n]])
                        dst = (KTD[cc] if cc < 4 else KTS[cc - 4])[:, g * 512:(g + 1) * 512]
                        P.op("sp", lambda e, n=n, dst=dst: e.dma_start(out=dst, in_=ev[n][:]), reads=[Bev[n]], dma=True)
                    if hook is not None:
                        hook()
                    for i in range(4):
                        for sec, dstT in ((1024, VD), (2560, VS)):
                            n = cnt[0] % 4
                            cnt[0] += 1
                            for k in range(8):
                                P.op("pe", lambda e, k=k, sec=sec, n=n, i=i: e.matmul(po[n][:], lhsT=hT[b][:, k, i * 128:(i + 1) * 128], rhs=win[:, k, sec:sec + 512], start=(k == 0), stop=(k == 7)),
                                     reads=[Bw, BhT[b]], writes=[Bpo[n]])
                            blk = 4 * g + i
                            P.op("dve", lambda e, n=n, blk=blk: e.tensor_scalar(out=ev[n][:], in0=po[n][:], scalar1=vflag[:, blk:blk + 1], scalar2=None, op0=ALU.mult),
                                 reads=[Bpo[n], B_const], writes=[Bev[n]])
                            dst = dstT[blk * 128:(blk + 1) * 128, :]
                            P.op("sp", lambda e, n=n, dst=dst: e.dma_start(out=dst, in_=ev[n][:]), reads=[Bev[n]], dma=True)
                    for cc in range(8):
                        col = cc * 128 if cc < 4 else 1536 + (cc - 4) * 128
                        n = cnt[0] % 4
                        cnt[0] += 1
                        for s in range(2):
                            for k in range(8):
                                P.op("pe", lambda e, k=k, col=col, n=n, s=s: e.matmul(po[n][:, s * 128:(s + 1) * 128], lhsT=win[:, k, col:col + 128], rhs=hT[b][:, k, s * 256:s * 256 + 128],
                                                                                start=(k == 0 and s == 0), stop=(k == 7), skip_group_check=True),
                                     reads=[Bw, BhT[b]], writes=[Bpo[n]])
                        P.op("act", lambda e, n=n: e.activation(out=ev[n][:, 0:256], in_=po[n][:, 0:256], func=AF.Copy, scale=0.125), reads=[Bpo[n]], writes=[Bev[n]])
                        dst = (QTD[cc] if cc < 4 else QTS[cc - 4])[:, g * 256:(g + 1) * 256]
                        P.op("sp", lambda e, n=n, dst=dst: e.dma_start(out=dst, in_=ev[n][:, 0:256]), reads=[Bev[n]], dma=True)

                load(0)
                for t in range(NG + 1):
                    if t + 1 < NG:
                        load(t + 1)
                    if t >= 1:
                        main(t - 1, hook=(lambda t=t: pre(t)) if t < NG else None)
                    elif t < NG:
                        pre(t)
                P.next_phase()

        def diff_phase():
            with contextlib.ExitStack() as es2:
                def sb2(name, shape, dt=F32):
                    return es2.enter_context(nc.sbuf_tensor(f"da_{name}", list(shape), dt))

                def ps2(name, shape, dt=F32):
                    return es2.enter_context(nc.psum_tensor(f"dap_{name}", list(shape), dt))
                KTa = [[sb2(f"KT{hb}{c}", [128, S], BF16) for c in range(2)] for hb in range(2)]
                QTa = [[sb2(f"QT{hb}{c}", [128, S // 2], BF16) for c in range(2)] for hb in range(2)]
                Va = [sb2(f"Va{hb}", [128, NBLK, 130], BF16) for hb in range(2)]
                dmb = sb2("dmb", [128, 3, 512], BF16)
                pT = [sb2(f"pT{i}", [128, 512], BF16) for i in range(4)]
                od = [sb2(f"od{i}", [128, 2, 128]) for i in range(3)]
                rs = [sb2(f"rs{i}", [128, 8]) for i in range(2)]
                junk = sb2("junk", [128, 128], BF16)
                pS = [ps2(f"pS{i}", [128, 512]) for i in range(3)]
                pO = [[ps2(f"pO{st}{i}", [128, 512]) for i in range(2)] for st in range(2)]
                Bop = [Buf("op0"), Buf("op1")]
                Bva1 = [Buf("va1_0"), Buf("va1_1")]
                Bdm = Buf("dm")
                BpT = [Buf(f"pT{i}") for i in range(4)]
                Bod = [Buf(f"od{i}") for i in range(3)]
                Brs = [Buf("rs0"), Buf("rs1")]
                Bjunk = Buf("junk")
                BpS = [Buf(f"pS{i}") for i in range(3)]
                BpO = [[Buf(f"pO{st}{i}") for i in range(2)] for st in range(2)]
                P.op("pool", lambda e: e.dma_start(out=dmb[:], in_=dmask_d.rearrange("j p n -> p j n")), writes=[Bdm], dma=True)
                for hb in range(2):
                    P.op("pool", lambda e, hb=hb: e.memset(Va[hb][:, :, 128:130], 1.0), writes=[Bva1[hb]])

                def load_head(hd):
                    hb = hd % 2
                    for c in range(2):
                        P.op("sp", lambda e, c=c: e.dma_start(out=KTa[hb][c][0:64, :], in_=KTD[hd][c * 64:(c + 1) * 64, :]), writes=[Bop[hb]], dma=True)
                        P.op("pool", lambda e, c=c: e.dma_start(out=KTa[hb][c][64:68, :].rearrange("r (a b) -> r a b", b=2048), in_=kx_d.rearrange("r (a b) -> r a b", b=2048)), writes=[Bop[hb]], dma=True)
                        P.op("sp", lambda e, c=c: e.dma_start(out=QTa[hb][c][0:64, :], in_=QTD[hd][c * 64:(c + 1) * 64, :]), writes=[Bop[hb]], dma=True)
                        P.op("pool", lambda e, c=c: e.dma_start(out=QTa[hb][c][64:68, :].rearrange("r (a b) -> r a b", b=2048), in_=qx_d[hd].rearrange("r (a b) -> r a b", b=2048)), writes=[Bop[hb]], dma=True)
                    for kq in range(8):
                        P.op("sp", lambda e, kq=kq: e.dma_start(out=Va[hb][:, kq * 8:(kq + 1) * 8, 0:128], in_=VD[kq * 1024:(kq + 1) * 1024, hd * 128:(hd + 1) * 128].rearrange("(k p) e -> p k e", p=128)), writes=[Bop[hb]], dma=True)

                qn = [0]
                gn = [0]

                def compute_head(hd):
                    hb = hd % 2
                    K0, K1 = KTa[hb]
                    Q0, Q1 = QTa[hb]
                    V = Va[hb]
                    rdeps = [Bop[hb], Bva1[hb]]
                    tiles = [(j, kb) for j in range(16) for kb in range(4 * j, NBLK)]
                    N = len(tiles)
                    base = qn[0]
                    qn[0] += N

                    def A(n):
                        j, kb = tiles[n]
                        q = (base + n) % 3
                        t4 = (base + n) % 4
                        psq = pS[q]
                        rel = kb - 4 * j
                        P.op("pe", lambda e: e.matmul(psq[:, 0:256], lhsT=K0[0:68, kb * 128:(kb + 1) * 128], rhs=Q0[0:68, j * 256:(j + 1) * 256], start=True, stop=False, skip_group_check=True),
                             reads=rdeps, writes=[BpS[q]])
                        P.op("pe", lambda e: e.matmul(psq[:, 256:512], lhsT=K1[0:68, kb * 128:(kb + 1) * 128], rhs=Q1[0:68, j * 256:(j + 1) * 256], start=False, stop=(rel >= 3), skip_group_check=True),
                             reads=rdeps, writes=[BpS[q]])
                        if rel < 3:
                            P.op("pe", lambda e: e.matmul(psq[:, 0:512], lhsT=identb[:], rhs=dmb[:, rel, :], start=False, stop=True, skip_group_check=True),
                                 reads=[Bdm, B_const], writes=[BpS[q]])
                        P.op("act", lambda e: e.activation(out=pT[t4][:], in_=psq[:], func=AF.Exp), reads=[BpS[q]], writes=[BpT[t4]])

                    def B(n):
                        j, kb = tiles[n]
                        t4 = (base + n) % 4
                        rel = kb - 4 * j
                        st = j % 2
                        for m in range(2):
                            pOm = pO[st][m]
                            for s in range(2):
                                if s == 1 and rel < 2:
                                    continue
                                P.op("pe", lambda e, m=m, s=s, pOm=pOm: e.matmul(pOm[:, s * 130:s * 130 + 129], lhsT=pT[t4][:, m * 256 + s * 128:m * 256 + (s + 1) * 128], rhs=V[:, kb, 0:129],
                                                                             start=(rel == 0 and s == 0), stop=(kb == NBLK - 1), skip_group_check=True),
                                     reads=[BpT[t4]] + rdeps, writes=[BpO[st][m]])
                        if kb == NBLK - 1:
                            norm(j, st)

                    def norm(j, st):
                        g3 = gn[0] % 3
                        g2 = gn[0] % 2
                        gn[0] += 1
                        o = od[g3]
                        r = rs[g2]
                        pO0, pO1 = pO[st]
                        for m in range(2):
                            for s in range(2):
                                P.op("dve", lambda e, m=m, s=s: e.reciprocal(out=r[:, 2 * m + s:2 * m + s + 1], in_=pO[st][m][:, s * 130 + 128:s * 130 + 129]), reads=[BpO[st][m]], writes=[Brs[g2]])
                        P.op("dve", lambda e: e.tensor_scalar(out=r[:, 4:6], in0=r[:, 2:4], scalar1=lamneg[:, 0:1], scalar2=None, op0=ALU.mult), reads=[Brs[g2], B_const], writes=[Brs[g2]])
                        for s in range(2):
                            P.op("dve", lambda e, s=s: e.tensor_scalar(out=o[:, s, :], in0=pO0[:, s * 130:s * 130 + 128], scalar1=r[:, s:s + 1], scalar2=None, op0=ALU.mult),
                                 reads=[BpO[st][0], Brs[g2]], writes=[Bod[g3]])
                            P.op("dve", lambda e, s=s: e.scalar_tensor_tensor(out=o[:, s, :], in0=pO1[:, s * 130:s * 130 + 128], scalar=r[:, 4 + s:5 + s], in1=o[:, s, :], op0=ALU.mult, op1=ALU.add),
                                 reads=[BpO[st][1], Brs[g2], Bod[g3]], writes=[Bod[g3]])
                            P.op("act", lambda e, s=s: e.activation(out=junk[:], in_=o[:, s, :], func=AF.Square, accum_out=r[:, 6 + s:7 + s]), reads=[Bod[g3]], writes=[Bjunk, Brs[g2]])
                        P.op("dve", lambda e: e.tensor_scalar(out=r[:, 6:8], in0=r[:, 6:8], scalar1=1.0 / 128, scalar2=EPS, op0=ALU.mult, op1=ALU.add), reads=[Brs[g2]], writes=[Brs[g2]])
                        P.op("pool", lambda e: e.tensor_tensor(out=r[:, 6:8], in0=r[:, 6:8], in1=mhalf[:, 0:2], op=ALU.pow), reads=[Brs[g2], B_const], writes=[Brs[g2]])
                        for s in range(2):
                            P.op("dve", lambda e, s=s: e.scalar_tensor_tensor(out=o[:, s, :], in0=o[:, s, :], scalar=r[:, 6 + s:7 + s], in1=sublnw[:], op0=ALU.mult, op1=ALU.mult),
                                 reads=[Bod[g3], Brs[g2], B_const], writes=[Bod[g3]])
                        dst = CAT[j * 256:(j + 1) * 256, hd * 128:(hd + 1) * 128].rearrange("(s p) e -> p s e", p=128)
                        P.op("sp", lambda e: e.dma_start(out=dst, in_=o[:]), reads=[Bod[g3]], dma=True)

                    for n in range(N + 2):
                        if n < N:
                            A(n)
                        if 0 <= n - 2 < N:
                            B(n - 2)

                load_head(0)
                for hd in range(4):
                    if hd + 1 < 4:
                        load_head(hd + 1)
                    compute_head(hd)
                P.next_phase()

        def sb_phase():
            with contextlib.ExitStack() as es2:
                def sb2(name, shape, dt=F32):
                    return es2.enter_context(nc.sbuf_tensor(f"sa_{name}", list(shape), dt))

                def ps2(name, shape, dt=F32):
                    return es2.enter_context(nc.psum_tensor(f"sap_{name}", list(shape), dt))
                KTs = [sb2(f"KT{hb}", [128, S], BF16) for hb in range(2)]
                Qz = [[sb2(f"Qz{hb}{h2}", [128, S // 2], BF16) for h2 in range(2)] for hb in range(2)]
                Vs = [sb2(f"Vs{hb}", [128, NBLK, 128], BF16) for hb in range(2)]
                osb = sb2("osb", [128, 32, 128])
                smb = sb2("smb", [128, 128], BF16)
                om = [sb2(f"om{i}", [128, 512]) for i in range(3)]
                cp = [sb2(f"cp{i}", [128, 513]) for i in range(4)]
                att = [sb2(f"att{i}", [128, 512], BF16) for i in range(4)]
                attT = [sb2(f"attT{i}", [128, 512], BF16) for i in range(3)]
                pz = [ps2(f"pz{i}", [128, 512]) for i in range(3)]
                pTt = [ps2(f"pTt{i}", [128, 1024], BF16) for i in range(2)]
                pOs = [ps2(f"pOs{i}", [128, 512]) for i in range(2)]
                Bop = [Buf("op0"), Buf("op1")]
                Bz = [Buf("z0"), Buf("z1")]
                Bosb = Buf("osb")
                Bsm = Buf("sm")
                Bom = [Buf(f"om{i}") for i in range(3)]
                Bcp = [Buf(f"cp{i}") for i in range(4)]
                Bc0 = [Buf(f"c0{i}") for i in range(4)]
                Batt = [Buf(f"att{i}") for i in range(4)]
                BattT = [Buf(f"attT{i}") for i in range(3)]
                Bpz = [Buf(f"pz{i}") for i in range(3)]
                BpTt = [Buf(f"pTt{i}") for i in range(2)]
                BpOs = [Buf(f"pOs{i}") for i in range(2)]
                P.op("pool", lambda e: e.dma_start(out=smb[:], in_=smask_d), writes=[Bsm], dma=True)
                for hb in range(2):
                    P.op("pool", lambda e, hb=hb: e.memset(Qz[hb][0][64:128, :], 0.0), writes=[Bz[hb]])
                    P.op("pool", lambda e, hb=hb: e.memset(Qz[hb][1][0:64, :], 0.0), writes=[Bz[hb]])

                def load_pair(ch):
                    hb = ch % 2
                    P.op("sp", lambda e: e.dma_start(out=KTs[hb][:], in_=KTS[ch]), writes=[Bop[hb]], dma=True)
                    P.op("sp", lambda e: e.dma_start(out=Qz[hb][0][0:64, :], in_=QTS[ch][0:64, :]), writes=[Bop[hb]], dma=True)
                    P.op("sp", lambda e: e.dma_start(out=Qz[hb][1][64:128, :], in_=QTS[ch][64:128, :]), writes=[Bop[hb]], dma=True)
                    for kq in range(8):
                        P.op("sp", lambda e, kq=kq: e.dma_start(out=Vs[hb][:, kq * 8:(kq + 1) * 8, :], in_=VS[kq * 1024:(kq + 1) * 1024, ch * 128:(ch + 1) * 128].rearrange("(k p) e -> p k e", p=128)), writes=[Bop[hb]], dma=True)

                state = {"n": 0, "slot": 0}

                def compute_pair(ch):
                    hb = ch % 2
                    KT = KTs[hb]
                    V = Vs[hb]
                    rdeps = [Bop[hb], Bz[hb]]
                    tasks = []
                    for h2 in range(2):
                        for i in range(32):
                            b0 = 2 * i
                            first = True
                            if i % 2 == 1:
                                tasks.append((h2, i, b0, 2, True, b0 + 2 >= NBLK))
                                b0 += 2
                                first = False
                            while b0 < NBLK:
                                tasks.append((h2, i, b0, 4, first, b0 + 4 >= NBLK))
                                first = False
                                b0 += 4
                    N = len(tasks)
                    base = state["n"]
                    slotmap = {}

                    def A(n):
                        h2, i, b0, nb, first, last = tasks[n]
                        gnn = base + n
                        W = nb * 128
                        z = pz[gnn % 3]
                        Q = Qz[hb][h2]
                        P.op("pe", lambda e: e.matmul(z[:, 0:W], lhsT=Q[:, i * 128:(i + 1) * 128], rhs=KT[:, b0 * 128:b0 * 128 + W], start=True, stop=(not first), skip_group_check=True),
                             reads=rdeps, writes=[Bpz[gnn % 3]])
                        if first:
                            P.op("pe", lambda e: e.matmul(z[:, 0:128], lhsT=identb[:], rhs=smb[:], start=False, stop=True, skip_group_check=True),
                                 reads=[Bsm, B_const], writes=[Bpz[gnn % 3]])
                        o = om[gnn % 3]
                        P.op("act", lambda e: e.activation(out=o[:, 0:W], in_=z[:, 0:W], func=AF.Sigmoid, scale=-1.0), reads=[Bpz[gnn % 3]], writes=[Bom[gnn % 3]])
                        c = cp[gnn % 4]
                        if first:
                            P.op("dve", lambda e: e.tensor_tensor_scan(out=c[:, 1:W + 1], data0=o[:, 0:W], data1=z[:, 0:W], initial=1.0, op0=ALU.mult, op1=ALU.bypass),
                                 reads=[Bom[gnn % 3], Bpz[gnn % 3]], writes=[Bcp[gnn % 4]])
                        else:
                            pc = cp[(gnn - 1) % 4]
                            pW = tasks[n - 1][3] * 128
                            P.op("dve", lambda e: e.tensor_tensor_scan(out=c[:, 1:W + 1], data0=o[:, 0:W], data1=z[:, 0:W], initial=pc[:, pW:pW + 1], op0=ALU.mult, op1=ALU.bypass),
                                 reads=[Bom[gnn % 3], Bpz[gnn % 3], Bcp[(gnn - 1) % 4]], writes=[Bcp[gnn % 4]])

                    def A2(n):
                        h2, i, b0, nb, first, last = tasks[n]
                        gnn = base + n
                        W = nb * 128
                        c = cp[gnn % 4]
                        if first:
                            P.op("pool", lambda e: e.memset(c[:, 0:1], 1.0), writes=[Bc0[gnn % 4]])
                        else:
                            pc = cp[(gnn - 1) % 4]
                            pW = tasks[n - 1][3] * 128
                            P.op("act", lambda e: e.activation(out=c[:, 0:1], in_=pc[:, pW:pW + 1], func=AF.Copy), reads=[Bcp[(gnn - 1) % 4]], writes=[Bc0[gnn % 4]])
                        a = att[gnn % 4]
                        P.op("pool", lambda e: e.tensor_tensor(out=a[:, 0:W], in0=c[:, 0:W], in1=c[:, 1:W + 1], op=ALU.subtract), reads=[Bcp[gnn % 4], Bc0[gnn % 4]], writes=[Batt[gnn % 4]])

                    def Bst(n):
                        h2, i, b0, nb, first, last = tasks[n]
                        gnn = base + n
                        W = nb * 128
                        a = att[gnn % 4]
                        t = pTt[gnn % 2]
                        for b in range(nb):
                            P.op("pe", lambda e, b=b: e.transpose(out=t[:, b * 128:(b + 1) * 128], in_=a[:, b * 128:(b + 1) * 128], identity=identb[:]),
                                 reads=[Batt[gnn % 4], B_const], writes=[BpTt[gnn % 2]])
                        aT = attT[gnn % 3]
                        P.op("act", lambda e: e.activation(out=aT[:, 0:W], in_=t[:, 0:W], func=AF.Copy), reads=[BpTt[gnn % 2]], writes=[BattT[gnn % 3]])

                    def C(n):
                        h2, i, b0, nb, first, last = tasks[n]
                        gnn = base + n
                        aT = attT[gnn % 3]
                        if first:
                            slotmap[(h2, i)] = state["slot"] % 2
                            state["slot"] += 1
                        sl = slotmap[(h2, i)]
                        for b in range(nb):
                            P.op("pe", lambda e, b=b: e.matmul(pOs[sl][:, 0:64], lhsT=aT[:, b * 128:(b + 1) * 128], rhs=V[:, b0 + b, h2 * 64:(h2 + 1) * 64], start=(first and b == 0), stop=(last and b == nb - 1)),
                                 reads=[BattT[gnn % 3]] + rdeps, writes=[BpOs[sl]])
                        if last:
                            P.op("dve", lambda e: e.tensor_copy(out=osb[:, i, h2 * 64:(h2 + 1) * 64], in_=pOs[sl][:, 0:64]), reads=[BpOs[sl]], writes=[Bosb])

                    for n in range(N + 5):
                        if n < N:
                            A(n)
                        if 0 <= n - 1 < N:
                            A2(n - 1)
                        if 0 <= n - 4 < N:
                            Bst(n - 4)
                        if 0 <= n - 5 < N:
                            C(n - 5)
                    state["n"] += N
                    for iq in range(4):
                        dst = CAT[iq * 1024:(iq + 1) * 1024, 512 + ch * 128:512 + (ch + 1) * 128].rearrange("(i p) e -> p i e", p=128)
                        P.op("sp", lambda e, iq=iq, dst=dst: e.dma_start(out=dst, in_=osb[:, iq * 8:(iq + 1) * 8, :]), reads=[Bosb], dma=True)

                load_pair(0)
                for ch in range(4):
                    if ch + 1 < 4:
                        load_pair(ch + 1)
                    compute_pair(ch)
                P.next_phase()

        def out_phase():
            with contextlib.ExitStack() as es2:
                def sb2(name, shape, dt=F32):
                    return es2.enter_context(nc.sbuf_tensor(f"op_{name}", list(shape), dt))

                def ps2(name, shape, dt=F32):
                    return es2.enter_context(nc.psum_tensor(f"opp_{name}", list(shape), dt))
                wout = sb2("wout", [128, 8, D], BF16)
                G = sb2("G", [128, D])
                ct = [sb2(f"ct{i}", [128, 4, D]) for i in range(2)]
                x1t = [sb2(f"x1t{i}", [128, 4, D]) for i in range(2)]
                cT = [sb2(f"cT{i}", [128, 8, 512], BF16) for i in range(2)]
                junk = sb2("junk", [128, 512], BF16)
                ss = [sb2(f"ss{i}", [128, 4]) for i in range(2)]
                st2 = [sb2(f"st2{i}", [128, 4]) for i in range(2)]
                tmp = [sb2(f"tmp{i}", [128, 512]) for i in range(2)]
                ptr = [ps2(f"ptr{i}", [128, 512]) for i in range(2)]
                po = [ps2(f"po{i}", [128, 512]) for i in range(4)]
                Bw, BG = Buf("w"), Buf("G")
                Bct = [Buf("ct0"), Buf("ct1")]
                Bx1 = [Buf("x10"), Buf("x11")]
                BcT = [Buf("cT0"), Buf("cT1")]
                Bjunk = Buf("junk")
                Bss = [Buf("ss0"), Buf("ss1")]
                Bst2 = [Buf("st20"), Buf("st21")]
                Btmp = [Buf("tmp0"), Buf("tmp1")]
                Bptr = [Buf("ptr0"), Buf("ptr1")]
                Bpo = [Buf(f"po{i}") for i in range(4)]
                for k in range(8):
                    P.op("pool", lambda e, k=k: e.dma_start(out=wout[:, k, :], in_=w_out[k * 128:(k + 1) * 128, :]), writes=[Bw], dma=True)
                P.op("sp", lambda e: e.dma_start(out=G[:], in_=GREP[1]), writes=[BG], dma=True)
                NG = 8
                X1v = X1.rearrange("(blk two p) d -> p blk two d", two=2, p=128)
                cnt = [0, 0, 0]

                def load(g):
                    b = g % 2
                    P.op("sp", lambda e: e.dma_start(out=ct[b][:], in_=CAT[g * 512:(g + 1) * 512, :].rearrange("(i p) d -> p i d", p=128)), writes=[Bct[b]], dma=True)
                    P.op("sp", lambda e: e.dma_start(out=x1t[b][:], in_=X1v[:, 4 * g:4 * g + 4, 0, :]), writes=[Bx1[b]], dma=True)

                def pre(g):
                    b = g % 2
                    for i in range(4):
                        P.op("act", lambda e, i=i: e.activation(out=junk[:], in_=ct[b][:, i, 512:1024], func=AF.Square, accum_out=ss[b][:, i:i + 1]),
                             reads=[Bct[b]], writes=[Bjunk, Bss[b]])
                    P.op("dve", lambda e: e.tensor_scalar(out=ss[b][:], in0=ss[b][:], scalar1=1.0 / 512, scalar2=EPS, op0=ALU.mult, op1=ALU.add), reads=[Bss[b]], writes=[Bss[b]])
                    P.op("pool", lambda e: e.tensor_tensor(out=ss[b][:], in0=ss[b][:], in1=mhalf[:, 0:4], op=ALU.pow), reads=[Bss[b], B_const], writes=[Bss[b]])
                    for i in range(4):
                        P.op("dve", lambda e, i=i: e.tensor_scalar(out=ct[b][:, i, 512:1024], in0=ct[b][:, i, 512:1024], scalar1=ss[b][:, i:i + 1], scalar2=None, op0=ALU.mult),
                             reads=[Bss[b], Bct[b]], writes=[Bct[b]])
                    for c in range(8):
                        pt = ptr[c % 2]
                        for i in range(4):
                            P.op("pe", lambda e, c=c, i=i, pt=pt: e.transpose(out=pt[:, i * 128:(i + 1) * 128], in_=ct[b][:, i, c * 128:(c + 1) * 128], identity=identf[:]),
                                 reads=[Bct[b], B_const], writes=[Bptr[c % 2]])
                        if c < 4:
                            P.op("act", lambda e, c=c, pt=pt: e.activation(out=cT[b][:, c, :], in_=pt[:], func=AF.Copy), reads=[Bptr[c % 2]], writes=[BcT[b]])
                        else:
                            P.op("dve", lambda e, c=c, pt=pt: e.tensor_scalar(out=cT[b][:, c, :], in0=pt[:], scalar1=sbbeta[:, c - 4:c - 3], scalar2=None, op0=ALU.mult),
                                 reads=[Bptr[c % 2], B_const], writes=[BcT[b]])

                def main(g):
                    b = g % 2
                    for i in range(4):
                        pp = []
                        for hf in range(2):
                            n = cnt[0] % 4
                            cnt[0] += 1
                            pp.append(n)
                            for c in range(8):
                                P.op("pe", lambda e, c=c, n=n, hf=hf, i=i: e.matmul(po[n][:], lhsT=cT[b][:, c, i * 128:(i + 1) * 128], rhs=wout[:, c, hf * 512:(hf + 1) * 512], start=(c == 0), stop=(c == 7)),
                                     reads=[BcT[b], Bw], writes=[Bpo[n]])
                        s2 = cnt[1] % 2
                        cnt[1] += 1
                        r = st2[s2]
                        for hf in range(2):
                            P.op("act", lambda e, r=r, hf=hf, n=pp[hf]: e.activation(out=junk[:], in_=po[n][:], func=AF.Square, accum_out=r[:, hf:hf + 1]), reads=[Bpo[pp[hf]]], writes=[Bjunk, Bst2[s2]])
                        P.op("dve", lambda e, r=r: e.tensor_tensor(out=r[:, 2:3], in0=r[:, 0:1], in1=r[:, 1:2], op=ALU.add), reads=[Bst2[s2]], writes=[Bst2[s2]])
                        P.op("dve", lambda e, r=r: e.tensor_scalar(out=r[:, 2:3], in0=r[:, 2:3], scalar1=1.0 / D, scalar2=EPS, op0=ALU.mult, op1=ALU.add), reads=[Bst2[s2]], writes=[Bst2[s2]])
                        P.op("pool", lambda e, r=r: e.tensor_tensor(out=r[:, 3:4], in0=r[:, 2:3], in1=mhalf[:, 0:1], op=ALU.pow), reads=[Bst2[s2], B_const], writes=[Bst2[s2]])
                        for hf in range(2):
                            tn = cnt[2] % 2
                            cnt[2] += 1
                            t = tmp[tn]
                            P.op("dve", lambda e, r=r, hf=hf, n=pp[hf], t=t: e.scalar_tensor_tensor(out=t[:], in0=po[n][:], scalar=r[:, 3:4], in1=G[:, hf * 512:(hf + 1) * 512], op0=ALU.mult, op1=ALU.mult),
                                 reads=[Bpo[pp[hf]], Bst2[s2], BG], writes=[Btmp[tn]])
                            P.op("dve", lambda e, r=r, hf=hf, t=t, i=i: e.tensor_tensor(out=x1t[b][:, i, hf * 512:(hf + 1) * 512], in0=x1t[b][:, i, hf * 512:(hf + 1) * 512], in1=t[:], op=ALU.add),
                                 reads=[Btmp[tn], Bx1[b]], writes=[Bx1[b]])
                    P.op("sp", lambda e: e.dma_start(out=X2[g * 512:(g + 1) * 512, :].rearrange("(i p) d -> p i d", p=128), in_=x1t[b][:]), reads=[Bx1[b]], dma=True)

                load(0)
                for t in range(NG + 1):
                    if t < NG:
                        pre(t)
                    if t >= 1:
                        main(t - 1)
                    if t + 1 < NG:
                        load(t + 1)
                P.next_phase()

        if stop_after >= 2:
            proj_phase()
        if stop_after >= 3:
            diff_phase()
        if stop_after >= 4:
            sb_phase()
        if stop_after >= 5:
            out_phase()
        if stop_after >= 6:
            ffn_phase("f2", lambda g: X2[g * 256:(g + 1) * 256, :], lambda g: out[g * 256:(g + 1) * 256, :], S // 2 // 256, 2, ffw[1])
        P.emit()
    return nc


def _tables():
    ident = np.eye(128, dtype=np.float32)
    r = np.arange(128)
    smask = np.where(r[None, :] <= r[:, None], NEG, 0.0).astype(np.float32)
    kk = r[:, None]
    qq = r[None, :]
    tri = np.where(kk >= qq, 0.0, NEG).astype(np.float32)
    full = np.zeros((128, 128), np.float32)
    none = np.full((128, 128), NEG, np.float32)
    dm = np.zeros((3, 128, 2, 2, 128), np.float32)
    for m in range(2):
        dm[0, :, m, 0] = tri
        dm[0, :, m, 1] = none
        dm[1, :, m, 0] = full
        dm[1, :, m, 1] = none
        dm[2, :, m, 0] = full
        dm[2, :, m, 1] = tri
    dmask = dm.reshape(3, 128, 512)
    return ident, smask, dmask


def make_in_maps(inputs):
    f = lambda a: np.ascontiguousarray(np.asarray(a, dtype=np.float32))
    x = f(inputs["x"])
    c = f(inputs["c"])
    ident, smask, dmask = _tables()
    pk = lambda v, n: np.ascontiguousarray(v.reshape(n, 128).T)
    gpre_pk = np.concatenate([pk(f(inputs[k])[0], 8) for k in ("ffn1_g_pre", "mix_g_pre", "ffn2_g_pre")], axis=1)
    gpost_row = np.stack([f(inputs[k])[0] for k in ("ffn1_g_post", "mix_g_post", "ffn2_g_post")], axis=0)
    lam_in = np.stack([f(inputs[k])[0] for k in ("lam_q1", "lam_k1", "lam_q2", "lam_k2")], axis=0)
    common = {
        "b_pk": pk(f(inputs["b_ada"])[0], 72),
        "b_row": f(inputs["b_ada"]),
        "w_ada": f(inputs["w_ada"])[0],
        "gpre_pk": np.ascontiguousarray(gpre_pk),
        "gpost_row": np.ascontiguousarray(gpost_row),
        "f1_wg": f(inputs["ffn1_w_gate"])[0], "f1_wu": f(inputs["ffn1_w_up"])[0], "f1_wd": f(inputs["ffn1_w_down"])[0],
        "f2_wg": f(inputs["ffn2_w_gate"])[0], "f2_wu": f(inputs["ffn2_w_up"])[0], "f2_wd": f(inputs["ffn2_w_down"])[0],
        "w_in": f(inputs["w_in"])[0], "w_out": f(inputs["w_out"])[0],
        "lam_in": np.ascontiguousarray(lam_in),
        "subln_row": f(inputs["diff_subln"]),
        "sbbeta_pk": pk(f(inputs["sb_beta"])[0], 4),
        "ident_d": ident, "smask_d": smask, "dmask_d": dmask,
    }
    pos = np.arange(S)
    qpos = (np.arange(S // 2) // 128) * 256 + (np.arange(S // 2) % 128)
    qx = np.zeros((4, 4, S // 2), np.float32)
    for h in range(4):
        sl = 2.0 ** (-2.0 * (h + 1))
        qx[h, 0] = sl * (qpos % 128)
        qx[h, 1] = sl * 128.0 * (qpos // 128)
        qx[h, 2] = -sl
        qx[h, 3] = -sl
    in_maps = []
    for core in range(8):
        b, hh = core // 2, core % 2
        xr = x[b, ::-1]
        if hh == 0:
            xin = np.ascontiguousarray(xr)
        else:
            xin = np.concatenate([xr[128:], np.zeros((128, D), np.float32)], axis=0)
        kx = np.zeros((4, S), np.float32)
        kx[0] = 1.0
        kx[1] = 1.0
        kx[2] = pos % 128
        kx[3] = 128.0 * (pos // 128)
        vflag = np.ones((128, NBLK), np.float32)
        if hh == 1:
            kx[3, S - 128:] = 2.0 ** 20
            vflag[:, NBLK - 1] = 0.0
        m = dict(common)
        m.update({"xin": xin, "c_pk": pk(c[b], 8), "kx_d": kx, "qx_d": qx, "vflag_d": vflag})
        in_maps.append(m)
    return in_maps


def assemble(outs):
    res = np.zeros((4, S, D), np.float32)
    for core in range(8):
        b, hh = core // 2, core % 2
        o = np.asarray(outs[core]).reshape(32, 128, D)
        for i in range(32):
            p0 = (2 * i + hh) * 128
            res[b, S - 1 - p0 - 127:S - p0] = o[i][::-1]
    return res


_NC_CACHE = {}


def kernel(**inputs):
    if "nc" not in _NC_CACHE:
        _NC_CACHE["nc"] = build_program()
    nc = _NC_CACHE["nc"]
    in_maps = make_in_maps(inputs)
    r = run_bass_kernel_spmd(nc, in_maps, core_ids=list(range(8)))
    return assemble([r.results[i]["out"] for i in range(8)])
```
